# Optimizing a Trainium2 kernel written in Bass

```python
import jax, jax.numpy as jnp
from jax import lax
import numpy as np

D_MODEL = 1024
BATCH = 8
SEQ = 2048
DEPTH = 2
DEC_BATCH = 32
DEC_SEQ = 8
PAST_LEN = 8192
PAGE_SIZE = 128

BRANCH_W = D_MODEL // 2
POOL_WINDOWS = (2, 4, 8, 16)
POOL_GROUPS = len(POOL_WINDOWS)
POOL_GC = BRANCH_W // POOL_GROUPS
POOL_HIST = max(POOL_WINDOWS) - 1
N_HEADS = 8
HEAD_DIM = BRANCH_W // N_HEADS
Q_BLOCK = 128
CONV_W = 3
CONV_HIST = CONV_W - 1
N_BRANCH = 3
RMS_EPS = 1e-6
FORGET_BIAS_MEAN = 7.0
FORGET_NOISE = 0.5
COND_STD = 0.3
SPLIT_SIZES = (BRANCH_W, BRANCH_W,
               BRANCH_W, BRANCH_W, BRANCH_W, N_HEADS, BRANCH_W,
               BRANCH_W, BRANCH_W, BRANCH_W, BRANCH_W,
               N_BRANCH * D_MODEL)
D_IN = sum(SPLIT_SIZES)

kernel_name = 'hybrid_pool_fox_conv_decoder_step'


def rmsnorm(x, g):
    xf = x.astype(jnp.float32)
    y = xf * lax.rsqrt(jnp.mean(xf * xf, axis=-1, keepdims=True) + RMS_EPS)
    return (y * g.astype(jnp.float32)).astype(x.dtype)


def split_cols(z):
    idx, off = [], 0
    for s in SPLIT_SIZES[:-1]:
        off += s
        idx.append(off)
    return jnp.split(z, idx, axis=-1)


def pool_mixer(u, hist, pos0, w_grp, scale):
    B, T, _ = u.shape
    ext = jnp.concatenate([hist, u], axis=1)
    cs = jnp.cumsum(ext.astype(jnp.float32), axis=1)
    cs = jnp.concatenate([jnp.zeros((B, 1, BRANCH_W), jnp.float32), cs], axis=1)
    end = cs[:, POOL_HIST + 1:]
    pos = pos0 + jnp.arange(T)
    means = []
    for g, w in enumerate(POOL_WINDOWS):
        sl = slice(g * POOL_GC, (g + 1) * POOL_GC)
        start = lax.slice_in_dim(cs, POOL_HIST + 1 - w, POOL_HIST + 1 - w + T, axis=1)[..., sl]
        cnt = jnp.minimum(pos + 1, w).astype(jnp.float32)[None, :, None]
        means.append((end[..., sl] - start) / cnt)
    mean = jnp.stack(means, axis=2)
    d = mean - u.reshape(B, T, POOL_GROUPS, POOL_GC).astype(jnp.float32)
    y = jnp.einsum('btgc,gcd->btgd', d.astype(u.dtype), w_grp).reshape(B, T, BRANCH_W)
    return y * scale, ext[:, -POOL_HIST:]


def short_conv(z, hist, w):
    T = z.shape[1]
    ext = jnp.concatenate([hist, z], axis=1)
    y = ext[:, 0:T] * w[0] + ext[:, 1:1 + T] * w[1] + ext[:, 2:2 + T] * w[2]
    return y, ext[:, -CONV_HIST:]


def forgetting_attention(q, k, v, lq, lk, q_pos, k_pos):
    B, T, H, Dh = q.shape
    qb = min(Q_BLOCK, T)
    nb = -(-T // qb)
    pad = nb * qb - T
    if pad:
        q = jnp.pad(q, ((0, 0), (0, pad), (0, 0), (0, 0)))
        lq = jnp.pad(lq, ((0, 0), (0, pad), (0, 0)))
        q_pos = jnp.pad(q_pos, (0, pad), mode='edge')
    qs = q.reshape(B, nb, qb, H, Dh).transpose(1, 0, 2, 3, 4)
    lqs = lq.reshape(B, nb, qb, H).transpose(1, 0, 3, 2)
    ps = q_pos.reshape(nb, qb)
    lkT = lk.transpose(0, 2, 1)[:, :, None, :]
    scale = HEAD_DIM ** -0.5

    def block(args):
        qi, li, pi = args
        s = jnp.einsum('bqhd,bkhd->bhqk', qi, k).astype(jnp.float32) * scale
        s = s + li[..., None] - lkT
        mask = k_pos[None, :] <= pi[:, None]
        s = jnp.where(mask[None, None], s, -jnp.inf)
        p = jax.nn.softmax(s, axis=-1)
        return jnp.einsum('bhqk,bkhd->bqhd', p.astype(v.dtype), v)

    o = lax.map(block, (qs, lqs, ps))
    return o.transpose(1, 0, 2, 3, 4).reshape(B, nb * qb, H, Dh)[:, :T]


def mixer_layer(x, c, hist_pool, hist_conv, kv_past, pos0, norm_g, w_cond, b_cond, w_in, b_f,
                pool_w, pool_scale, conv_w, w_br_a, w_br_b, w_br_c, w_o):
    B, T, _ = x.shape
    mod = (c @ w_cond + b_cond)[:, None, :]
    shift, scale, gate = jnp.split(mod, 3, axis=-1)
    h = rmsnorm(x, norm_g) * (1 + scale) + shift
    z = h @ w_in
    pu, pg, q, k, v, fl, ag, ch, cb, cc, cg, mg = split_cols(z)

    ya, new_pool = pool_mixer(pu, hist_pool, pos0, pool_w, pool_scale)
    ya = (ya * jax.nn.silu(pg)) @ w_br_a

    q = q.reshape(B, T, N_HEADS, HEAD_DIM)
    k = k.reshape(B, T, N_HEADS, HEAD_DIM)
    v = v.reshape(B, T, N_HEADS, HEAD_DIM)
    logf = jax.nn.log_sigmoid((fl + b_f).astype(jnp.float32))
    lnew = jnp.cumsum(logf, axis=1)
    q_pos = pos0 + jnp.arange(T)
    if kv_past is None:
        o = forgetting_attention(q, k, v, lnew, lnew, q_pos, q_pos)
    else:
        kp, vp, lfp = kv_past
        P = kp.shape[1]
        lfp = lfp.astype(jnp.float32)
        a = lfp - lax.cumsum(lfp, axis=1, reverse=True)
        k_all = jnp.concatenate([kp, k], axis=1)
        v_all = jnp.concatenate([vp, v], axis=1)
        lk = jnp.concatenate([a, lnew], axis=1)
        o = forgetting_attention(q, k_all, v_all, lnew, lk, q_pos, jnp.arange(P + T))
    yb = (o.reshape(B, T, BRANCH_W) * jax.nn.silu(ag)) @ w_br_b

    yconv, new_conv = short_conv(cc * ch, hist_conv, conv_w)
    yc = (cb * yconv * jax.nn.silu(cg)) @ w_br_c

    ga, gb, gc = jnp.split(jax.nn.sigmoid(mg), 3, axis=-1)
    merged = ga * ya + gb * yb + gc * yc
    out = x + gate * (merged @ w_o)
    return out, (k, v, logf.astype(x.dtype), new_pool, new_conv)


def setup_inputs(seed: int = 0) -> dict:
    key = jax.random.key(seed)
    ks = jax.random.split(key, 24)
    n_pages = PAST_LEN // PAGE_SIZE
    n_used = DEC_BATCH * n_pages
    n_pool = n_used + n_used // 4

    def nrm(k, shape, s):
        return jax.random.normal(k, shape, jnp.float32) * s

    perm = jax.random.permutation(ks[0], n_pool)
    page_table = perm[:n_used].reshape(DEC_BATCH, n_pages).astype(jnp.int32)
    return {
        'x_prompt': nrm(ks[1], (BATCH, SEQ, D_MODEL), 1.0),
        'x_sample': nrm(ks[2], (DEC_BATCH, DEC_SEQ, D_MODEL), 1.0),
        'cache_k': nrm(ks[3], (DEPTH, n_pool, PAGE_SIZE, N_HEADS, HEAD_DIM), 1.0),
        'cache_v': nrm(ks[4], (DEPTH, n_pool, PAGE_SIZE, N_HEADS, HEAD_DIM), 1.0),
        'cache_logf': jax.nn.log_sigmoid(FORGET_BIAS_MEAN + nrm(ks[5], (DEPTH, n_pool, PAGE_SIZE, N_HEADS), FORGET_NOISE)),
        'state_pool': nrm(ks[6], (DEPTH, DEC_BATCH, POOL_HIST, BRANCH_W), 1.0),
        'state_conv': nrm(ks[7], (DEPTH, DEC_BATCH, CONV_HIST, BRANCH_W), 1.0),
        'page_table': page_table,
        'c_prompt': nrm(ks[8], (BATCH, D_MODEL), 1.0),
        'c_sample': nrm(ks[9], (DEC_BATCH, D_MODEL), 1.0),
        'norm_g': 1.0 + nrm(ks[10], (DEPTH, D_MODEL), 0.1),
        'w_cond': nrm(ks[11], (DEPTH, D_MODEL, 3 * D_MODEL), COND_STD * D_MODEL ** -0.5),
        'b_cond': nrm(ks[12], (DEPTH, 3 * D_MODEL), 0.01),
        'w_in': nrm(ks[13], (DEPTH, D_MODEL, D_IN), D_MODEL ** -0.5),
        'b_f': FORGET_BIAS_MEAN + nrm(ks[14], (DEPTH, N_HEADS), FORGET_NOISE),
        'pool_w': nrm(ks[15], (DEPTH, POOL_GROUPS, POOL_GC, POOL_GC), POOL_GC ** -0.5),
        'pool_scale': 1.0 + nrm(ks[16], (DEPTH, BRANCH_W), 0.1),
        'conv_w': nrm(ks[17], (DEPTH, CONV_W, BRANCH_W), CONV_W ** -0.5),
        'w_br_a': nrm(ks[18], (DEPTH, BRANCH_W, D_MODEL), BRANCH_W ** -0.5),
        'w_br_b': nrm(ks[19], (DEPTH, BRANCH_W, D_MODEL), BRANCH_W ** -0.5),
        'w_br_c': nrm(ks[20], (DEPTH, BRANCH_W, D_MODEL), BRANCH_W ** -0.5),
        'w_o': nrm(ks[21], (DEPTH, D_MODEL, D_MODEL), D_MODEL ** -0.5),
        'final_g': 1.0 + nrm(ks[22], (D_MODEL,), 0.1),
    }


def reference(x_prompt, x_sample, cache_k, cache_v, cache_logf, state_pool, state_conv, page_table,
              c_prompt, c_sample, norm_g, w_cond, b_cond, w_in, b_f, pool_w, pool_scale, conv_w,
              w_br_a, w_br_b, w_br_c, w_o, final_g):
    B = x_prompt.shape[0]
    DB = x_sample.shape[0]
    P = page_table.shape[1] * cache_k.shape[2]
    xp, xs = x_prompt, x_sample
    zero_pool = jnp.zeros((B, POOL_HIST, BRANCH_W), xp.dtype)
    zero_conv = jnp.zeros((B, CONV_HIST, BRANCH_W), xp.dtype)
    st_p, st_s = [], []
    for l in range(DEPTH):
        w = (norm_g[l], w_cond[l], b_cond[l], w_in[l], b_f[l], pool_w[l], pool_scale[l], conv_w[l],
             w_br_a[l], w_br_b[l], w_br_c[l], w_o[l])
        xp, sp = mixer_layer(xp, c_prompt, zero_pool, zero_conv, None, 0, *w)
        kp = cache_k[l][page_table].reshape(DB, P, N_HEADS, HEAD_DIM)
        vp = cache_v[l][page_table].reshape(DB, P, N_HEADS, HEAD_DIM)
        lfp = cache_logf[l][page_table].reshape(DB, P, N_HEADS)
        xs, ss = mixer_layer(xs, c_sample, state_pool[l], state_conv[l], (kp, vp, lfp), P, *w)
        st_p.append(sp)
        st_s.append(ss)
    y_prompt = rmsnorm(xp, final_g)
    y_sample = rmsnorm(xs, final_g)
    new_k_prompt = jnp.stack([s[0] for s in st_p])
    new_v_prompt = jnp.stack([s[1] for s in st_p])
    new_logf_prompt = jnp.stack([s[2] for s in st_p])
    new_pool_prompt = jnp.stack([s[3] for s in st_p])
    new_conv_prompt = jnp.stack([s[4] for s in st_p])
    new_k_sample = jnp.stack([s[0] for s in st_s])
    new_v_sample = jnp.stack([s[1] for s in st_s])
    new_logf_sample = jnp.stack([s[2] for s in st_s])
    new_pool_sample = jnp.stack([s[3] for s in st_s])
    new_conv_sample = jnp.stack([s[4] for s in st_s])
    return (y_prompt, y_sample, new_k_prompt, new_v_prompt, new_logf_prompt, new_pool_prompt, new_conv_prompt,
            new_k_sample, new_v_sample, new_logf_sample, new_pool_sample, new_conv_sample)
```

```python
import contextlib
import numpy as np
import concourse.bass as bass
import concourse.mybir as mybir
from concourse.bass_utils import run_bass_kernel_spmd

F32 = mybir.dt.float32
BF16 = mybir.dt.bfloat16
I32 = mybir.dt.int32
AF = mybir.ActivationFunctionType
ALU = mybir.AluOpType

NL = 2
D = 1024
NPR = 2048
NSM = 32
NT = NPR + NSM
NSEQ = 4
BLK = [(0, 512), (512, 512), (1024, 512), (1536, 512), (2048, 32)]
TIL = [(i * 128, 128) for i in range(16)] + [(2048, 32)]
COL = dict(pu=0, pg=512, q=1024, k=1536, v=2048, fl=2560, ag=2568, ch=3080, cb=3592, cc=4104, cg=4616,
           mg=5128)
WIN = (2, 4, 8, 16)
NPOOL = 2560
NPG = 64
NEG = -30000.0
DEBUG = False
STOP = 99
NOVA = False


class Prog:
    ENG = ('pe', 'act', 'dve', 'pool', 'sp')

    def __init__(self, nc):
        self.nc = nc
        self.q = {e: [] for e in self.ENG}
        self.cnt = {e: 0 for e in self.ENG}
        self.last_w = {}
        self.readers = {}
        self.seen = {e: {} for e in self.ENG}
        self.slot_cnt = {}
        self.sems = {}

    def _deps(self, eng, reads, writes):
        deps = {}

        def add(d):
            if d is None:
                return
            k, v = d
            if k == eng and eng == 'pe':
                return
            if deps.get(k, 0) < v:
                deps[k] = v
        for r in reads:
            add(self.last_w.get(r))
        for w in writes:
            add(self.last_w.get(w))
            for d in self.readers.get(w, ()):
                add(d)
        out = []
        seen = self.seen[eng]
        for k, v in deps.items():
            if seen.get(k, 0) >= v:
                continue
            seen[k] = v
            out.append((k, v))
        return out

    def _commit(self, dep, reads, writes):
        for w in writes:
            self.last_w[w] = dep
            self.readers[w] = []
        for r in reads:
            self.readers.setdefault(r, []).append(dep)

    def op(self, eng, fn, reads=(), writes=()):
        reads = list(reads)
        writes = list(writes)
        waits = self._deps(eng, reads, writes)
        self.cnt[eng] += 1
        dep = (eng, self.cnt[eng])
        self.q[eng].append((waits, fn, (eng, 1)))
        self._commit(dep, reads, writes)
        return dep

    def dma(self, eng, slot, fn, reads=(), writes=()):
        reads = list(reads)
        writes = list(writes)
        key = ('dma', slot)
        waits = self._deps(eng, reads, writes)
        prev = self.slot_cnt.get(key, 0)
        if prev and self.seen[eng].get(key, 0) < prev:
            self.seen[eng][key] = prev
            waits.append((key, prev))
        self.slot_cnt[key] = prev + 16
        dep = (key, prev + 16)
        self.q[eng].append((waits, fn, (key, 16)))
        self._commit(dep, reads, writes)
        return dep

    def barrier(self):
        tgt = {}
        for e in self.ENG:
            if e != 'sp' and self.cnt[e] > 0:
                tgt[e] = self.cnt[e]
        for k, v in self.slot_cnt.items():
            tgt[k] = v
        for e in self.ENG:
            waits = []
            for k, v in tgt.items():
                if k == e:
                    continue
                if self.seen[e].get(k, 0) < v:
                    self.seen[e][k] = v
                    waits.append((k, v))
            if waits:
                self.q[e].append((waits, None, None))

    def emit(self):
        nc = self.nc
        with contextlib.ExitStack() as st:
            for e in self.ENG:
                if e != 'sp':
                    self.sems[e] = st.enter_context(nc.semaphore('s_' + e))
            for key in sorted(self.slot_cnt):
                self.sems[key] = st.enter_context(nc.semaphore('d_%d' % key[1]))
            fin = {}
            for k, v in self.slot_cnt.items():
                fin[k] = v
            for e in self.ENG:
                if e != 'sp' and self.cnt[e] > 0:
                    fin[e] = self.cnt[e]
            block = st.enter_context(nc.Block())
            sems = self.sems

            def replay(name, eng, final=False):
                for waits, fn, inc in self.q[name]:
                    for k, v in waits:
                        eng.wait_ge(sems[k], v)
                    if fn is not None:
                        fn(eng).then_inc(sems[inc[0]], inc[1])
                if final:
                    for k, v in fin.items():
                        eng.wait_ge(sems[k], v)

            @block.tensor
            def _(e):
                replay('pe', e)

            @block.scalar
            def _(e):
                replay('act', e)

            @block.vector
            def _(e):
                replay('dve', e)

            @block.gpsimd
            def _(e):
                replay('pool', e)

            @block.sync
            def _(e):
                replay('sp', e, final=True)


def build():
    nc = bass.Bass("TRN2", target_bir_lowering=False)
    dt_in = lambda n, s, d=F32: nc.dram_tensor(n, s, d, kind="ExternalInput").ap()
    dt_out = lambda n, s, d=F32: nc.dram_tensor(n, s, d, kind="ExternalOutput").ap()
    xin = dt_in("xin", [NT, D])
    cT_d = dt_in("cT", [128, 8, 5])
    ck_d = dt_in("cache_k", [NL, NPOOL * 128, 512])
    cv_d = dt_in("cache_v", [NL, NPOOL * 128, 512])
    clf_d = dt_in("cache_logf", [NL, NPOOL, 1024])
    pt_d = dt_in("pt", [1, NSEQ * NPG], I32)
    ptT_d = dt_in("ptT", [128, NSEQ], I32)
    spT_d = dt_in("spT", [NL, 128, 4, NSEQ, 15])
    sp_d = dt_in("sp_tm", [NL, NSEQ, 15, 512])
    scT_d = dt_in("scT", [NL, 128, 4, NSEQ, 2])
    w_in_d = dt_in("w_in", [NL, D, 8200])
    w_cond_d = dt_in("w_cond", [NL, D, 3072])
    w_bra_d = dt_in("w_br_a", [NL, 512, D])
    w_brb_d = dt_in("w_br_b", [NL, 512, D])
    w_brc_d = dt_in("w_br_c", [NL, 512, D])
    w_o_d = dt_in("w_o", [NL, D, D])
    pool_w_d = dt_in("pool_w", [NL, 4, 128, 128])
    ng_d = dt_in("norm_g_fm", [128, NL, 8])
    bc_d = dt_in("b_cond_fm", [128, NL, 24])
    bf_d = dt_in("b_f", [8, NL])
    psc_d = dt_in("pool_scale_fm", [128, NL, 4])
    cw_d = dt_in("conv_w_fm", [128, NL, 4, 3])
    fg_d = dt_in("final_g", [1, D])
    ident_d = dt_in("ident", [128, 128])
    maskc_d = dt_in("maskc", [128, 128])
    u64_d = dt_in("u64", [64, 64])
    i8_d = dt_in("i8", [8, 8])
    maskn_d = dt_in("maskn", [32, NSEQ, 8])
    rc_d = dt_in("rc", [128, 4, 16])
    piota_d = dt_in("piota", [128, 1])

    y_d = dt_out("y", [NT, D])
    nk_d = dt_out("nk", [NL, NT, 512])
    nv_d = dt_out("nv", [NL, NT, 512])
    nlf_d = dt_out("nlf", [NL, NT, 8])
    npp_d = dt_out("npool_p", [NL, 15, 512])
    ncp_d = dt_out("nconv_p", [NL, 2, 512])
    nps_d = dt_out("npool_s", [NL, NSEQ, 15, 512])
    ncs_d = dt_out("nconv_s", [NL, NSEQ, 2, 512])
    x1_d = nc.dram_tensor("x1s", [NT, D], F32, kind="Internal").ap()

    with contextlib.ExitStack() as st:
        def sb(name, shape, dt=F32):
            return st.enter_context(nc.sbuf_tensor(name, shape, dt))
        XT = [sb("xt%d" % i, [128, D]) for i in range(2)]
        XN = sb("xn", [128, D], BF16)
        HT = sb("ht", [128, 8, NT], BF16)
        R1 = sb("r1", [128, 16896], BF16)
        R2 = sb("r2", [128, 22528], BF16)
        BINB = sb("binb", [128, 4, NT], BF16)
        W = [sb("w%d" % i, [128, 8, 512], BF16) for i in range(3)]
        WBR = [sb("wbr%d" % i, [128, 4, 512], BF16) for i in range(3)]
        WPW = sb("wpw", [128, 4, 128], BF16)
        WFL = sb("wfl", [128, 8, 8], BF16)
        MOD = sb("mod", [128, NL, 24, 5])
        CT32 = sb("ct32", [128, 8, 5])
        CTB = sb("ctb", [128, 8, 5], BF16)
        NG = sb("ng", [128, NL, 8])
        BC = sb("bc", [128, NL, 24])
        BFN = sb("bfn", [8, NL])
        PSC = sb("psc", [128, NL, 4])
        CW = sb("cw", [128, NL, 4, 3])
        IDF = sb("idf", [128, 128])
        IDB = sb("idb", [128, 128], BF16)
        MASKC = sb("maskc_s", [128, 128])
        U64 = sb("u64_s", [64, 64])
        I8 = sb("i8_s", [8, 8])
        MASKN = sb("maskn_s", [32, NSEQ, 8])
        RC = sb("rc_s", [128, 4, 16])
        PIOTA = sb("piota_s", [128, 1])
        ONE1 = sb("one1", [128, 1])
        ONES8 = sb("ones8", [8, 128])
        ONEB = sb("oneb", [128, 1], BF16)
        SAGS = sb("sags", [128, 4, NSM], BF16)
        NCV = sb("ncv", [2, 5, 128])
        NPV = sb("npv", [16, 5, 128])
        SELH = sb("selh", [8, 2, 128])
        T16 = sb("t16", [128, 16])
        LC = sb("lc", [8, 2, NT])
        LFTM = sb("lftm", [128, 17, 8])
        NCTM = sb("nctm", [128, 17, 8])
        CTM = sb("ctm", [128, 8])
        SS = sb("ss", [128, 4])
        STGT = sb("stgt", [128, 2, 512])
        STG = [STGT[:, i, :] for i in range(2)]
        FG = STGT[:, :, :].rearrange("p a b -> p (a b)")
        PTB = sb("ptb", [128, NSEQ * NPG], I32)
        IDX = sb("idx", [128, NSEQ * NPG], I32)
        PTT = sb("ptt", [128, NSEQ], I32)
        PS = st.enter_context(nc.psum_tensor("ps", [128, 8, 512], F32))

        P = Prog(nc)
        cnt = {'bank': 0, 'sbank': 0, 'abank': 0, 'slot': 0, 'stg': 0, 'w': 0}

        def nbank():
            cnt['bank'] = (cnt['bank'] + 1) % 4
            return cnt['bank']

        def sbank():
            cnt['sbank'] = (cnt['sbank'] + 1) % 2
            return 4 + cnt['sbank']

        def abank():
            cnt['abank'] = (cnt['abank'] + 1) % 2
            return 6 + cnt['abank']

        def mm(out, lhsT, rhs, start, stop, reads, writes):
            P.op('pe', lambda e: e.matmul(out, lhsT, rhs, start=start, stop=stop), reads, writes)

        def tr(out, in_, ident, reads, writes):
            P.op('pe', lambda e: e.transpose(out, in_, ident), reads, writes)

        def act(out, in_, func, reads, writes, bias=0.0, scale=1.0, accum_out=None):
            if accum_out is None:
                P.op('act', lambda e: e.activation(out, in_, func, bias=bias, scale=scale), reads, writes)
            else:
                P.op('act', lambda e: e.activation(out, in_, func, bias=bias, scale=scale,
                                                   accum_out=accum_out), reads, writes)

        def tt(out, in0, in1, op, reads, writes, eng='dve'):
            P.op(eng, lambda e: e.tensor_tensor(out, in0, in1, op), reads, writes)

        def ts(out, in0, s1, s2, op0, op1, reads, writes, eng='dve'):
            if s2 is None:
                P.op(eng, lambda e: e.tensor_scalar(out, in0, s1, None, op0), reads, writes)
            else:
                P.op(eng, lambda e: e.tensor_scalar(out, in0, s1, s2, op0, op1), reads, writes)

        def stt(out, in0, scalar, in1, op0, op1, reads, writes, eng='dve'):
            P.op(eng, lambda e: e.scalar_tensor_tensor(out, in0, scalar, in1, op0, op1), reads, writes)

        def cp(out, in_, reads, writes, eng='dve'):
            if eng == 'act':
                P.op('act', lambda e: e.copy(out, in_), reads, writes)
            else:
                P.op(eng, lambda e: e.tensor_copy(out, in_), reads, writes)

        def dma(eng, out, in_, reads, writes, slot=None):
            if slot is None:
                cnt['slot'] = (cnt['slot'] + 1) % 6
                slot = cnt['slot']
            P.dma(eng, slot, lambda e: e.dma_start(out=out, in_=in_), reads, writes)

        def gather(out, in_, idx_ap, reads, writes, slot, eoff=0):
            P.dma('pool', slot, lambda e: e.indirect_dma_start(
                out=out, out_offset=None, in_=in_,
                in_offset=bass.IndirectOffsetOnAxis(ap=idx_ap, axis=0), element_offset=eoff), reads, writes)

        def load_w(src2d, ncols, nk=8, dst=None, key=None, slot=None, coff=0):
            if dst is None:
                cnt['w'] = (cnt['w'] + 1) % 3
                s = cnt['w']
                dst = W[s]
                key = ('w', s)
                slot = 8 + s
            if slot is None:
                slot = 11
            P.dma('pool', slot, lambda e: e.dma_start(
                out=dst[:, 0:nk, coff:coff + ncols], in_=src2d.rearrange("(k p) n -> p k n", p=128)), [], [key])
            return dst, key

        def blk_of_tile(i):
            return min(i // 4, 4) if i < 16 else 4

        for dst, src in ((CT32, cT_d), (NG, ng_d), (BC, bc_d), (BFN, bf_d), (PSC, psc_d), (CW, cw_d),
                         (IDF, ident_d), (MASKC, maskc_d), (U64, u64_d), (I8, i8_d),
                         (MASKN, maskn_d), (RC, rc_d), (PIOTA, piota_d), (PTT, ptT_d)):
            dma('sp', dst[:], src, [], [('c', dst.name)], slot=6)
        dma('sp', PTB[:], pt_d.partition_broadcast(128), [], [('c', 'ptb')], slot=6)
        cp(IDB[:], IDF[:], [('c', 'idf')], [('c', 'idb')])
        cp(CTB[:], CT32[:], [('c', 'ct32')], [('c', 'ctb')])
        P.op('dve', lambda e: e.memset(ONE1[:], 1.0), [], [('c', 'one1')])
        P.op('dve', lambda e: e.memset(ONES8[:], 1.0), [], [('c', 'ones8')])
        P.op('dve', lambda e: e.memset(ONEB[:], 1.0), [], [('c', 'oneb')])
        ts(IDX[:], PTB[:], 128.0, PIOTA[:, 0:1], ALU.mult, ALU.add, [('c', 'ptb'), ('c', 'piota_s')],
           [('c', 'idx')])
        ts(BFN[:], BFN[:], -1.0, None, ALU.mult, None, [('c', 'bfn')], [('c', 'bfn')])
        ts(BC[:, :, 8:16], BC[:, :, 8:16], 1.0, None, ALU.add, None, [('c', 'bc')], [('c', 'bc')])
        P.barrier()

        for l in range(NL):
            for cb in range(6):
                wt, wk = load_w(w_cond_d[l][:, cb * 512:(cb + 1) * 512], 512)
                for jj in range(4):
                    j = cb * 4 + jj
                    bk = nbank()
                    for kc in range(8):
                        mm(PS[:, bk, 0:5], wt[:, kc, jj * 128:(jj + 1) * 128], CTB[:, kc, :], kc == 0, kc == 7,
                           [wk, ('c', 'ctb')], [('ps', bk)])
                    if j < 8 or j >= 16:
                        ts(MOD[:, l, j, :], PS[:, bk, 0:5], BC[:, l, j:j + 1], None, ALU.add, None,
                           [('ps', bk), ('c', 'bc')], [('mod',)])
                    else:
                        ts(MOD[:, l, j, :], PS[:, bk, 0:5], BC[:, l, j:j + 1], NG[:, l, j - 8:j - 7], ALU.add,
                           ALU.mult, [('ps', bk), ('c', 'bc'), ('c', 'ng')], [('mod',)])
        P.barrier()
        if STOP <= 1:
            P.emit()
            return nc

        def seq_groups(c0, n):
            if c0 < NPR:
                return [(c0, n, 0)]
            return [(NPR + 8 * s, 8, 1 + s) for s in range(NSEQ)]

        def norm_tile(l, i, xt, xkey):
            t0, n = TIL[i]
            b = blk_of_tile(i)
            P.op('dve', lambda e: e.memset(SS[:, 0:1], 0.0), [], [('ss',)])
            act(XN[:n, :], xt[:n, :], AF.Square, [xkey, ('ss',)], [('xn',), ('ss',)], accum_out=SS[:n, 0:1])
            ts(SS[:n, 1:2], SS[:n, 0:1], 1.0 / D, 1e-6, ALU.mult, ALU.add, [('ss',)], [('ss',)])
            act(SS[:n, 3:4], SS[:n, 1:2], AF.Sqrt, [('ss',)], [('ss',)])
            P.op('dve', lambda e: e.reciprocal(SS[:n, 2:3], SS[:n, 3:4]), [('ss',)], [('ss',)])
            act(XN[:n, :], xt[:n, :], AF.Copy, [xkey, ('ss',)], [('xn',)], scale=SS[:n, 2:3])
            bk = nbank()
            pb = PS[:, bk, :].bitcast(BF16).rearrange("p (a b) -> p a b", a=8)
            for kc in range(8):
                tr(pb[:, kc, 0:n], XN[:n, kc * 128:(kc + 1) * 128], IDB[:n, :n], [('xn',), ('c', 'idb')],
                   [('ps', bk)])
            for kc in range(8):
                for (c0, nn, sq) in seq_groups(t0, n):
                    ts(HT[:, kc, c0:c0 + nn], pb[:, kc, c0 - t0:c0 - t0 + nn], MOD[:, l, 8 + kc, sq:sq + 1],
                       MOD[:, l, kc, sq:sq + 1], ALU.mult, ALU.add, [('ps', bk), ('mod',)], [('ht', b)])

        def proj_fm(wt, wk, chunks, evac, blocks=None, nk=8, rhs=None, rkey=None, m=128):
            for jj in chunks:
                for b, (c0, n) in enumerate(BLK):
                    if blocks is not None and b not in blocks:
                        continue
                    bk = nbank()
                    for kc in range(nk):
                        if rhs is None:
                            r_ap, rk = HT[:, kc, c0:c0 + n], ('ht', b)
                        else:
                            r_ap, rk = rhs[:, kc, c0:c0 + n], rkey
                        mm(PS[0:m, bk, 0:n], wt[:, kc, jj * 128:jj * 128 + m], r_ap, kc == 0, kc == nk - 1,
                           [wk, rk], [('ps', bk)])
                    evac(jj, b, c0, n, bk)

        def proj_tm(wt, wk, i, ncols=512):
            t0, n = TIL[i]
            bk = nbank()
            for kc in range(8):
                mm(PS[0:n, bk, 0:ncols], HT[:, kc, t0:t0 + n], wt[:, kc, 0:ncols], kc == 0, kc == 7,
                   [wk, ('ht', blk_of_tile(i))], [('ps', bk)])
            return bk

        def stage():
            cnt['stg'] = (cnt['stg'] + 1) % 2
            return cnt['stg']

        for l in range(NL):
            if l == 0:
                for i, (t0, n) in enumerate(TIL):
                    xt = XT[i % 2]
                    dma('sp', xt[:n, :], xin[t0:t0 + n, :], [], [('xt', i % 2)])
                    norm_tile(0, i, xt, ('xt', i % 2))
            P.barrier()
            if STOP <= 2:
                P.emit()
                return nc

            QT = R1[:, 0:4 * NT].rearrange("p (c t) -> p c t", c=4)
            KT = R1[:, 4 * NT:8 * NT].rearrange("p (c t) -> p c t", c=4)
            VA = R2[:, 0:16 * 768].rearrange("p (i c) -> p i c", i=16)
            o = 16 * 768
            LQ = [R2[:, o + 1024 * k:o + 1024 * (k + 1)].bitcast(F32) for k in range(2)]
            o += 2048
            TMP = [R2[:, o + 1024 * k:o + 1024 * (k + 1)].bitcast(F32) for k in range(2)]
            o += 2048
            PT = [R2[:, o + 512 * k:o + 512 * (k + 1)] for k in range(2)]
            o += 1024
            SAG = R2[:, o:o + NT]
            o += NT
            RD = R2[:, o:o + 1024].bitcast(F32)
            o += 1024
            OTM = R2[:, o:o + 1024].bitcast(F32)
            o += 1024
            VS = R2[:, o:o + 512]
            o += 512
            assert o <= 22528

            VA4 = VA.rearrange("p i (c k) -> p i c k", c=4)
            P.op('dve', lambda e: e.memset(VA4[:, :, :, 64:128], 1.0), [], [('va', i) for i in range(16)])

            wt, wk = load_w(w_in_d[l][:, COL['q']:COL['q'] + 512], 512)
            proj_fm(wt, wk, range(4), lambda jj, b, c0, n, bk: cp(
                QT[:, jj, c0:c0 + n], PS[:, bk, 0:n], [('ps', bk)], [('qt', b)], eng='act'))
            if STOP <= 2.1:
                P.emit()
                return nc
            wt, wk = load_w(w_in_d[l][:, COL['k']:COL['k'] + 512], 512)
            proj_fm(wt, wk, range(4), lambda jj, b, c0, n, bk: cp(
                KT[:, jj, c0:c0 + n], PS[:, bk, 0:n], [('ps', bk)], [('kt', b)], eng='act'))
            if STOP <= 2.15:
                P.emit()
                return nc
            for i, (t0, n) in enumerate(TIL):
                bk = proj_tm(wt, wk, i)
                s = stage()
                cp(STG[s][:n, :], PS[:n, bk, :], [('ps', bk)], [('stg', s)])
                dma('sp', nk_d[l, t0:t0 + n, :], STG[s][:n, :], [('stg', s)], [])
            if STOP <= 2.2:
                P.emit()
                return nc
            wt, wk = load_w(w_in_d[l][:, COL['v']:COL['v'] + 512], 512)
            for i, (t0, n) in enumerate(TIL):
                bk = proj_tm(wt, wk, i)
                s = stage()
                cp(STG[s][:n, :], PS[:n, bk, :], [('ps', bk)], [('stg', s)])
                dma('sp', nv_d[l, t0:t0 + n, :], STG[s][:n, :], [('stg', s)], [])
                if NOVA:
                    pass
                elif i < 16:
                    src = PS[:, bk, :].rearrange("p (c h k) -> p c h k", c=4, h=2)
                    dst = VA[:, i, :].rearrange("p (c k) -> p c k", c=4)
                    cp(dst[:, :, 0:64], src[:, :, 0, :], [('ps', bk)], [('va', i)])
                    cp(dst[:, :, 128:192], src[:, :, 1, :], [('ps', bk)], [('va', i)])
                else:
                    cp(VS[:n, :], PS[:n, bk, :], [('ps', bk)], [('vs',)])
            if STOP <= 2.3:
                P.emit()
                return nc
            load_w(w_in_d[l][:, COL['fl']:COL['fl'] + 8], 8, dst=WFL, key=('wfl',))

            def ev_fl(jj, b, c0, n, bk):
                act(LC[:, 1, c0:c0 + n], PS[0:8, bk, 0:n], AF.Exp, [('ps', bk), ('c', 'bfn')], [('lc',)],
                    bias=BFN[:, l:l + 1], scale=-1.0)
                act(LC[:, 1, c0:c0 + n], LC[:, 1, c0:c0 + n], AF.Ln, [('lc',)], [('lc',)], bias=1.0)
                ts(LC[:, 0, c0:c0 + n], LC[:, 1, c0:c0 + n], -1.0, None, ALU.mult, None, [('lc',)], [('lc',)])
            proj_fm(WFL, ('wfl',), [0], ev_fl, m=8)
            if STOP <= 2.5:
                P.emit()
                return nc
            P.op('dve', lambda e: e.tensor_tensor_scan(
                LC[:, 1, 0:NPR], ONE1[0:8, 0:1].to_broadcast([8, NPR]), LC[:, 0, 0:NPR], 0.0, ALU.mult,
                ALU.add), [('lc',), ('c', 'one1')], [('lc',)])
            for s in range(NSEQ):
                a0 = NPR + 8 * s
                P.op('dve', lambda e, a0=a0: e.tensor_tensor_scan(
                    LC[:, 1, a0:a0 + 8], ONE1[0:8, 0:1].to_broadcast([8, 8]), LC[:, 0, a0:a0 + 8], 0.0,
                    ALU.mult, ALU.add), [('lc',), ('c', 'one1')], [('lc',)])
            if STOP <= 2.6:
                P.emit()
                return nc
            for i, (t0, n) in enumerate(TIL):
                bk = nbank()
                tr(PS[0:n, bk, 0:8], LC[:, 0, t0:t0 + n], IDF[0:8, 0:8], [('lc',), ('c', 'idf')], [('ps', bk)])
                tr(PS[0:n, bk, 8:16], LC[:, 1, t0:t0 + n], IDF[0:8, 0:8], [('lc',), ('c', 'idf')], [('ps', bk)])
                cp(LFTM[:n, i, :], PS[0:n, bk, 0:8], [('ps', bk)], [('lftm',)])
                ts(NCTM[:n, i, :], PS[0:n, bk, 8:16], -1.0, None, ALU.mult, None, [('ps', bk)], [('nctm',)])
                if i == 16:
                    cp(CTM[:n, :], PS[0:n, bk, 8:16], [('ps', bk)], [('ctm',)])
            if STOP <= 2.8:
                P.emit()
                return nc
            dma('sp', nlf_d[l, 0:NPR, :].rearrange("(i p) h -> p i h", p=128), LFTM[:, 0:16, :], [('lftm',)], [])
            dma('sp', nlf_d[l, NPR:NT, :], LFTM[0:32, 16, :], [('lftm',)], [])

            if STOP <= 3:
                P.emit()
                return nc
            wag, wagk = load_w(w_in_d[l][:, COL['ag']:COL['ag'] + 512], 512)
            for c in range(4):
                def ev_ag(jj, b, c0, n, bk, c=c):
                    if b < 4:
                        act(SAG[:, c0:c0 + n], PS[:, bk, 0:n], AF.Silu, [('ps', bk)], [('sag',)])
                    else:
                        act(SAGS[:, c, :], PS[:, bk, 0:n], AF.Silu, [('ps', bk)], [('sags',)])
                proj_fm(wag, wagk, [c], ev_ag)
                its = [(half, b) for half in range(2) for b in range(4)]

                def emit_lq(j, c=c):
                    half, b = its[j]
                    h = 2 * c + half
                    q0 = 512 * b
                    bk = nbank()
                    ts(SELH[:, j % 2, :], ONES8[:, :], I8[:, h:h + 1], None, ALU.mult, None,
                       [('c', 'ones8'), ('c', 'i8_s')], [('selh', j % 2)])
                    mm(PS[:, bk, :], SELH[:, j % 2, :], LC[:, 1, q0:q0 + 512], True, True,
                       [('selh', j % 2), ('lc',)], [('ps', bk)])
                    cp(LQ[j % 2], PS[:, bk, :], [('ps', bk)], [('lq', j % 2)], eng='act')
                emit_lq(0)
                for j, (half, b) in enumerate(its):
                    h = 2 * c + half
                    p0 = 64 * half
                    q0 = 512 * b
                    lq = LQ[j % 2]
                    lqk = ('lq', j % 2)
                    ab = abank()
                    last = 4 * b + 3

                    def emit_s(i, c=c, p0=p0, q0=q0, b=b):
                        r = i - 4 * b
                        lo = 128 * r if r >= 0 else 0
                        n = 512 - lo
                        sbk = sbank()
                        mm(PS[:, sbk, 0:n], KT[p0:p0 + 64, c, 128 * i:128 * i + 128],
                           QT[p0:p0 + 64, c, q0 + lo:q0 + 512], True, True,
                           [('kt', i // 4), ('qt', b)], [('ps', sbk)])
                        return sbk, r, lo, n
                    pend = emit_s(0)
                    if j + 1 < len(its):
                        emit_lq(j + 1)
                    for i in range(last + 1):
                        sbk, r, lo, n = pend
                        if i < last:
                            pend = emit_s(i + 1)
                        k2 = (i % 2)
                        stt(TMP[k2][:, 0:n], PS[:, sbk, 0:n], 0.125, lq[:, lo:512], ALU.mult, ALU.add,
                            [('ps', sbk), lqk], [('tmp', k2)])
                        if r >= 0:
                            tt(TMP[k2][:, 0:128], TMP[k2][:, 0:128], MASKC[:], ALU.add,
                               [('tmp', k2), ('c', 'maskc_s')], [('tmp', k2)])
                        act(PT[k2][:, 0:n], TMP[k2][:, 0:n], AF.Exp, [('tmp', k2), ('nctm',)], [('pt', k2)],
                            bias=NCTM[:, i, h:h + 1])
                        mm(PS[:, ab, lo:512], VA[:, i, c * 192 + 64 * half:c * 192 + 64 * half + 128],
                           PT[k2][:, 0:n], i == 0, i == last, [('va', i), ('pt', k2)], [('ps', ab)])
                    d0 = 64 - p0
                    act(RD[d0:d0 + 64, :], PS[d0:d0 + 64, ab, :], AF.Ln, [('ps', ab)], [('rd',)])
                    act(RD[d0:d0 + 64, :], RD[d0:d0 + 64, :], AF.Exp, [('rd',)], [('rd',)], scale=-1.0)
                    tt(OTM[p0:p0 + 64, :], PS[p0:p0 + 64, ab, :], RD[d0:d0 + 64, :], ALU.mult,
                       [('ps', ab), ('rd',)], [('otm',)])
                    tt(BINB[p0:p0 + 64, c, q0:q0 + 512], OTM[p0:p0 + 64, :], SAG[p0:p0 + 64, q0:q0 + 512],
                       ALU.mult, [('otm',), ('sag',)], [('binb', b)])
            P.barrier()

            if STOP <= 4:
                P.emit()
                return nc
            o = 0
            KP = [R2[:, o + 512 * k:o + 512 * (k + 1)] for k in range(4)]
            o += 2048
            VP = [R2[:, o + 512 * k:o + 512 * (k + 1)] for k in range(16)]
            o += 8192
            KTP = [R2[:, o + 512 * k:o + 512 * (k + 1)].rearrange("p (c t) -> p c t", c=4) for k in range(2)]
            o += 1024
            QBD = R2[:, o:o + 64].rearrange("p (c k) -> p c k", c=4)
            o += 64
            LFPF = R2[:, o:o + 2048].bitcast(F32)
            LFP = LFPF[0:64, :]
            o += 2048
            PFX = R2[0:64, o:o + 2048].bitcast(F32)
            o += 2048
            EPG = R2[0:64, o:o + 16].bitcast(F32)
            o += 16
            SUF = R2[:, o:o + 1024].bitcast(F32).rearrange("p (h g) -> p h g", h=8)
            o += 1024
            BD = R2[0:8, o:o + 128].bitcast(F32).rearrange("p (h t) -> p h t", h=8)
            o += 128
            LN = R2[:, o:o + 128].bitcast(F32)
            o += 128
            BN = R2[0:32, o:o + 128].bitcast(F32)
            o += 128
            T1 = [R2[:, o + 1024 * k:o + 1024 * (k + 1)].bitcast(F32) for k in range(2)]
            o += 2048
            PD = [R2[:, o + 512 * k:o + 512 * (k + 1)] for k in range(2)]
            o += 1024
            OD = R2[0:64, o:o + 1024].bitcast(F32)
            o += 1024
            RDD = R2[0:64, o:o + 2].bitcast(F32)
            o += 2
            assert o <= 22528
            ck_l = ck_d.rearrange("l r c -> (l r) c")
            cv_l = cv_d.rearrange("l r c -> (l r) c")
            clf_l = clf_d.rearrange("l r c -> (l r) c")
            kv_off = l * NPOOL * 128 * 512
            lf_off = l * NPOOL * 1024
            for s in range(NSEQ):
                sc0 = NPR + 8 * s
                gather(LFPF[:, :], clf_l, PTT[:, s:s + 1], [('c', 'ptt')], [('lfp',)], slot=12, eoff=lf_off)
                for h in range(8):
                    P.op('dve', lambda e, h=h: e.tensor_tensor_scan(
                        PFX[:, h:1024:8], ONE1[0:64, 0:1].to_broadcast([64, 128]), LFP[:, h:1024:8], 0.0,
                        ALU.mult, ALU.add), [('lfp',), ('c', 'one1')], [('pfx',)])
                bk = nbank()
                mm(PS[0:64, bk, 0:8], U64[:, :], PFX[:, 1016:1024], True, True, [('c', 'u64_s'), ('pfx',)],
                   [('ps', bk)])
                tt(EPG[:, :], PS[0:64, bk, 0:8], PFX[:, 1016:1024], ALU.add, [('ps', bk), ('pfx',)], [('epg',)])
                pf3 = PFX.rearrange("p (t h) -> p t h", h=8)
                tt(pf3, EPG[:, :].unsqueeze(1).to_broadcast([64, 128, 8]), pf3, ALU.subtract,
                   [('epg',), ('pfx',)], [('pfx',)])
                bk = nbank()
                for h in range(8):
                    tr(PS[:, bk, 64 * h:64 * h + 64], PFX[:, h:1024:8], IDF[0:64, 0:64], [('pfx',), ('c', 'idf')],
                       [('ps', bk)])
                cp(SUF.rearrange("p h g -> p (h g)"), PS[:, bk, :], [('ps', bk)], [('suf',)])
                tt(BD, LC[:, 1, sc0:sc0 + 8].unsqueeze(1).to_broadcast([8, 8, 8]),
                   I8[:, :].unsqueeze(2).to_broadcast([8, 8, 8]), ALU.mult, [('lc',), ('c', 'i8_s')], [('bd',)])
                bk = nbank()
                mm(PS[:, bk, 0:64], ONES8[:, :], BD.rearrange("p h t -> p (h t)"),
                   True, True, [('bd',), ('c', 'ones8')], [('ps', bk)])
                cp(LN[:, :], PS[:, bk, 0:64], [('ps', bk)], [('ln',)])
                bn3 = BN.rearrange("p (h t) -> p h t", h=8)
                tt(bn3, LN[0:32, :].rearrange("p (h t) -> p h t", h=8),
                   CTM[0:32, :].unsqueeze(2).to_broadcast([32, 8, 8]), ALU.subtract, [('ln',), ('ctm',)], [('bn',)])
                tt(bn3, bn3, MASKN[:, s, :].unsqueeze(1).to_broadcast([32, 8, 8]), ALU.add,
                   [('bn',), ('c', 'maskn_s')], [('bn',)])
                P.op('dve', lambda e: e.memset(QBD, 0.0), [], [('qbd',)])
                for c in range(4):
                    cp(QBD[0:64, c, 0:8], QT[0:64, c, sc0:sc0 + 8], [('qt', 4)], [('qbd',)])
                    cp(QBD[64:128, c, 8:16], QT[64:128, c, sc0:sc0 + 8], [('qt', 4)], [('qbd',)])
                ob = abank()
                db = abank()
                npages = NPG + 1
                for g0 in range(0, NPG, 8):
                    sbk = sbank()
                    for gi in range(8):
                        g = g0 + gi
                        kslot = g % 4
                        vslot = g % 16
                        col = s * NPG + g
                        gather(KP[kslot], ck_l, IDX[:, col:col + 1], [('c', 'idx')], [('kp', kslot)], slot=13 + g % 4, eoff=kv_off)
                        gather(VP[vslot], cv_l, IDX[:, col:col + 1], [('c', 'idx')], [('vp', vslot)], slot=17 + g % 4, eoff=kv_off)
                        tb = nbank()
                        tpb = PS[:, tb, :].bitcast(BF16).rearrange("p (a b) -> p a b", a=8)
                        for c in range(4):
                            tr(tpb[:, c, :], KP[kslot][:, 128 * c:128 * c + 128], IDB[:, :], [('kp', kslot), ('c', 'idb')],
                               [('ps', tb)])
                        k2 = g % 2
                        if g % 2 == 0:
                            cp(KTP[k2], tpb[:, 0:4, :], [('ps', tb)], [('ktp', k2)], eng='act')
                        else:
                            cp(KTP[k2], tpb[:, 0:4, :], [('ps', tb)], [('ktp', k2)])
                        for c in range(4):
                            mm(PS[:, sbk, 64 * gi + 16 * c:64 * gi + 16 * c + 16], KTP[k2][:, c, :], QBD[:, c, :],
                               True, True, [('ktp', k2), ('qbd',)], [('ps', sbk)])
                    k3 = (g0 // 8) % 2
                    for gi in range(8):
                        s3 = PS[:, sbk, 64 * gi:64 * gi + 64].rearrange("p (h t) -> p h t", h=8)
                        t3 = T1[k3][:, 64 * gi:64 * gi + 64].rearrange("p (h t) -> p h t", h=8)
                        suf3 = SUF[:, :, g0 + gi:g0 + gi + 1].to_broadcast([128, 8, 8])
                        stt(t3, s3, 0.125, suf3, ALU.mult, ALU.add, [('ps', sbk), ('suf',)], [('t1', k3)])
                    t3 = T1[k3].rearrange("p (g k) -> p g k", g=8)
                    ln3 = LN[:, :].unsqueeze(1).to_broadcast([128, 8, 64])
                    tt(t3, t3, ln3, ALU.add, [('t1', k3), ('ln',)], [('t1', k3)])
                    act(PD[k3][:, :], T1[k3], AF.Exp, [('t1', k3)], [('pd', k3)])
                    for gi in range(8):
                        g = g0 + gi
                        vslot = g % 16
                        mm(PS[0:64, ob, :], PD[k3][:, 64 * gi:64 * gi + 64], VP[vslot][:, :], g == 0, False,
                           [('pd', k3), ('vp', vslot)], [('ps', ob)])
                        mm(PS[0:64, db, 0:1], PD[k3][:, 64 * gi:64 * gi + 64], ONEB[:, 0:1], g == 0, False,
                           [('pd', k3), ('c', 'oneb')], [('ps', db)])
                sbk = sbank()
                for c in range(4):
                    mm(PS[0:32, sbk, 16 * c:16 * c + 16], KT[:, c, NPR:NT], QBD[:, c, :], True, True,
                       [('kt', 4), ('qbd',)], [('ps', sbk)])
                stt(T1[0][0:32, 0:64], PS[0:32, sbk, 0:64], 0.125, BN[:, :], ALU.mult, ALU.add,
                    [('ps', sbk), ('bn',)], [('t1', 0)])
                act(PD[0][0:32, 0:64], T1[0][0:32, 0:64], AF.Exp, [('t1', 0)], [('pd', 0)])
                mm(PS[0:64, ob, :], PD[0][0:32, 0:64], VS[0:32, :], False, True, [('pd', 0), ('vs',)], [('ps', ob)])
                mm(PS[0:64, db, 0:1], PD[0][0:32, 0:64], ONEB[0:32, 0:1], False, True, [('pd', 0), ('c', 'oneb')],
                   [('ps', db)])
                P.op('dve', lambda e, db=db: e.reciprocal(RDD[:, 0:1], PS[0:64, db, 0:1]), [('ps', db)], [('rdd',)])
                ts(OD[:, :], PS[0:64, ob, :], RDD[:, 0:1], None, ALU.mult, None, [('ps', ob), ('rdd',)], [('od',)])
                for c in range(4):
                    bk = nbank()
                    tr(PS[:, bk, 0:64], OD[:, 128 * c:128 * c + 128], IDF[0:64, 0:64], [('od',), ('c', 'idf')],
                       [('ps', bk)])
                    tt(BINB[0:64, c, sc0:sc0 + 8], PS[0:64, bk, 16 * c:16 * c + 8], SAGS[0:64, c, 8 * s:8 * s + 8],
                       ALU.mult, [('ps', bk), ('sags',)], [('binb', 4)])
                    tt(BINB[64:128, c, sc0:sc0 + 8], PS[64:128, bk, 16 * c + 8:16 * c + 16],
                       SAGS[64:128, c, 8 * s:8 * s + 8], ALU.mult, [('ps', bk), ('sags',)], [('binb', 4)])
            P.barrier()
            if STOP <= 5:
                P.emit()
                return nc
            BINA = R1[:, 0:4 * NT].rearrange("p (c t) -> p c t", c=4)
            BINC = R1[:, 4 * NT:8 * NT].rearrange("p (c t) -> p c t", c=4)
            LP = 15 + NPR
            o = 0
            PU = R2[:, o:o + 4128].bitcast(F32)
            o += 4128
            SA = R2[:, o:o + 4128].bitcast(F32)
            o += 4128
            SB_ = R2[:, o:o + 4128].bitcast(F32)
            o += 4128
            DD = R2[:, o:o + NT]
            o += NT
            SPG = R2[:, o:o + NT]
            o += NT
            PUS = R2[:, o:o + 184].bitcast(F32).rearrange("p (s t) -> p s t", s=4)
            o += 184
            SAS = R2[:, o:o + 184].bitcast(F32).rearrange("p (s t) -> p s t", s=4)
            o += 184
            SBS = R2[:, o:o + 184].bitcast(F32).rearrange("p (s t) -> p s t", s=4)
            o += 184
            o = 0
            UU = R2[:, o:o + 4104].bitcast(F32)
            o += 4104
            US = R2[:, o:o + 80].bitcast(F32).rearrange("p (s t) -> p s t", s=4)
            o += 80
            CHS = R2[:, o:o + 1024].bitcast(F32)
            o += 1024
            SG = R2[:, o:o + 1024].bitcast(F32)
            o += 1024
            TB = R2[:, o:o + 1024].bitcast(F32)
            o += 1024
            YC = R2[:, o:o + 1024].bitcast(F32)
            o += 1024
            assert o <= 22528
            P.op('dve', lambda e: e.memset(R2[:, 0:18000], 0.0), [], [('r2z',)])
            P.barrier()
            wpu, wpuk = load_w(w_in_d[l][:, COL['pu']:COL['pu'] + 512], 512)
            wpg, wpgk = load_w(w_in_d[l][:, COL['pg']:COL['pg'] + 512], 512)
            load_w(pool_w_d[l].rearrange("g c d -> (g c) d"), 128, nk=4, dst=WPW, key=('wpw',))
            dma('sp', nps_d[l, :, 0:7, :], sp_d[l, :, 8:15, :], [], [])
            for g in range(4):
                w = WIN[g]

                def ev_pu(jj, b, c0, n, bk):
                    if b < 4:
                        cp(PU[:, 15 + c0:15 + c0 + n], PS[:, bk, 0:n], [('ps', bk)], [('pu',)], eng='act')
                    else:
                        cp(PUS[:, :, 15:23], PS[:, bk, 0:32].rearrange("p (s t) -> p s t", s=4), [('ps', bk)],
                           [('pus',)], eng='act')
                dma('sp', PUS[:, :, 0:15], spT_d[l][:, g, :, :], [], [('pus',)])
                proj_fm(wpu, wpuk, [g], ev_pu)

                def ev_pg(jj, b, c0, n, bk):
                    act(SPG[:, c0:c0 + n], PS[:, bk, 0:n], AF.Silu, [('ps', bk)], [('spg',)])
                proj_fm(wpg, wpgk, [g], ev_pg)
                cur, curs = PU, PUS
                k = 1
                nxt = [(SA, SAS), (SB_, SBS)]
                ni = 0
                while k < w:
                    d, ds_ = nxt[ni]
                    ni ^= 1
                    tt(d[:, k:LP], cur[:, k:LP], cur[:, 0:LP - k], ALU.add, [('pu',), ('sw',)], [('sw',)])
                    tt(ds_[:, :, k:23], curs[:, :, k:23], curs[:, :, 0:23 - k], ALU.add, [('pus',), ('sws',)],
                       [('sws',)])
                    cur, curs = d, ds_
                    k *= 2
                for b, (c0, n) in enumerate(BLK):
                    if b < 4:
                        stt(DD[:, c0:c0 + n], cur[:, 15 + c0:15 + c0 + n], 1.0 / w, PU[:, 15 + c0:15 + c0 + n],
                            ALU.mult, ALU.subtract, [('sw',), ('pu',)], [('dd', b)])
                        if b == 0:
                            tt(T16[:, :], cur[:, 15:31], RC[:, g, :], ALU.mult, [('sw',), ('c', 'rc_s')], [('t16',)])
                            tt(DD[:, 0:16], T16[:, :], PU[:, 15:31], ALU.subtract, [('t16',), ('pu',)], [('dd', 0)])
                    else:
                        stt(DD[:, NPR:NT].rearrange("p (s t) -> p s t", s=4), curs[:, :, 15:23], 1.0 / w,
                            PUS[:, :, 15:23], ALU.mult, ALU.subtract, [('sws',), ('pus',)], [('dd', 4)])
                    bk = nbank()
                    mm(PS[:, bk, 0:n], WPW[:, g, :], DD[:, c0:c0 + n], True, True, [('wpw',), ('dd', b)],
                       [('ps', bk)])
                    stt(BINA[:, g, c0:c0 + n], PS[:, bk, 0:n], PSC[:, l, g:g + 1], SPG[:, c0:c0 + n], ALU.mult,
                        ALU.mult, [('ps', bk), ('c', 'psc'), ('spg',)], [('bina', b)])
                bk = nbank()
                tr(PS[0:15, bk, 0:128], PU[:, LP - 15:LP], IDF[:, :], [('pu',), ('c', 'idf')], [('ps', bk)])
                cp(NPV[0:15, 0, :], PS[0:15, bk, 0:128], [('ps', bk)], [('npv', 0)])
                dma('sp', npp_d[l, :, 128 * g:128 * g + 128], NPV[0:15, 0, :], [('npv', 0)], [])
                for s in range(NSEQ):
                    bk = nbank()
                    tr(PS[0:8, bk, 0:128], PUS[:, s, 15:23], IDF[:, :], [('pus',), ('c', 'idf')], [('ps', bk)])
                    cp(NPV[0:8, 1 + s, :], PS[0:8, bk, 0:128], [('ps', bk)], [('npv', 1 + s)])
                    dma('sp', nps_d[l, s, 7:15, 128 * g:128 * g + 128], NPV[0:8, 1 + s, :], [('npv', 1 + s)], [])

            P.barrier()
            if STOP <= 6:
                P.emit()
                return nc
            for c in range(4):
                wa, wak = W[0], ('w', 0)
                wb, wbk = W[1], ('w', 1)
                a0 = c * 128
                load_w(w_in_d[l][:, COL['ch'] + a0:COL['ch'] + a0 + 128], 128, dst=W[0], key=wak, slot=8, coff=0)
                load_w(w_in_d[l][:, COL['cc'] + a0:COL['cc'] + a0 + 128], 128, dst=W[0], key=wak, slot=8, coff=128)
                load_w(w_in_d[l][:, COL['cb'] + a0:COL['cb'] + a0 + 128], 128, dst=W[1], key=wbk, slot=9, coff=0)
                load_w(w_in_d[l][:, COL['cg'] + a0:COL['cg'] + a0 + 128], 128, dst=W[1], key=wbk, slot=9, coff=128)
                dma('sp', US[:, :, 0:2], scT_d[l][:, c, :, :], [], [('us',)])
                P.op('dve', lambda e: e.memset(UU[:, 0:2], 0.0), [], [('uu',)])
                for b, (c0, n) in enumerate(BLK):
                    def one(wt, wk, jj):
                        bk = nbank()
                        for kc in range(8):
                            mm(PS[:, bk, 0:n], wt[:, kc, jj * 128:jj * 128 + 128], HT[:, kc, c0:c0 + n], kc == 0,
                               kc == 7, [wk, ('ht', b)], [('ps', bk)])
                        return bk
                    bk = one(wa, wak, 0)
                    cp(CHS[:, 0:n], PS[:, bk, 0:n], [('ps', bk)], [('chs',)], eng='act')
                    bk = one(wa, wak, 1)
                    if b < 4:
                        tt(UU[:, 2 + c0:2 + c0 + n], PS[:, bk, 0:n], CHS[:, 0:n], ALU.mult, [('ps', bk), ('chs',)],
                           [('uu',)])
                    else:
                        tt(US[:, :, 2:10], PS[:, bk, 0:32].rearrange("p (s t) -> p s t", s=4),
                           CHS[:, 0:32].rearrange("p (s t) -> p s t", s=4), ALU.mult, [('ps', bk), ('chs',)],
                           [('us',)])
                    bk = one(wb, wbk, 1)
                    act(SG[:, 0:n], PS[:, bk, 0:n], AF.Silu, [('ps', bk)], [('sg',)])
                    bk = one(wb, wbk, 0)
                    tt(TB[:, 0:n], PS[:, bk, 0:n], SG[:, 0:n], ALU.mult, [('ps', bk), ('sg',)], [('tb',)])
                    if b < 4:
                        u0, u1, u2 = (UU[:, c0 + k:c0 + k + n] for k in range(3))
                        yc, tb, oc = YC[:, 0:n], TB[:, 0:n], BINC[:, c, c0:c0 + n]
                        uk = ('uu',)
                    else:
                        u0, u1, u2 = (US[:, :, k:k + 8] for k in range(3))
                        yc = YC[:, 0:32].rearrange("p (s t) -> p s t", s=4)
                        tb = TB[:, 0:32].rearrange("p (s t) -> p s t", s=4)
                        oc = BINC[:, c, NPR:NT].rearrange("p (s t) -> p s t", s=4)
                        uk = ('us',)
                    ts(yc, u0, CW[:, l, c, 0:1], None, ALU.mult, None, [uk, ('c', 'cw')], [('yc',)])
                    stt(yc, u1, CW[:, l, c, 1:2], yc, ALU.mult, ALU.add, [uk, ('c', 'cw'), ('yc',)], [('yc',)])
                    stt(yc, u2, CW[:, l, c, 2:3], yc, ALU.mult, ALU.add, [uk, ('c', 'cw'), ('yc',)], [('yc',)])
                    tt(oc, yc, tb, ALU.mult, [('yc',), ('tb',)], [('binc', b)])
                bk = nbank()
                tr(PS[0:2, bk, 0:128], UU[:, NPR:NPR + 2], IDF[:, :], [('uu',), ('c', 'idf')], [('ps', bk)])
                cp(NCV[0:2, 0, :], PS[0:2, bk, 0:128], [('ps', bk)], [('ncv', 0)])
                dma('sp', ncp_d[l, :, 128 * c:128 * c + 128], NCV[0:2, 0, :], [('ncv', 0)], [])
                for s in range(NSEQ):
                    bk = nbank()
                    tr(PS[0:2, bk, 0:128], US[:, s, 8:10], IDF[:, :], [('us',), ('c', 'idf')], [('ps', bk)])
                    cp(NCV[0:2, 1 + s, :], PS[0:2, bk, 0:128], [('ps', bk)], [('ncv', 1 + s)])
                    dma('sp', ncs_d[l, s, :, 128 * c:128 * c + 128], NCV[0:2, 1 + s, :], [('ncv', 1 + s)], [])
            P.barrier()

            if STOP <= 7:
                P.emit()
                return nc
            MRG = R2[:, 0:8 * NT].rearrange("p (c t) -> p c t", c=8)
            o = 8 * NT
            GS = [R2[:, o + 1024 * k:o + 1024 * (k + 1)].bitcast(F32) for k in range(3)]
            o += 3072
            TP = [R2[:, o + 1024 * k:o + 1024 * (k + 1)].bitcast(F32) for k in range(2)]
            o += 2048
            assert o <= 22528
            bins = (BINA, BINB, BINC)
            binkeys = ('bina', 'binb', 'binc')
            wbr_d = (w_bra_d, w_brb_d, w_brc_d)
            for J in range(2):
                gw = []
                for x in range(3):
                    gw.append(load_w(w_in_d[l][:, COL['mg'] + 1024 * x + 512 * J:COL['mg'] + 1024 * x + 512 * J + 512],
                                     512, dst=W[x], key=('w', x), slot=8 + x))
                    load_w(wbr_d[x][l][:, 512 * J:512 * J + 512], 512, nk=4, dst=WBR[x], key=('wbr', x), slot=11)
                for jj in range(4):
                    for b, (c0, n) in enumerate(BLK):
                        for x in range(3):
                            bk = nbank()
                            for kc in range(8):
                                mm(PS[:, bk, 0:n], W[x][:, kc, jj * 128:jj * 128 + 128], HT[:, kc, c0:c0 + n],
                                   kc == 0, kc == 7, [('w', x), ('ht', b)], [('ps', bk)])
                            act(GS[x][:, 0:n], PS[:, bk, 0:n], AF.Sigmoid, [('ps', bk)], [('gs', x)])
                            bk = nbank()
                            for kc in range(4):
                                mm(PS[:, bk, 0:n], WBR[x][:, kc, jj * 128:jj * 128 + 128], bins[x][:, kc, c0:c0 + n],
                                   kc == 0, kc == 3, [('wbr', x), (binkeys[x], b)], [('ps', bk)])
                            if x == 0:
                                tt(TP[0][:, 0:n], PS[:, bk, 0:n], GS[x][:, 0:n], ALU.mult, [('ps', bk), ('gs', x)],
                                   [('tp', 0)])
                            else:
                                tt(TP[1][:, 0:n], PS[:, bk, 0:n], GS[x][:, 0:n], ALU.mult, [('ps', bk), ('gs', x)],
                                   [('tp', 1)])
                                dst = TP[0][:, 0:n] if x == 1 else MRG[:, 4 * J + jj, c0:c0 + n]
                                dk = ('tp', 0) if x == 1 else ('mrg', b)
                                tt(dst, TP[0][:, 0:n], TP[1][:, 0:n], ALU.add, [('tp', 0), ('tp', 1)], [dk],
                                   eng='pool')
            P.barrier()

            if STOP <= 8:
                P.emit()
                return nc
            WO = R1[:, 0:8192].rearrange("p (k n) -> p k n", k=8)
            GO = R1[:, 8192:16384].bitcast(F32).rearrange("p (j t) -> p j t", j=8)
            load_w(w_o_d[l], 1024, dst=WO, key=('wo',), slot=8)
            if l == 1:
                dma('sp', FG, fg_d.partition_broadcast(128), [], [('c', 'fg')], slot=6)
            for b, (c0, n) in enumerate(BLK):
                for j in range(8):
                    bk = nbank()
                    for kc in range(8):
                        mm(PS[:, bk, 0:n], WO[:, kc, j * 128:j * 128 + 128], MRG[:, kc, c0:c0 + n], kc == 0, kc == 7,
                           [('wo',), ('mrg', b)], [('ps', bk)])
                    for (cc0, nn, sq) in seq_groups(c0, n):
                        ts(GO[:, j, cc0 - c0:cc0 - c0 + nn], PS[:, bk, cc0 - c0:cc0 - c0 + nn],
                           MOD[:, l, 16 + j, sq:sq + 1], None, ALU.mult, None, [('ps', bk), ('mod',)], [('go',)])
                tiles = [i for i in range(17) if blk_of_tile(i) == b]
                for i in tiles:
                    t0, tn = TIL[i]
                    xt = XT[i % 2]
                    xk = ('xt', i % 2)
                    src = xin if l == 0 else x1_d
                    dma('sp', xt[:tn, :], src[t0:t0 + tn, :], [('x1', i)] if l == 1 else [], [xk])
                    bA = nbank()
                    bB = nbank()
                    for j in range(8):
                        bk = bA if j < 4 else bB
                        tr(PS[0:tn, bk, (j % 4) * 128:(j % 4) * 128 + 128], GO[:, j, t0 - c0:t0 - c0 + tn], IDF[:, :],
                           [('go',), ('c', 'idf')], [('ps', bk)])
                    tt(xt[:tn, 0:512], PS[0:tn, bA, :], xt[:tn, 0:512], ALU.add, [('ps', bA), xk], [xk])
                    tt(xt[:tn, 512:1024], PS[0:tn, bB, :], xt[:tn, 512:1024], ALU.add, [('ps', bB), xk], [xk])
                    if l == 0:
                        dma('sp', x1_d[t0:t0 + tn, :], xt[:tn, :], [xk], [('x1', i)])
                        norm_tile(1, i, xt, xk)
                    else:
                        P.op('dve', lambda e: e.memset(SS[:, 0:1], 0.0), [], [('ss',)])
                        act(XN[:tn, :], xt[:tn, :], AF.Square, [xk, ('ss',)], [('xn',), ('ss',)],
                            accum_out=SS[:tn, 0:1])
                        ts(SS[:tn, 1:2], SS[:tn, 0:1], 1.0 / D, 1e-6, ALU.mult, ALU.add, [('ss',)], [('ss',)])
                        act(SS[:tn, 3:4], SS[:tn, 1:2], AF.Sqrt, [('ss',)], [('ss',)])
                        P.op('dve', lambda e, tn=tn: e.reciprocal(SS[:tn, 2:3], SS[:tn, 3:4]), [('ss',)], [('ss',)])
                        stt(xt[:tn, :], xt[:tn, :], SS[:tn, 2:3], FG[:tn, :], ALU.mult, ALU.mult,
                            [xk, ('ss',), ('c', 'fg')], [xk])
                        dma('sp', y_d[t0:t0 + tn, :], xt[:tn, :], [xk], [])
            P.barrier()
        P.emit()
    return nc


_CACHE = {}


def _consts():
    ident = np.eye(128, dtype=np.float32)
    p = np.arange(128)[:, None]
    q = np.arange(128)[None, :]
    maskc = np.where(p <= q, 0.0, NEG).astype(np.float32)
    sel = np.zeros((8, 8, 128), np.float32)
    for h in range(8):
        sel[h, h, :] = 1.0
    a = np.arange(64)
    u64 = (a[:, None] > a[None, :]).astype(np.float32)
    i8 = np.eye(8, dtype=np.float32)
    maskn = np.full((32, NSEQ, 8), NEG, np.float32)
    for tk in range(32):
        for s in range(NSEQ):
            for tq in range(8):
                if tk // 8 == s and tk % 8 <= tq:
                    maskn[tk, s, tq] = 0.0
    rc = np.zeros((128, 4, 16), np.float32)
    for g, w in enumerate(WIN):
        rc[:, g, :] = 1.0 / np.minimum(np.arange(16) + 1, w)
    piota = np.arange(128, dtype=np.float32)[:, None]
    return dict(ident=ident, maskc=maskc, u64=u64, i8=i8, maskn=maskn, rc=rc, piota=piota)


def _fm(v, nchunk):
    v = np.asarray(v, np.float32)
    lead = v.shape[:-1]
    v = v.reshape(lead + (nchunk, 128))
    return np.ascontiguousarray(np.moveaxis(v, -1, 0))


def kernel(x_prompt, x_sample, cache_k, cache_v, cache_logf, state_pool, state_conv, page_table,
           c_prompt, c_sample, norm_g, w_cond, b_cond, w_in, b_f, pool_w, pool_scale, conv_w,
           w_br_a, w_br_b, w_br_c, w_o, final_g):
    f32 = lambda a: np.ascontiguousarray(np.asarray(a, dtype=np.float32))
    if 'nc' not in _CACHE:
        _CACHE['nc'] = build()
    nc = _CACHE['nc']
    consts = _consts()
    ck = f32(cache_k).reshape(NL, NPOOL * 128, 512)
    cv = f32(cache_v).reshape(NL, NPOOL * 128, 512)
    clf = f32(cache_logf).reshape(NL, NPOOL, 1024)
    shared = dict(cache_k=ck, cache_v=cv, cache_logf=clf, w_in=f32(w_in), w_cond=f32(w_cond),
                  w_br_a=f32(w_br_a), w_br_b=f32(w_br_b), w_br_c=f32(w_br_c), w_o=f32(w_o), pool_w=f32(pool_w),
                  norm_g_fm=_fm(norm_g, 8), b_cond_fm=_fm(b_cond, 24),
                  b_f=np.ascontiguousarray(f32(b_f).T), pool_scale_fm=_fm(pool_scale, 4),
                  conv_w_fm=np.ascontiguousarray(_fm(conv_w, 4).transpose(0, 1, 3, 2)),
                  final_g=f32(final_g).reshape(1, D), **consts)
    pt = np.asarray(page_table, dtype=np.int32)
    in_maps = []
    for c in range(8):
        sl = slice(4 * c, 4 * c + 4)
        m = dict(shared)
        m['xin'] = np.concatenate([f32(x_prompt[c]), f32(x_sample[sl]).reshape(NSM, D)], axis=0)
        cc = np.concatenate([f32(c_prompt[c:c + 1]), f32(c_sample[sl])], axis=0)
        m['cT'] = np.ascontiguousarray(cc.reshape(5, 8, 128).transpose(2, 1, 0))
        m['pt'] = np.ascontiguousarray(pt[sl].reshape(1, NSEQ * NPG))
        m['ptT'] = np.ascontiguousarray(np.concatenate([pt[sl].T, pt[sl].T], axis=0))
        sp = f32(state_pool[:, sl])
        m['sp_tm'] = sp
        m['spT'] = np.ascontiguousarray(sp.reshape(NL, NSEQ, 15, 4, 128).transpose(0, 4, 3, 1, 2))
        sc = f32(state_conv[:, sl])
        m['scT'] = np.ascontiguousarray(sc.reshape(NL, NSEQ, 2, 4, 128).transpose(0, 4, 3, 1, 2))
        in_maps.append(m)
    ncores = _CACHE.get('ncores', 8)
    res = run_bass_kernel_spmd(nc, in_maps[:ncores], core_ids=list(range(ncores)))
    R = list(res.results)
    while len(R) < 8:
        R.append({k: np.zeros_like(v) for k, v in R[0].items()})
    cat = lambda f: np.concatenate([f(r) for r in R], axis=0)
    y_prompt = np.stack([r['y'][:NPR] for r in R])
    y_sample = cat(lambda r: r['y'][NPR:].reshape(NSEQ, 8, D))
    nkp = np.stack([r['nk'][:, :NPR].reshape(NL, NPR, 8, 64) for r in R], axis=1)
    nvp = np.stack([r['nv'][:, :NPR].reshape(NL, NPR, 8, 64) for r in R], axis=1)
    nlp = np.stack([r['nlf'][:, :NPR] for r in R], axis=1)
    npp = np.stack([r['npool_p'] for r in R], axis=1)
    ncp = np.stack([r['nconv_p'] for r in R], axis=1)
    nks = np.concatenate([r['nk'][:, NPR:].reshape(NL, NSEQ, 8, 8, 64) for r in R], axis=1)
    nvs = np.concatenate([r['nv'][:, NPR:].reshape(NL, NSEQ, 8, 8, 64) for r in R], axis=1)
    nls = np.concatenate([r['nlf'][:, NPR:].reshape(NL, NSEQ, 8, 8) for r in R], axis=1)
    nps = np.concatenate([r['npool_s'] for r in R], axis=1)
    ncs = np.concatenate([r['nconv_s'] for r in R], axis=1)
    outs = (y_prompt, y_sample, nkp, nvp, nlp, npp, ncp, nks, nvs, nls, nps, ncs)
    return tuple(np.ascontiguousarray(o, dtype=np.float32) for o in outs)
```

```python
import contextlib
import numpy as np
import concourse.bass as bass
import concourse.mybir as mybir
from concourse.bass_utils import run_bass_kernel_spmd

F32 = mybir.dt.float32
BF16 = mybir.dt.bfloat16
I32 = mybir.dt.int32
AF = mybir.ActivationFunctionType
ALU = mybir.AluOpType

NL = 2
D = 1024
NPR = 2048
NSM = 32
NT = NPR + NSM
NSEQ = 4
BLK = [(0, 512), (512, 512), (1024, 512), (1536, 512), (2048, 32)]
TIL = [(i * 128, 128) for i in range(16)] + [(2048, 32)]
COL = dict(pu=0, pg=512, q=1024, k=1536, v=2048, fl=2560, ag=2568, ch=3080, cb=3592, cc=4104, cg=4616,
           mg=5128)
WIN = (2, 4, 8, 16)
NPOOL = 2560
NPG = 64
NEG = -30000.0
DEBUG = False
STOP = 99
NOVA = False


class Prog:
    ENG = ('pe', 'act', 'dve', 'pool', 'sp')

    def __init__(self, nc):
        self.nc = nc
        self.q = {e: [] for e in self.ENG}
        self.cnt = {e: 0 for e in self.ENG}
        self.last_w = {}
        self.readers = {}
        self.seen = {e: {} for e in self.ENG}
        self.slot_cnt = {}
        self.sems = {}

    def _deps(self, eng, reads, writes):
        deps = {}

        def add(d):
            if d is None:
                return
            k, v = d
            if k == eng and eng == 'pe':
                return
            if deps.get(k, 0) < v:
                deps[k] = v
        for r in reads:
            add(self.last_w.get(r))
        for w in writes:
            add(self.last_w.get(w))
            for d in self.readers.get(w, ()):
                add(d)
        out = []
        seen = self.seen[eng]
        for k, v in deps.items():
            if seen.get(k, 0) >= v:
                continue
            seen[k] = v
            out.append((k, v))
        return out

    def _commit(self, dep, reads, writes):
        for w in writes:
            self.last_w[w] = dep
            self.readers[w] = []
        for r in reads:
            self.readers.setdefault(r, []).append(dep)

    def op(self, eng, fn, reads=(), writes=()):
        reads = list(reads)
        writes = list(writes)
        waits = self._deps(eng, reads, writes)
        self.cnt[eng] += 1
        dep = (eng, self.cnt[eng])
        self.q[eng].append((waits, fn, (eng, 1)))
        self._commit(dep, reads, writes)
        return dep

    def dma(self, eng, slot, fn, reads=(), writes=()):
        reads = list(reads)
        writes = list(writes)
        key = ('dma', slot)
        waits = self._deps(eng, reads, writes)
        prev = self.slot_cnt.get(key, 0)
        if prev and self.seen[eng].get(key, 0) < prev:
            self.seen[eng][key] = prev
            waits.append((key, prev))
        self.slot_cnt[key] = prev + 16
        dep = (key, prev + 16)
        self.q[eng].append((waits, fn, (key, 16)))
        self._commit(dep, reads, writes)
        return dep

    def barrier(self):
        tgt = {}
        for e in self.ENG:
            if e != 'sp' and self.cnt[e] > 0:
                tgt[e] = self.cnt[e]
        for k, v in self.slot_cnt.items():
            tgt[k] = v
        for e in self.ENG:
            waits = []
            for k, v in tgt.items():
                if k == e:
                    continue
                if self.seen[e].get(k, 0) < v:
                    self.seen[e][k] = v
                    waits.append((k, v))
            if waits:
                self.q[e].append((waits, None, None))

    def emit(self):
        nc = self.nc
        with contextlib.ExitStack() as st:
            for e in self.ENG:
                if e != 'sp':
                    self.sems[e] = st.enter_context(nc.semaphore('s_' + e))
            for key in sorted(self.slot_cnt):
                self.sems[key] = st.enter_context(nc.semaphore('d_%d' % key[1]))
            fin = {}
            for k, v in self.slot_cnt.items():
                fin[k] = v
            for e in self.ENG:
                if e != 'sp' and self.cnt[e] > 0:
                    fin[e] = self.cnt[e]
            block = st.enter_context(nc.Block())
            sems = self.sems

            def replay(name, eng, final=False):
                for waits, fn, inc in self.q[name]:
                    for k, v in waits:
                        eng.wait_ge(sems[k], v)
                    if fn is not None:
                        fn(eng).then_inc(sems[inc[0]], inc[1])
                if final:
                    for k, v in fin.items():
                        eng.wait_ge(sems[k], v)

            @block.tensor
            def _(e):
                replay('pe', e)

            @block.scalar
            def _(e):
                replay('act', e)

            @block.vector
            def _(e):
                replay('dve', e)

            @block.gpsimd
            def _(e):
                replay('pool', e)

            @block.sync
            def _(e):
                replay('sp', e, final=True)


def build():
    nc = bass.Bass("TRN2", target_bir_lowering=False)
    dt_in = lambda n, s, d=F32: nc.dram_tensor(n, s, d, kind="ExternalInput").ap()
    dt_out = lambda n, s, d=F32: nc.dram_tensor(n, s, d, kind="ExternalOutput").ap()
    xin = dt_in("xin", [NT, D])
    cT_d = dt_in("cT", [128, 8, 5])
    ckv_d = dt_in("cache_kv", [NL, NPOOL * 128, 1024])
    clf_d = dt_in("cache_logf", [NL, NPOOL, 1024])
    pt_d = dt_in("pt", [1, NSEQ * NPG], I32)
    ptT_d = dt_in("ptT", [128, NSEQ], I32)
    spT_d = dt_in("spT", [NL, 128, 4, NSEQ, 15])
    sp_d = dt_in("sp_tm", [NL, NSEQ, 15, 512])
    scT_d = dt_in("scT", [NL, 128, 4, NSEQ, 2])
    w_in_d = dt_in("w_in", [NL, D, 8200])
    w_cond_d = dt_in("w_cond", [NL, D, 3072])
    w_bra_d = dt_in("w_br_a", [NL, 512, D])
    w_brb_d = dt_in("w_br_b", [NL, 512, D])
    w_brc_d = dt_in("w_br_c", [NL, 512, D])
    w_o_d = dt_in("w_o", [NL, D, D])
    pool_w_d = dt_in("pool_w", [NL, 4, 128, 128])
    ng_d = dt_in("norm_g_fm", [128, NL, 8])
    bc_d = dt_in("b_cond_fm", [128, NL, 24])
    bf_d = dt_in("b_f", [8, NL])
    psc_d = dt_in("pool_scale_fm", [128, NL, 4])
    cw_d = dt_in("conv_w_fm", [128, NL, 4, 3])
    fg_d = dt_in("final_g", [1, D])
    ident_d = dt_in("ident", [128, 128])
    maskc_d = dt_in("maskc", [128, 128])
    u64_d = dt_in("u64", [64, 64])
    i8_d = dt_in("i8", [8, 8])
    maskn_d = dt_in("maskn", [32, NSEQ, 8])
    rc_d = dt_in("rc", [128, 4, 16])
    piota_d = dt_in("piota", [128, 1])

    y_d = dt_out("y", [NT, D])
    nk_d = dt_out("nk", [NL, NT, 512])
    nv_d = dt_out("nv", [NL, NT, 512])
    nlf_d = dt_out("nlf", [NL, NT, 8])
    npp_d = dt_out("npool_p", [NL, 15, 512])
    ncp_d = dt_out("nconv_p", [NL, 2, 512])
    nps_d = dt_out("npool_s", [NL, NSEQ, 15, 512])
    ncs_d = dt_out("nconv_s", [NL, NSEQ, 2, 512])
    x1_d = nc.dram_tensor("x1s", [NT, D], F32, kind="Internal").ap()

    with contextlib.ExitStack() as st:
        def sb(name, shape, dt=F32):
            return st.enter_context(nc.sbuf_tensor(name, shape, dt))
        XT = [sb("xt%d" % i, [128, D]) for i in range(2)]
        XN = sb("xn", [128, D], BF16)
        HT = sb("ht", [128, 8, NT], BF16)
        R1 = sb("r1", [128, 16896], BF16)
        R2 = sb("r2", [128, 22528], BF16)
        BINB = sb("binb", [128, 4, NT], BF16)
        W = [sb("w%d" % i, [128, 8, 512], BF16) for i in range(3)]
        WBR = [sb("wbr%d" % i, [128, 4, 512], BF16) for i in range(3)]
        WPW = sb("wpw", [128, 4, 128], BF16)
        WFL = sb("wfl", [128, 8, 8], BF16)
        MOD = sb("mod", [128, NL, 24, 5])
        CT32 = sb("ct32", [128, 8, 5])
        CTB = sb("ctb", [128, 8, 5], BF16)
        NG = sb("ng", [128, NL, 8])
        BC = sb("bc", [128, NL, 24])
        BFN = sb("bfn", [8, NL])
        PSC = sb("psc", [128, NL, 4])
        CW = sb("cw", [128, NL, 4, 3])
        IDF = sb("idf", [128, 128])
        IDB = sb("idb", [128, 128], BF16)
        MASKC = sb("maskc_s", [128, 128])
        U64 = sb("u64_s", [64, 64])
        I8 = sb("i8_s", [8, 8])
        MASKN = sb("maskn_s", [32, NSEQ, 8])
        RC = sb("rc_s", [128, 4, 16])
        PIOTA = sb("piota_s", [128, 1])
        ONE1 = sb("one1", [128, 1])
        ONES8 = sb("ones8", [8, 128])
        ONEB = sb("oneb", [128, 1], BF16)
        SAGS = sb("sags", [128, 4, NSM], BF16)
        NCV = sb("ncv", [2, 5, 128])
        NPV = sb("npv", [16, 5, 128])
        SELH = sb("selh", [8, 2, 128])
        T16 = sb("t16", [128, 16])
        LC = sb("lc", [8, 2, NT])
        LFTM = sb("lftm", [128, 17, 8])
        NCTM = sb("nctm", [128, 17, 8])
        CTM = sb("ctm", [128, 8])
        SS = sb("ss", [128, 4])
        STGT = sb("stgt", [128, 2, 512])
        STG = [STGT[:, i, :] for i in range(2)]
        FG = STGT[:, :, :].rearrange("p a b -> p (a b)")
        PTB = sb("ptb", [128, NSEQ * NPG], I32)
        IDX = sb("idx", [128, NSEQ * NPG], I32)
        PTT = sb("ptt", [128, NSEQ], I32)
        PS = st.enter_context(nc.psum_tensor("ps", [128, 8, 512], F32))

        P = Prog(nc)
        cnt = {'bank': 0, 'sbank': 0, 'abank': 0, 'slot': 0, 'stg': 0, 'w': 0}

        def nbank():
            cnt['bank'] = (cnt['bank'] + 1) % 4
            return cnt['bank']

        def sbank():
            cnt['sbank'] = (cnt['sbank'] + 1) % 2
            return 4 + cnt['sbank']

        def abank():
            cnt['abank'] = (cnt['abank'] + 1) % 2
            return 6 + cnt['abank']

        def mm(out, lhsT, rhs, start, stop, reads, writes):
            P.op('pe', lambda e: e.matmul(out, lhsT, rhs, start=start, stop=stop), reads, writes)

        def tr(out, in_, ident, reads, writes):
            P.op('pe', lambda e: e.transpose(out, in_, ident), reads, writes)

        def act(out, in_, func, reads, writes, bias=0.0, scale=1.0, accum_out=None):
            if accum_out is None:
                P.op('act', lambda e: e.activation(out, in_, func, bias=bias, scale=scale), reads, writes)
            else:
                P.op('act', lambda e: e.activation(out, in_, func, bias=bias, scale=scale,
                                                   accum_out=accum_out), reads, writes)

        def tt(out, in0, in1, op, reads, writes, eng='dve'):
            P.op(eng, lambda e: e.tensor_tensor(out, in0, in1, op), reads, writes)

        def ts(out, in0, s1, s2, op0, op1, reads, writes, eng='dve'):
            if s2 is None:
                P.op(eng, lambda e: e.tensor_scalar(out, in0, s1, None, op0), reads, writes)
            else:
                P.op(eng, lambda e: e.tensor_scalar(out, in0, s1, s2, op0, op1), reads, writes)

        def stt(out, in0, scalar, in1, op0, op1, reads, writes, eng='dve'):
            P.op(eng, lambda e: e.scalar_tensor_tensor(out, in0, scalar, in1, op0, op1), reads, writes)

        def cp(out, in_, reads, writes, eng='dve'):
            if eng == 'act':
                P.op('act', lambda e: e.copy(out, in_), reads, writes)
            else:
                P.op(eng, lambda e: e.tensor_copy(out, in_), reads, writes)

        def dma(eng, out, in_, reads, writes, slot=None):
            if slot is None:
                cnt['slot'] = (cnt['slot'] + 1) % 6
                slot = cnt['slot']
            P.dma(eng, slot, lambda e: e.dma_start(out=out, in_=in_), reads, writes)

        def gather(out, in_, idx_ap, reads, writes, slot, eoff=0):
            P.dma('pool', slot, lambda e: e.indirect_dma_start(
                out=out, out_offset=None, in_=in_,
                in_offset=bass.IndirectOffsetOnAxis(ap=idx_ap, axis=0), element_offset=eoff), reads, writes)

        def load_w(src2d, ncols, nk=8, dst=None, key=None, slot=None, coff=0):
            if dst is None:
                cnt['w'] = (cnt['w'] + 1) % 3
                s = cnt['w']
                dst = W[s]
                key = ('w', s)
                slot = 8 + s
            if slot is None:
                slot = 11
            P.dma('pool', slot, lambda e: e.dma_start(
                out=dst[:, 0:nk, coff:coff + ncols], in_=src2d.rearrange("(k p) n -> p k n", p=128)), [], [key])
            return dst, key

        def blk_of_tile(i):
            return min(i // 4, 4) if i < 16 else 4

        for dst, src in ((CT32, cT_d), (NG, ng_d), (BC, bc_d), (BFN, bf_d), (PSC, psc_d), (CW, cw_d),
                         (IDF, ident_d), (MASKC, maskc_d), (U64, u64_d), (I8, i8_d),
                         (MASKN, maskn_d), (RC, rc_d), (PIOTA, piota_d), (PTT, ptT_d)):
            dma('sp', dst[:], src, [], [('c', dst.name)], slot=6)
        dma('sp', PTB[:], pt_d.partition_broadcast(128), [], [('c', 'ptb')], slot=6)
        cp(IDB[:], IDF[:], [('c', 'idf')], [('c', 'idb')])
        cp(CTB[:], CT32[:], [('c', 'ct32')], [('c', 'ctb')])
        P.op('dve', lambda e: e.memset(ONE1[:], 1.0), [], [('c', 'one1')])
        P.op('dve', lambda e: e.memset(ONES8[:], 1.0), [], [('c', 'ones8')])
        P.op('dve', lambda e: e.memset(ONEB[:], 1.0), [], [('c', 'oneb')])
        ts(IDX[:], PTB[:], 128.0, PIOTA[:, 0:1], ALU.mult, ALU.add, [('c', 'ptb'), ('c', 'piota_s')],
           [('c', 'idx')])
        ts(BFN[:], BFN[:], -1.0, None, ALU.mult, None, [('c', 'bfn')], [('c', 'bfn')])
        ts(BC[:, :, 8:16], BC[:, :, 8:16], 1.0, None, ALU.add, None, [('c', 'bc')], [('c', 'bc')])
        P.barrier()

        for l in range(NL):
            for cb in range(6):
                wt, wk = load_w(w_cond_d[l][:, cb * 512:(cb + 1) * 512], 512)
                for jj in range(4):
                    j = cb * 4 + jj
                    bk = nbank()
                    for kc in range(8):
                        mm(PS[:, bk, 0:5], wt[:, kc, jj * 128:(jj + 1) * 128], CTB[:, kc, :], kc == 0, kc == 7,
                           [wk, ('c', 'ctb')], [('ps', bk)])
                    if j < 8 or j >= 16:
                        ts(MOD[:, l, j, :], PS[:, bk, 0:5], BC[:, l, j:j + 1], None, ALU.add, None,
                           [('ps', bk), ('c', 'bc')], [('mod',)])
                    else:
                        ts(MOD[:, l, j, :], PS[:, bk, 0:5], BC[:, l, j:j + 1], NG[:, l, j - 8:j - 7], ALU.add,
                           ALU.mult, [('ps', bk), ('c', 'bc'), ('c', 'ng')], [('mod',)])
        P.barrier()
        if STOP <= 1:
            P.emit()
            return nc

        def seq_groups(c0, n):
            if c0 < NPR:
                return [(c0, n, 0)]
            return [(NPR + 8 * s, 8, 1 + s) for s in range(NSEQ)]

        def norm_tile(l, i, xt, xkey):
            t0, n = TIL[i]
            b = blk_of_tile(i)
            P.op('dve', lambda e: e.memset(SS[:, 0:1], 0.0), [], [('ss',)])
            act(XN[:n, :], xt[:n, :], AF.Square, [xkey, ('ss',)], [('xn',), ('ss',)], accum_out=SS[:n, 0:1])
            ts(SS[:n, 1:2], SS[:n, 0:1], 1.0 / D, 1e-6, ALU.mult, ALU.add, [('ss',)], [('ss',)])
            act(SS[:n, 3:4], SS[:n, 1:2], AF.Sqrt, [('ss',)], [('ss',)])
            P.op('dve', lambda e: e.reciprocal(SS[:n, 2:3], SS[:n, 3:4]), [('ss',)], [('ss',)])
            act(XN[:n, :], xt[:n, :], AF.Copy, [xkey, ('ss',)], [('xn',)], scale=SS[:n, 2:3])
            bk = nbank()
            pb = PS[:, bk, :].bitcast(BF16).rearrange("p (a b) -> p a b", a=8)
            for kc in range(8):
                tr(pb[:, kc, 0:n], XN[:n, kc * 128:(kc + 1) * 128], IDB[:n, :n], [('xn',), ('c', 'idb')],
                   [('ps', bk)])
            for kc in range(8):
                for (c0, nn, sq) in seq_groups(t0, n):
                    ts(HT[:, kc, c0:c0 + nn], pb[:, kc, c0 - t0:c0 - t0 + nn], MOD[:, l, 8 + kc, sq:sq + 1],
                       MOD[:, l, kc, sq:sq + 1], ALU.mult, ALU.add, [('ps', bk), ('mod',)], [('ht', b)])

        def proj_fm(wt, wk, chunks, evac, blocks=None, nk=8, rhs=None, rkey=None, m=128):
            for jj in chunks:
                for b, (c0, n) in enumerate(BLK):
                    if blocks is not None and b not in blocks:
                        continue
                    bk = nbank()
                    for kc in range(nk):
                        if rhs is None:
                            r_ap, rk = HT[:, kc, c0:c0 + n], ('ht', b)
                        else:
                            r_ap, rk = rhs[:, kc, c0:c0 + n], rkey
                        mm(PS[0:m, bk, 0:n], wt[:, kc, jj * 128:jj * 128 + m], r_ap, kc == 0, kc == nk - 1,
                           [wk, rk], [('ps', bk)])
                    evac(jj, b, c0, n, bk)

        def proj_tm(wt, wk, i, ncols=512):
            t0, n = TIL[i]
            bk = nbank()
            for kc in range(8):
                mm(PS[0:n, bk, 0:ncols], HT[:, kc, t0:t0 + n], wt[:, kc, 0:ncols], kc == 0, kc == 7,
                   [wk, ('ht', blk_of_tile(i))], [('ps', bk)])
            return bk

        def stage():
            cnt['stg'] = (cnt['stg'] + 1) % 2
            return cnt['stg']

        for l in range(NL):
            if l == 0:
                for i, (t0, n) in enumerate(TIL):
                    xt = XT[i % 2]
                    dma('sp', xt[:n, :], xin[t0:t0 + n, :], [], [('xt', i % 2)])
                    norm_tile(0, i, xt, ('xt', i % 2))
            P.barrier()
            if STOP <= 2:
                P.emit()
                return nc

            QT = R1[:, 0:4 * NT].rearrange("p (c t) -> p c t", c=4)
            KT = R1[:, 4 * NT:8 * NT].rearrange("p (c t) -> p c t", c=4)
            VA = R2[:, 0:16 * 768].rearrange("p (i c) -> p i c", i=16)
            o = 16 * 768
            LQ = [R2[:, o + 1024 * k:o + 1024 * (k + 1)].bitcast(F32) for k in range(2)]
            o += 2048
            TMP = [R2[:, o + 1024 * k:o + 1024 * (k + 1)].bitcast(F32) for k in range(2)]
            o += 2048
            PT = [R2[:, o + 512 * k:o + 512 * (k + 1)] for k in range(2)]
            o += 1024
            SAG = R2[:, o:o + NT]
            o += NT
            RD = R2[:, o:o + 1024].bitcast(F32)
            o += 1024
            OTM = R2[:, o:o + 1024].bitcast(F32)
            o += 1024
            VS = R2[:, o:o + 512]
            o += 512
            assert o <= 22528

            VA4 = VA.rearrange("p i (c k) -> p i c k", c=4)
            P.op('dve', lambda e: e.memset(VA4[:, :, :, 64:128], 1.0), [], [('va', i) for i in range(16)])

            wt, wk = load_w(w_in_d[l][:, COL['q']:COL['q'] + 512], 512)
            proj_fm(wt, wk, range(4), lambda jj, b, c0, n, bk: cp(
                QT[:, jj, c0:c0 + n], PS[:, bk, 0:n], [('ps', bk)], [('qt', b)], eng='act'))
            if STOP <= 2.1:
                P.emit()
                return nc
            wt, wk = load_w(w_in_d[l][:, COL['k']:COL['k'] + 512], 512)
            proj_fm(wt, wk, range(4), lambda jj, b, c0, n, bk: cp(
                KT[:, jj, c0:c0 + n], PS[:, bk, 0:n], [('ps', bk)], [('kt', b)], eng='act'))
            if STOP <= 2.15:
                P.emit()
                return nc
            for i, (t0, n) in enumerate(TIL):
                bk = proj_tm(wt, wk, i)
                s = stage()
                cp(STG[s][:n, :], PS[:n, bk, :], [('ps', bk)], [('stg', s)])
                dma('sp', nk_d[l, t0:t0 + n, :], STG[s][:n, :], [('stg', s)], [])
            if STOP <= 2.2:
                P.emit()
                return nc
            wt, wk = load_w(w_in_d[l][:, COL['v']:COL['v'] + 512], 512)
            for i, (t0, n) in enumerate(TIL):
                bk = proj_tm(wt, wk, i)
                s = stage()
                cp(STG[s][:n, :], PS[:n, bk, :], [('ps', bk)], [('stg', s)])
                dma('sp', nv_d[l, t0:t0 + n, :], STG[s][:n, :], [('stg', s)], [])
                if NOVA:
                    pass
                elif i < 16:
                    src = PS[:, bk, :].rearrange("p (c h k) -> p c h k", c=4, h=2)
                    dst = VA[:, i, :].rearrange("p (c k) -> p c k", c=4)
                    cp(dst[:, :, 0:64], src[:, :, 0, :], [('ps', bk)], [('va', i)])
                    cp(dst[:, :, 128:192], src[:, :, 1, :], [('ps', bk)], [('va', i)])
                else:
                    cp(VS[:n, :], PS[:n, bk, :], [('ps', bk)], [('vs',)])
            if STOP <= 2.3:
                P.emit()
                return nc
            load_w(w_in_d[l][:, COL['fl']:COL['fl'] + 8], 8, dst=WFL, key=('wfl',))

            def ev_fl(jj, b, c0, n, bk):
                act(LC[:, 1, c0:c0 + n], PS[0:8, bk, 0:n], AF.Exp, [('ps', bk), ('c', 'bfn')], [('lc',)],
                    bias=BFN[:, l:l + 1], scale=-1.0)
                act(LC[:, 1, c0:c0 + n], LC[:, 1, c0:c0 + n], AF.Ln, [('lc',)], [('lc',)], bias=1.0)
                ts(LC[:, 0, c0:c0 + n], LC[:, 1, c0:c0 + n], -1.0, None, ALU.mult, None, [('lc',)], [('lc',)])
            proj_fm(WFL, ('wfl',), [0], ev_fl, m=8)
            if STOP <= 2.5:
                P.emit()
                return nc
            P.op('dve', lambda e: e.tensor_tensor_scan(
                LC[:, 1, 0:NPR], ONE1[0:8, 0:1].to_broadcast([8, NPR]), LC[:, 0, 0:NPR], 0.0, ALU.mult,
                ALU.add), [('lc',), ('c', 'one1')], [('lc',)])
            for s in range(NSEQ):
                a0 = NPR + 8 * s
                P.op('dve', lambda e, a0=a0: e.tensor_tensor_scan(
                    LC[:, 1, a0:a0 + 8], ONE1[0:8, 0:1].to_broadcast([8, 8]), LC[:, 0, a0:a0 + 8], 0.0,
                    ALU.mult, ALU.add), [('lc',), ('c', 'one1')], [('lc',)])
            if STOP <= 2.6:
                P.emit()
                return nc
            for i, (t0, n) in enumerate(TIL):
                bk = nbank()
                tr(PS[0:n, bk, 0:8], LC[:, 0, t0:t0 + n], IDF[0:8, 0:8], [('lc',), ('c', 'idf')], [('ps', bk)])
                tr(PS[0:n, bk, 8:16], LC[:, 1, t0:t0 + n], IDF[0:8, 0:8], [('lc',), ('c', 'idf')], [('ps', bk)])
                cp(LFTM[:n, i, :], PS[0:n, bk, 0:8], [('ps', bk)], [('lftm',)])
                ts(NCTM[:n, i, :], PS[0:n, bk, 8:16], -1.0, None, ALU.mult, None, [('ps', bk)], [('nctm',)])
                if i == 16:
                    cp(CTM[:n, :], PS[0:n, bk, 8:16], [('ps', bk)], [('ctm',)])
            if STOP <= 2.8:
                P.emit()
                return nc
            dma('sp', nlf_d[l, 0:NPR, :].rearrange("(i p) h -> p i h", p=128), LFTM[:, 0:16, :], [('lftm',)], [])
            dma('sp', nlf_d[l, NPR:NT, :], LFTM[0:32, 16, :], [('lftm',)], [])

            if STOP <= 3:
                P.emit()
                return nc
            wag, wagk = load_w(w_in_d[l][:, COL['ag']:COL['ag'] + 512], 512)
            for c in range(4):
                def ev_ag(jj, b, c0, n, bk, c=c):
                    if b < 4:
                        act(SAG[:, c0:c0 + n], PS[:, bk, 0:n], AF.Silu, [('ps', bk)], [('sag',)])
                    else:
                        act(SAGS[:, c, :], PS[:, bk, 0:n], AF.Silu, [('ps', bk)], [('sags',)])
                proj_fm(wag, wagk, [c], ev_ag)
                its = [(half, b) for half in range(2) for b in range(4)]

                def emit_lq(j, c=c):
                    half, b = its[j]
                    h = 2 * c + half
                    q0 = 512 * b
                    bk = nbank()
                    ts(SELH[:, j % 2, :], ONES8[:, :], I8[:, h:h + 1], None, ALU.mult, None,
                       [('c', 'ones8'), ('c', 'i8_s')], [('selh', j % 2)])
                    mm(PS[:, bk, :], SELH[:, j % 2, :], LC[:, 1, q0:q0 + 512], True, True,
                       [('selh', j % 2), ('lc',)], [('ps', bk)])
                    cp(LQ[j % 2], PS[:, bk, :], [('ps', bk)], [('lq', j % 2)], eng='act')
                emit_lq(0)
                for j, (half, b) in enumerate(its):
                    h = 2 * c + half
                    p0 = 64 * half
                    q0 = 512 * b
                    lq = LQ[j % 2]
                    lqk = ('lq', j % 2)
                    ab = abank()
                    last = 4 * b + 3

                    def emit_s(i, c=c, p0=p0, q0=q0, b=b):
                        r = i - 4 * b
                        lo = 128 * r if r >= 0 else 0
                        n = 512 - lo
                        sbk = sbank()
                        mm(PS[:, sbk, 0:n], KT[p0:p0 + 64, c, 128 * i:128 * i + 128],
                           QT[p0:p0 + 64, c, q0 + lo:q0 + 512], True, True,
                           [('kt', i // 4), ('qt', b)], [('ps', sbk)])
                        return sbk, r, lo, n
                    pend = emit_s(0)
                    if j + 1 < len(its):
                        emit_lq(j + 1)
                    for i in range(last + 1):
                        sbk, r, lo, n = pend
                        if i < last:
                            pend = emit_s(i + 1)
                        k2 = (i % 2)
                        stt(TMP[k2][:, 0:n], PS[:, sbk, 0:n], 0.125, lq[:, lo:512], ALU.mult, ALU.add,
                            [('ps', sbk), lqk], [('tmp', k2)])
                        if r >= 0:
                            tt(TMP[k2][:, 0:128], TMP[k2][:, 0:128], MASKC[:], ALU.add,
                               [('tmp', k2), ('c', 'maskc_s')], [('tmp', k2)])
                        act(PT[k2][:, 0:n], TMP[k2][:, 0:n], AF.Exp, [('tmp', k2), ('nctm',)], [('pt', k2)],
                            bias=NCTM[:, i, h:h + 1])
                        mm(PS[:, ab, lo:512], VA[:, i, c * 192 + 64 * half:c * 192 + 64 * half + 128],
                           PT[k2][:, 0:n], i == 0, i == last, [('va', i), ('pt', k2)], [('ps', ab)])
                    d0 = 64 - p0
                    act(RD[d0:d0 + 64, :], PS[d0:d0 + 64, ab, :], AF.Ln, [('ps', ab)], [('rd',)])
                    act(RD[d0:d0 + 64, :], RD[d0:d0 + 64, :], AF.Exp, [('rd',)], [('rd',)], scale=-1.0)
                    tt(OTM[p0:p0 + 64, :], PS[p0:p0 + 64, ab, :], RD[d0:d0 + 64, :], ALU.mult,
                       [('ps', ab), ('rd',)], [('otm',)])
                    tt(BINB[p0:p0 + 64, c, q0:q0 + 512], OTM[p0:p0 + 64, :], SAG[p0:p0 + 64, q0:q0 + 512],
                       ALU.mult, [('otm',), ('sag',)], [('binb', b)])
            P.barrier()

            if STOP <= 4:
                P.emit()
                return nc
            o = 0
            NKV = 10
            KV = [R2[:, o + 1024 * k:o + 1024 * (k + 1)] for k in range(NKV)]
            o += 1024 * NKV
            KTP = [R2[:, o + 512 * k:o + 512 * (k + 1)].rearrange("p (c t) -> p c t", c=4) for k in range(2)]
            o += 1024
            QBD = R2[:, o:o + 64].rearrange("p (c k) -> p c k", c=4)
            o += 64
            LFPF = R2[:, o:o + 2048].bitcast(F32)
            LFP = LFPF[0:64, :]
            o += 2048
            PFX = R2[0:64, o:o + 2048].bitcast(F32)
            o += 2048
            EPG = R2[0:64, o:o + 16].bitcast(F32)
            o += 16
            SUF = R2[:, o:o + 1024].bitcast(F32).rearrange("p (h g) -> p h g", h=8)
            o += 1024
            BD = R2[0:8, o:o + 128].bitcast(F32).rearrange("p (h t) -> p h t", h=8)
            o += 128
            LN = R2[:, o:o + 128].bitcast(F32)
            o += 128
            BN = R2[0:32, o:o + 128].bitcast(F32)
            o += 128
            T1 = [R2[:, o + 1024 * k:o + 1024 * (k + 1)].bitcast(F32) for k in range(2)]
            o += 2048
            PD = [R2[:, o + 512 * k:o + 512 * (k + 1)] for k in range(2)]
            o += 1024
            OD = R2[0:64, o:o + 1024].bitcast(F32)
            o += 1024
            RDD = R2[0:64, o:o + 2].bitcast(F32)
            o += 2
            assert o <= 22528
            kv_l = ckv_d.rearrange("l r c -> (l r) c")
            clf_l = clf_d.rearrange("l r c -> (l r) c")
            kv_off = l * NPOOL * 128 * 1024
            lf_off = l * NPOOL * 1024
            for s in range(NSEQ):
                sc0 = NPR + 8 * s
                gather(LFPF[:, :], clf_l, PTT[:, s:s + 1], [('c', 'ptt')], [('lfp',)], slot=12, eoff=lf_off)
                for h in range(8):
                    P.op('dve', lambda e, h=h: e.tensor_tensor_scan(
                        PFX[:, h:1024:8], ONE1[0:64, 0:1].to_broadcast([64, 128]), LFP[:, h:1024:8], 0.0,
                        ALU.mult, ALU.add), [('lfp',), ('c', 'one1')], [('pfx',)])
                bk = nbank()
                mm(PS[0:64, bk, 0:8], U64[:, :], PFX[:, 1016:1024], True, True, [('c', 'u64_s'), ('pfx',)],
                   [('ps', bk)])
                tt(EPG[:, :], PS[0:64, bk, 0:8], PFX[:, 1016:1024], ALU.add, [('ps', bk), ('pfx',)], [('epg',)])
                pf3 = PFX.rearrange("p (t h) -> p t h", h=8)
                tt(pf3, EPG[:, :].unsqueeze(1).to_broadcast([64, 128, 8]), pf3, ALU.subtract,
                   [('epg',), ('pfx',)], [('pfx',)])
                bk = nbank()
                for h in range(8):
                    tr(PS[:, bk, 64 * h:64 * h + 64], PFX[:, h:1024:8], IDF[0:64, 0:64], [('pfx',), ('c', 'idf')],
                       [('ps', bk)])
                cp(SUF.rearrange("p h g -> p (h g)"), PS[:, bk, :], [('ps', bk)], [('suf',)])
                tt(BD, LC[:, 1, sc0:sc0 + 8].unsqueeze(1).to_broadcast([8, 8, 8]),
                   I8[:, :].unsqueeze(2).to_broadcast([8, 8, 8]), ALU.mult, [('lc',), ('c', 'i8_s')], [('bd',)])
                bk = nbank()
                mm(PS[:, bk, 0:64], ONES8[:, :], BD.rearrange("p h t -> p (h t)"),
                   True, True, [('bd',), ('c', 'ones8')], [('ps', bk)])
                cp(LN[:, :], PS[:, bk, 0:64], [('ps', bk)], [('ln',)])
                bn3 = BN.rearrange("p (h t) -> p h t", h=8)
                tt(bn3, LN[0:32, :].rearrange("p (h t) -> p h t", h=8),
                   CTM[0:32, :].unsqueeze(2).to_broadcast([32, 8, 8]), ALU.subtract, [('ln',), ('ctm',)], [('bn',)])
                tt(bn3, bn3, MASKN[:, s, :].unsqueeze(1).to_broadcast([32, 8, 8]), ALU.add,
                   [('bn',), ('c', 'maskn_s')], [('bn',)])
                P.op('dve', lambda e: e.memset(QBD, 0.0), [], [('qbd',)])
                for c in range(4):
                    cp(QBD[0:64, c, 0:8], QT[0:64, c, sc0:sc0 + 8], [('qt', 4)], [('qbd',)])
                    cp(QBD[64:128, c, 8:16], QT[64:128, c, sc0:sc0 + 8], [('qt', 4)], [('qbd',)])
                ob = abank()
                db = abank()
                npages = NPG + 1
                def emit_T(g, s=s):
                    kslot = g % NKV
                    col = s * NPG + g
                    gather(KV[kslot], kv_l, IDX[:, col:col + 1], [('c', 'idx')], [('kv', kslot)], slot=13 + g % 4, eoff=kv_off)
                    tb = nbank()
                    tpb = PS[:, tb, :].bitcast(BF16).rearrange("p (a b) -> p a b", a=8)
                    for c in range(4):
                        tr(tpb[:, c, :], KV[kslot][:, 128 * c:128 * c + 128], IDB[:, :], [('kv', kslot), ('c', 'idb')],
                           [('ps', tb)])
                    k2 = g % 2
                    cp(KTP[k2], tpb[:, 0:4, :], [('ps', tb)], [('ktp', k2)], eng='act' if g % 2 == 0 else 'dve')

                def emit_S(g, sbk):
                    gi = g % 8
                    k2 = g % 2
                    for c in range(4):
                        mm(PS[:, sbk, 64 * gi + 16 * c:64 * gi + 16 * c + 16], KTP[k2][:, c, :], QBD[:, c, :],
                           True, True, [('ktp', k2), ('qbd',)], [('ps', sbk)])

                def emit_be(bt, sbk):
                    g0 = 8 * bt
                    k3 = bt % 2
                    for gi in range(8):
                        s3 = PS[:, sbk, 64 * gi:64 * gi + 64].rearrange("p (h t) -> p h t", h=8)
                        t3 = T1[k3][:, 64 * gi:64 * gi + 64].rearrange("p (h t) -> p h t", h=8)
                        suf3 = SUF[:, :, g0 + gi:g0 + gi + 1].to_broadcast([128, 8, 8])
                        stt(t3, s3, 0.125, suf3, ALU.mult, ALU.add, [('ps', sbk), ('suf',)], [('t1', k3)])
                    t3 = T1[k3].rearrange("p (g k) -> p g k", g=8)
                    ln3 = LN[:, :].unsqueeze(1).to_broadcast([128, 8, 64])
                    tt(t3, t3, ln3, ALU.add, [('t1', k3), ('ln',)], [('t1', k3)])
                    act(PD[k3][:, :], T1[k3], AF.Exp, [('t1', k3)], [('pd', k3)])

                def emit_PV(bt):
                    k3 = bt % 2
                    for gi in range(8):
                        g = 8 * bt + gi
                        vslot = g % NKV
                        mm(PS[0:64, ob, :], PD[k3][:, 64 * gi:64 * gi + 64], KV[vslot][:, 512:1024], g == 0, False,
                           [('pd', k3), ('kv', vslot)], [('ps', ob)])
                        mm(PS[0:64, db, 0:1], PD[k3][:, 64 * gi:64 * gi + 64], ONEB[:, 0:1], g == 0, False,
                           [('pd', k3), ('c', 'oneb')], [('ps', db)])
                emit_T(0)
                sbk = None
                for g in range(NPG):
                    if g % 8 == 0:
                        sbk = sbank()
                    if g % 8 == 1 and g >= 8:
                        emit_PV(g // 8 - 1)
                    if g + 1 < NPG:
                        emit_T(g + 1)
                    emit_S(g, sbk)
                    if g % 8 == 7:
                        emit_be(g // 8, sbk)
                emit_PV(NPG // 8 - 1)
                sbk = sbank()
                for c in range(4):
                    mm(PS[0:32, sbk, 16 * c:16 * c + 16], KT[:, c, NPR:NT], QBD[:, c, :], True, True,
                       [('kt', 4), ('qbd',)], [('ps', sbk)])
                stt(T1[0][0:32, 0:64], PS[0:32, sbk, 0:64], 0.125, BN[:, :], ALU.mult, ALU.add,
                    [('ps', sbk), ('bn',)], [('t1', 0)])
                act(PD[0][0:32, 0:64], T1[0][0:32, 0:64], AF.Exp, [('t1', 0)], [('pd', 0)])
                mm(PS[0:64, ob, :], PD[0][0:32, 0:64], VS[0:32, :], False, True, [('pd', 0), ('vs',)], [('ps', ob)])
                mm(PS[0:64, db, 0:1], PD[0][0:32, 0:64], ONEB[0:32, 0:1], False, True, [('pd', 0), ('c', 'oneb')],
                   [('ps', db)])
                P.op('dve', lambda e, db=db: e.reciprocal(RDD[:, 0:1], PS[0:64, db, 0:1]), [('ps', db)], [('rdd',)])
                ts(OD[:, :], PS[0:64, ob, :], RDD[:, 0:1], None, ALU.mult, None, [('ps', ob), ('rdd',)], [('od',)])
                for c in range(4):
                    bk = nbank()
                    tr(PS[:, bk, 0:64], OD[:, 128 * c:128 * c + 128], IDF[0:64, 0:64], [('od',), ('c', 'idf')],
                       [('ps', bk)])
                    tt(BINB[0:64, c, sc0:sc0 + 8], PS[0:64, bk, 16 * c:16 * c + 8], SAGS[0:64, c, 8 * s:8 * s + 8],
                       ALU.mult, [('ps', bk), ('sags',)], [('binb', 4)])
                    tt(BINB[64:128, c, sc0:sc0 + 8], PS[64:128, bk, 16 * c + 8:16 * c + 16],
                       SAGS[64:128, c, 8 * s:8 * s + 8], ALU.mult, [('ps', bk), ('sags',)], [('binb', 4)])
            P.barrier()
            if STOP <= 5:
                P.emit()
                return nc
            BINA = R1[:, 0:4 * NT].rearrange("p (c t) -> p c t", c=4)
            BINC = R1[:, 4 * NT:8 * NT].rearrange("p (c t) -> p c t", c=4)
            LP = 15 + NPR
            o = 0
            PU = R2[:, o:o + 4128].bitcast(F32)
            o += 4128
            SA = R2[:, o:o + 4128].bitcast(F32)
            o += 4128
            SB_ = R2[:, o:o + 4128].bitcast(F32)
            o += 4128
            DD = R2[:, o:o + NT]
            o += NT
            SPG = R2[:, o:o + NT]
            o += NT
            PUS = R2[:, o:o + 184].bitcast(F32).rearrange("p (s t) -> p s t", s=4)
            o += 184
            SAS = R2[:, o:o + 184].bitcast(F32).rearrange("p (s t) -> p s t", s=4)
            o += 184
            SBS = R2[:, o:o + 184].bitcast(F32).rearrange("p (s t) -> p s t", s=4)
            o += 184
            o = 0
            UU = R2[:, o:o + 4104].bitcast(F32)
            o += 4104
            US = R2[:, o:o + 80].bitcast(F32).rearrange("p (s t) -> p s t", s=4)
            o += 80
            CHS = R2[:, o:o + 1024].bitcast(F32)
            o += 1024
            SG = R2[:, o:o + 1024].bitcast(F32)
            o += 1024
            TB = R2[:, o:o + 1024].bitcast(F32)
            o += 1024
            YC = R2[:, o:o + 1024].bitcast(F32)
            o += 1024
            assert o <= 22528
            P.op('dve', lambda e: e.memset(R2[:, 0:18000], 0.0), [], [('r2z',)])
            P.barrier()
            wpu, wpuk = load_w(w_in_d[l][:, COL['pu']:COL['pu'] + 512], 512)
            wpg, wpgk = load_w(w_in_d[l][:, COL['pg']:COL['pg'] + 512], 512)
            load_w(pool_w_d[l].rearrange("g c d -> (g c) d"), 128, nk=4, dst=WPW, key=('wpw',))
            dma('sp', nps_d[l, :, 0:7, :], sp_d[l, :, 8:15, :], [], [])
            for g in range(4):
                w = WIN[g]

                def ev_pu(jj, b, c0, n, bk):
                    if b < 4:
                        cp(PU[:, 15 + c0:15 + c0 + n], PS[:, bk, 0:n], [('ps', bk)], [('pu',)], eng='act')
                    else:
                        cp(PUS[:, :, 15:23], PS[:, bk, 0:32].rearrange("p (s t) -> p s t", s=4), [('ps', bk)],
                           [('pus',)], eng='act')
                dma('sp', PUS[:, :, 0:15], spT_d[l][:, g, :, :], [], [('pus',)])
                proj_fm(wpu, wpuk, [g], ev_pu)

                def ev_pg(jj, b, c0, n, bk):
                    act(SPG[:, c0:c0 + n], PS[:, bk, 0:n], AF.Silu, [('ps', bk)], [('spg',)])
                proj_fm(wpg, wpgk, [g], ev_pg)
                cur, curs = PU, PUS
                k = 1
                nxt = [(SA, SAS), (SB_, SBS)]
                ni = 0
                while k < w:
                    d, ds_ = nxt[ni]
                    ni ^= 1
                    tt(d[:, k:LP], cur[:, k:LP], cur[:, 0:LP - k], ALU.add, [('pu',), ('sw',)], [('sw',)])
                    tt(ds_[:, :, k:23], curs[:, :, k:23], curs[:, :, 0:23 - k], ALU.add, [('pus',), ('sws',)],
                       [('sws',)])
                    cur, curs = d, ds_
                    k *= 2
                for b, (c0, n) in enumerate(BLK):
                    if b < 4:
                        stt(DD[:, c0:c0 + n], cur[:, 15 + c0:15 + c0 + n], 1.0 / w, PU[:, 15 + c0:15 + c0 + n],
                            ALU.mult, ALU.subtract, [('sw',), ('pu',)], [('dd', b)])
                        if b == 0:
                            tt(T16[:, :], cur[:, 15:31], RC[:, g, :], ALU.mult, [('sw',), ('c', 'rc_s')], [('t16',)])
                            tt(DD[:, 0:16], T16[:, :], PU[:, 15:31], ALU.subtract, [('t16',), ('pu',)], [('dd', 0)])
                    else:
                        stt(DD[:, NPR:NT].rearrange("p (s t) -> p s t", s=4), curs[:, :, 15:23], 1.0 / w,
                            PUS[:, :, 15:23], ALU.mult, ALU.subtract, [('sws',), ('pus',)], [('dd', 4)])
                    bk = nbank()
                    mm(PS[:, bk, 0:n], WPW[:, g, :], DD[:, c0:c0 + n], True, True, [('wpw',), ('dd', b)],
                       [('ps', bk)])
                    stt(BINA[:, g, c0:c0 + n], PS[:, bk, 0:n], PSC[:, l, g:g + 1], SPG[:, c0:c0 + n], ALU.mult,
                        ALU.mult, [('ps', bk), ('c', 'psc'), ('spg',)], [('bina', b)])
                bk = nbank()
                tr(PS[0:15, bk, 0:128], PU[:, LP - 15:LP], IDF[:, :], [('pu',), ('c', 'idf')], [('ps', bk)])
                cp(NPV[0:15, 0, :], PS[0:15, bk, 0:128], [('ps', bk)], [('npv', 0)])
                dma('sp', npp_d[l, :, 128 * g:128 * g + 128], NPV[0:15, 0, :], [('npv', 0)], [])
                for s in range(NSEQ):
                    bk = nbank()
                    tr(PS[0:8, bk, 0:128], PUS[:, s, 15:23], IDF[:, :], [('pus',), ('c', 'idf')], [('ps', bk)])
                    cp(NPV[0:8, 1 + s, :], PS[0:8, bk, 0:128], [('ps', bk)], [('npv', 1 + s)])
                    dma('sp', nps_d[l, s, 7:15, 128 * g:128 * g + 128], NPV[0:8, 1 + s, :], [('npv', 1 + s)], [])

            P.barrier()
            if STOP <= 6:
                P.emit()
                return nc
            for c in range(4):
                wa, wak = W[0], ('w', 0)
                wb, wbk = W[1], ('w', 1)
                a0 = c * 128
                load_w(w_in_d[l][:, COL['ch'] + a0:COL['ch'] + a0 + 128], 128, dst=W[0], key=wak, slot=8, coff=0)
                load_w(w_in_d[l][:, COL['cc'] + a0:COL['cc'] + a0 + 128], 128, dst=W[0], key=wak, slot=8, coff=128)
                load_w(w_in_d[l][:, COL['cb'] + a0:COL['cb'] + a0 + 128], 128, dst=W[1], key=wbk, slot=9, coff=0)
                load_w(w_in_d[l][:, COL['cg'] + a0:COL['cg'] + a0 + 128], 128, dst=W[1], key=wbk, slot=9, coff=128)
                dma('sp', US[:, :, 0:2], scT_d[l][:, c, :, :], [], [('us',)])
                P.op('dve', lambda e: e.memset(UU[:, 0:2], 0.0), [], [('uu',)])
                for b, (c0, n) in enumerate(BLK):
                    def one(wt, wk, jj):
                        bk = nbank()
                        for kc in range(8):
                            mm(PS[:, bk, 0:n], wt[:, kc, jj * 128:jj * 128 + 128], HT[:, kc, c0:c0 + n], kc == 0,
                               kc == 7, [wk, ('ht', b)], [('ps', bk)])
                        return bk
                    bk = one(wa, wak, 0)
                    cp(CHS[:, 0:n], PS[:, bk, 0:n], [('ps', bk)], [('chs',)], eng='act')
                    bk = one(wa, wak, 1)
                    if b < 4:
                        tt(UU[:, 2 + c0:2 + c0 + n], PS[:, bk, 0:n], CHS[:, 0:n], ALU.mult, [('ps', bk), ('chs',)],
                           [('uu',)])
                    else:
                        tt(US[:, :, 2:10], PS[:, bk, 0:32].rearrange("p (s t) -> p s t", s=4),
                           CHS[:, 0:32].rearrange("p (s t) -> p s t", s=4), ALU.mult, [('ps', bk), ('chs',)],
                           [('us',)])
                    bk = one(wb, wbk, 1)
                    act(SG[:, 0:n], PS[:, bk, 0:n], AF.Silu, [('ps', bk)], [('sg',)])
                    bk = one(wb, wbk, 0)
                    tt(TB[:, 0:n], PS[:, bk, 0:n], SG[:, 0:n], ALU.mult, [('ps', bk), ('sg',)], [('tb',)])
                    if b < 4:
                        u0, u1, u2 = (UU[:, c0 + k:c0 + k + n] for k in range(3))
                        yc, tb, oc = YC[:, 0:n], TB[:, 0:n], BINC[:, c, c0:c0 + n]
                        uk = ('uu',)
                    else:
                        u0, u1, u2 = (US[:, :, k:k + 8] for k in range(3))
                        yc = YC[:, 0:32].rearrange("p (s t) -> p s t", s=4)
                        tb = TB[:, 0:32].rearrange("p (s t) -> p s t", s=4)
                        oc = BINC[:, c, NPR:NT].rearrange("p (s t) -> p s t", s=4)
                        uk = ('us',)
                    ts(yc, u0, CW[:, l, c, 0:1], None, ALU.mult, None, [uk, ('c', 'cw')], [('yc',)])
                    stt(yc, u1, CW[:, l, c, 1:2], yc, ALU.mult, ALU.add, [uk, ('c', 'cw'), ('yc',)], [('yc',)])
                    stt(yc, u2, CW[:, l, c, 2:3], yc, ALU.mult, ALU.add, [uk, ('c', 'cw'), ('yc',)], [('yc',)])
                    tt(oc, yc, tb, ALU.mult, [('yc',), ('tb',)], [('binc', b)])
                bk = nbank()
                tr(PS[0:2, bk, 0:128], UU[:, NPR:NPR + 2], IDF[:, :], [('uu',), ('c', 'idf')], [('ps', bk)])
                cp(NCV[0:2, 0, :], PS[0:2, bk, 0:128], [('ps', bk)], [('ncv', 0)])
                dma('sp', ncp_d[l, :, 128 * c:128 * c + 128], NCV[0:2, 0, :], [('ncv', 0)], [])
                for s in range(NSEQ):
                    bk = nbank()
                    tr(PS[0:2, bk, 0:128], US[:, s, 8:10], IDF[:, :], [('us',), ('c', 'idf')], [('ps', bk)])
                    cp(NCV[0:2, 1 + s, :], PS[0:2, bk, 0:128], [('ps', bk)], [('ncv', 1 + s)])
                    dma('sp', ncs_d[l, s, :, 128 * c:128 * c + 128], NCV[0:2, 1 + s, :], [('ncv', 1 + s)], [])
            P.barrier()

            if STOP <= 7:
                P.emit()
                return nc
            MRG = R2[:, 0:8 * NT].rearrange("p (c t) -> p c t", c=8)
            o = 8 * NT
            GS = [R2[:, o + 1024 * k:o + 1024 * (k + 1)].bitcast(F32) for k in range(3)]
            o += 3072
            TP = [R2[:, o + 1024 * k:o + 1024 * (k + 1)].bitcast(F32) for k in range(2)]
            o += 2048
            assert o <= 22528
            bins = (BINA, BINB, BINC)
            binkeys = ('bina', 'binb', 'binc')
            wbr_d = (w_bra_d, w_brb_d, w_brc_d)
            for J in range(2):
                gw = []
                for x in range(3):
                    gw.append(load_w(w_in_d[l][:, COL['mg'] + 1024 * x + 512 * J:COL['mg'] + 1024 * x + 512 * J + 512],
                                     512, dst=W[x], key=('w', x), slot=8 + x))
                    load_w(wbr_d[x][l][:, 512 * J:512 * J + 512], 512, nk=4, dst=WBR[x], key=('wbr', x), slot=11)
                for jj in range(4):
                    for b, (c0, n) in enumerate(BLK):
                        for x in range(3):
                            bk = nbank()
                            for kc in range(8):
                                mm(PS[:, bk, 0:n], W[x][:, kc, jj * 128:jj * 128 + 128], HT[:, kc, c0:c0 + n],
                                   kc == 0, kc == 7, [('w', x), ('ht', b)], [('ps', bk)])
                            act(GS[x][:, 0:n], PS[:, bk, 0:n], AF.Sigmoid, [('ps', bk)], [('gs', x)])
                            bk = nbank()
                            for kc in range(4):
                                mm(PS[:, bk, 0:n], WBR[x][:, kc, jj * 128:jj * 128 + 128], bins[x][:, kc, c0:c0 + n],
                                   kc == 0, kc == 3, [('wbr', x), (binkeys[x], b)], [('ps', bk)])
                            if x == 0:
                                tt(TP[0][:, 0:n], PS[:, bk, 0:n], GS[x][:, 0:n], ALU.mult, [('ps', bk), ('gs', x)],
                                   [('tp', 0)])
                            else:
                                tt(TP[1][:, 0:n], PS[:, bk, 0:n], GS[x][:, 0:n], ALU.mult, [('ps', bk), ('gs', x)],
                                   [('tp', 1)])
                                dst = TP[0][:, 0:n] if x == 1 else MRG[:, 4 * J + jj, c0:c0 + n]
                                dk = ('tp', 0) if x == 1 else ('mrg', b)
                                tt(dst, TP[0][:, 0:n], TP[1][:, 0:n], ALU.add, [('tp', 0), ('tp', 1)], [dk],
                                   eng='pool')
            P.barrier()

            if STOP <= 8:
                P.emit()
                return nc
            WO = R1[:, 0:8192].rearrange("p (k n) -> p k n", k=8)
            GO = R1[:, 8192:16384].bitcast(F32).rearrange("p (j t) -> p j t", j=8)
            load_w(w_o_d[l], 1024, dst=WO, key=('wo',), slot=8)
            if l == 1:
                dma('sp', FG, fg_d.partition_broadcast(128), [], [('c', 'fg')], slot=6)
            for b, (c0, n) in enumerate(BLK):
                for j in range(8):
                    bk = nbank()
                    for kc in range(8):
                        mm(PS[:, bk, 0:n], WO[:, kc, j * 128:j * 128 + 128], MRG[:, kc, c0:c0 + n], kc == 0, kc == 7,
                           [('wo',), ('mrg', b)], [('ps', bk)])
                    for (cc0, nn, sq) in seq_groups(c0, n):
                        ts(GO[:, j, cc0 - c0:cc0 - c0 + nn], PS[:, bk, cc0 - c0:cc0 - c0 + nn],
                           MOD[:, l, 16 + j, sq:sq + 1], None, ALU.mult, None, [('ps', bk), ('mod',)], [('go',)])
                tiles = [i for i in range(17) if blk_of_tile(i) == b]
                for i in tiles:
                    t0, tn = TIL[i]
                    xt = XT[i % 2]
                    xk = ('xt', i % 2)
                    src = xin if l == 0 else x1_d
                    dma('sp', xt[:tn, :], src[t0:t0 + tn, :], [('x1', i)] if l == 1 else [], [xk])
                    bA = nbank()
                    bB = nbank()
                    for j in range(8):
                        bk = bA if j < 4 else bB
                        tr(PS[0:tn, bk, (j % 4) * 128:(j % 4) * 128 + 128], GO[:, j, t0 - c0:t0 - c0 + tn], IDF[:, :],
                           [('go',), ('c', 'idf')], [('ps', bk)])
                    tt(xt[:tn, 0:512], PS[0:tn, bA, :], xt[:tn, 0:512], ALU.add, [('ps', bA), xk], [xk])
                    tt(xt[:tn, 512:1024], PS[0:tn, bB, :], xt[:tn, 512:1024], ALU.add, [('ps', bB), xk], [xk])
                    if l == 0:
                        dma('sp', x1_d[t0:t0 + tn, :], xt[:tn, :], [xk], [('x1', i)])
                        norm_tile(1, i, xt, xk)
                    else:
                        P.op('dve', lambda e: e.memset(SS[:, 0:1], 0.0), [], [('ss',)])
                        act(XN[:tn, :], xt[:tn, :], AF.Square, [xk, ('ss',)], [('xn',), ('ss',)],
                            accum_out=SS[:tn, 0:1])
                        ts(SS[:tn, 1:2], SS[:tn, 0:1], 1.0 / D, 1e-6, ALU.mult, ALU.add, [('ss',)], [('ss',)])
                        act(SS[:tn, 3:4], SS[:tn, 1:2], AF.Sqrt, [('ss',)], [('ss',)])
                        P.op('dve', lambda e, tn=tn: e.reciprocal(SS[:tn, 2:3], SS[:tn, 3:4]), [('ss',)], [('ss',)])
                        stt(xt[:tn, :], xt[:tn, :], SS[:tn, 2:3], FG[:tn, :], ALU.mult, ALU.mult,
                            [xk, ('ss',), ('c', 'fg')], [xk])
                        dma('sp', y_d[t0:t0 + tn, :], xt[:tn, :], [xk], [])
            P.barrier()
        P.emit()
    return nc


_CACHE = {}


def _consts():
    ident = np.eye(128, dtype=np.float32)
    p = np.arange(128)[:, None]
    q = np.arange(128)[None, :]
    maskc = np.where(p <= q, 0.0, NEG).astype(np.float32)
    sel = np.zeros((8, 8, 128), np.float32)
    for h in range(8):
        sel[h, h, :] = 1.0
    a = np.arange(64)
    u64 = (a[:, None] > a[None, :]).astype(np.float32)
    i8 = np.eye(8, dtype=np.float32)
    maskn = np.full((32, NSEQ, 8), NEG, np.float32)
    for tk in range(32):
        for s in range(NSEQ):
            for tq in range(8):
                if tk // 8 == s and tk % 8 <= tq:
                    maskn[tk, s, tq] = 0.0
    rc = np.zeros((128, 4, 16), np.float32)
    for g, w in enumerate(WIN):
        rc[:, g, :] = 1.0 / np.minimum(np.arange(16) + 1, w)
    piota = np.arange(128, dtype=np.float32)[:, None]
    return dict(ident=ident, maskc=maskc, u64=u64, i8=i8, maskn=maskn, rc=rc, piota=piota)


def _fm(v, nchunk):
    v = np.asarray(v, np.float32)
    lead = v.shape[:-1]
    v = v.reshape(lead + (nchunk, 128))
    return np.ascontiguousarray(np.moveaxis(v, -1, 0))


def kernel(x_prompt, x_sample, cache_k, cache_v, cache_logf, state_pool, state_conv, page_table,
           c_prompt, c_sample, norm_g, w_cond, b_cond, w_in, b_f, pool_w, pool_scale, conv_w,
           w_br_a, w_br_b, w_br_c, w_o, final_g):
    f32 = lambda a: np.ascontiguousarray(np.asarray(a, dtype=np.float32))
    if 'nc' not in _CACHE:
        _CACHE['nc'] = build()
    nc = _CACHE['nc']
    consts = _consts()
    ckv = np.empty((NL, NPOOL * 128, 2, 512), np.float32)
    ckv[:, :, 0, :] = np.asarray(cache_k, np.float32).reshape(NL, NPOOL * 128, 512)
    ckv[:, :, 1, :] = np.asarray(cache_v, np.float32).reshape(NL, NPOOL * 128, 512)
    ckv = ckv.reshape(NL, NPOOL * 128, 1024)
    clf = f32(cache_logf).reshape(NL, NPOOL, 1024)
    shared = dict(cache_kv=ckv, cache_logf=clf, w_in=f32(w_in), w_cond=f32(w_cond),
                  w_br_a=f32(w_br_a), w_br_b=f32(w_br_b), w_br_c=f32(w_br_c), w_o=f32(w_o), pool_w=f32(pool_w),
                  norm_g_fm=_fm(norm_g, 8), b_cond_fm=_fm(b_cond, 24),
                  b_f=np.ascontiguousarray(f32(b_f).T), pool_scale_fm=_fm(pool_scale, 4),
                  conv_w_fm=np.ascontiguousarray(_fm(conv_w, 4).transpose(0, 1, 3, 2)),
                  final_g=f32(final_g).reshape(1, D), **consts)
    pt = np.asarray(page_table, dtype=np.int32)
    in_maps = []
    for c in range(8):
        sl = slice(4 * c, 4 * c + 4)
        m = dict(shared)
        m['xin'] = np.concatenate([f32(x_prompt[c]), f32(x_sample[sl]).reshape(NSM, D)], axis=0)
        cc = np.concatenate([f32(c_prompt[c:c + 1]), f32(c_sample[sl])], axis=0)
        m['cT'] = np.ascontiguousarray(cc.reshape(5, 8, 128).transpose(2, 1, 0))
        m['pt'] = np.ascontiguousarray(pt[sl].reshape(1, NSEQ * NPG))
        m['ptT'] = np.ascontiguousarray(np.concatenate([pt[sl].T, pt[sl].T], axis=0))
        sp = f32(state_pool[:, sl])
        m['sp_tm'] = sp
        m['spT'] = np.ascontiguousarray(sp.reshape(NL, NSEQ, 15, 4, 128).transpose(0, 4, 3, 1, 2))
        sc = f32(state_conv[:, sl])
        m['scT'] = np.ascontiguousarray(sc.reshape(NL, NSEQ, 2, 4, 128).transpose(0, 4, 3, 1, 2))
        in_maps.append(m)
    ncores = _CACHE.get('ncores', 8)
    if _CACHE.get('trace'):
        res = run_bass_kernel_spmd(nc, in_maps[:ncores], core_ids=list(range(ncores)), trace=True)
        print('EXEC_TIME_NS', res.exec_time_ns, flush=True)
    else:
        res = run_bass_kernel_spmd(nc, in_maps[:ncores], core_ids=list(range(ncores)))
    R = list(res.results)
    while len(R) < 8:
        R.append({k: np.zeros_like(v) for k, v in R[0].items()})
    cat = lambda f: np.concatenate([f(r) for r in R], axis=0)
    y_prompt = np.stack([r['y'][:NPR] for r in R])
    y_sample = cat(lambda r: r['y'][NPR:].reshape(NSEQ, 8, D))
    nkp = np.stack([r['nk'][:, :NPR].reshape(NL, NPR, 8, 64) for r in R], axis=1)
    nvp = np.stack([r['nv'][:, :NPR].reshape(NL, NPR, 8, 64) for r in R], axis=1)
    nlp = np.stack([r['nlf'][:, :NPR] for r in R], axis=1)
    npp = np.stack([r['npool_p'] for r in R], axis=1)
    ncp = np.stack([r['nconv_p'] for r in R], axis=1)
    nks = np.concatenate([r['nk'][:, NPR:].reshape(NL, NSEQ, 8, 8, 64) for r in R], axis=1)
    nvs = np.concatenate([r['nv'][:, NPR:].reshape(NL, NSEQ, 8, 8, 64) for r in R], axis=1)
    nls = np.concatenate([r['nlf'][:, NPR:].reshape(NL, NSEQ, 8, 8) for r in R], axis=1)
    nps = np.concatenate([r['npool_s'] for r in R], axis=1)
    ncs = np.concatenate([r['nconv_s'] for r in R], axis=1)
    outs = (y_prompt, y_sample, nkp, nvp, nlp, npp, ncp, nks, nvs, nls, nps, ncs)
    return tuple(np.ascontiguousarray(o, dtype=np.float32) for o in outs)
```

```python
import contextlib
import numpy as np
import concourse.bass as bass
import concourse.mybir as mybir
from concourse.bass_utils import run_bass_kernel_spmd

F32 = mybir.dt.float32
BF16 = mybir.dt.bfloat16
I32 = mybir.dt.int32
AF = mybir.ActivationFunctionType
ALU = mybir.AluOpType

NL = 2
D = 1024
NPR = 2048
NSM = 32
NT = NPR + NSM
NSEQ = 4
BLK = [(0, 512), (512, 512), (1024, 512), (1536, 512), (2048, 32)]
TIL = [(i * 128, 128) for i in range(16)] + [(2048, 32)]
COL = dict(pu=0, pg=512, q=1024, k=1536, v=2048, fl=2560, ag=2568, ch=3080, cb=3592, cc=4104, cg=4616,
           mg=5128)
WIN = (2, 4, 8, 16)
NPOOL = 2560
NPG = 64
NEG = -30000.0
DEBUG = False
STOP = 99
NOVA = False


class Prog:
    ENG = ('pe', 'act', 'dve', 'pool', 'sp')

    def __init__(self, nc):
        self.nc = nc
        self.q = {e: [] for e in self.ENG}
        self.cnt = {e: 0 for e in self.ENG}
        self.last_w = {}
        self.readers = {}
        self.seen = {e: {} for e in self.ENG}
        self.slot_cnt = {}
        self.sems = {}

    def _deps(self, eng, reads, writes):
        deps = {}

        def add(d):
            if d is None:
                return
            k, v = d
            if k == eng and eng == 'pe':
                return
            if deps.get(k, 0) < v:
                deps[k] = v
        for r in reads:
            add(self.last_w.get(r))
        for w in writes:
            add(self.last_w.get(w))
            for d in self.readers.get(w, ()):
                add(d)
        out = []
        seen = self.seen[eng]
        for k, v in deps.items():
            if seen.get(k, 0) >= v:
                continue
            seen[k] = v
            out.append((k, v))
        return out

    def _commit(self, dep, reads, writes):
        for w in writes:
            self.last_w[w] = dep
            self.readers[w] = []
        for r in reads:
            self.readers.setdefault(r, []).append(dep)

    def op(self, eng, fn, reads=(), writes=()):
        reads = list(reads)
        writes = list(writes)
        waits = self._deps(eng, reads, writes)
        self.cnt[eng] += 1
        dep = (eng, self.cnt[eng])
        self.q[eng].append((waits, fn, (eng, 1)))
        self._commit(dep, reads, writes)
        return dep

    def dma(self, eng, slot, fn, reads=(), writes=()):
        reads = list(reads)
        writes = list(writes)
        key = ('dma', slot)
        waits = self._deps(eng, reads, writes)
        prev = self.slot_cnt.get(key, 0)
        if prev and self.seen[eng].get(key, 0) < prev:
            self.seen[eng][key] = prev
            waits.append((key, prev))
        self.slot_cnt[key] = prev + 16
        dep = (key, prev + 16)
        self.q[eng].append((waits, fn, (key, 16)))
        self._commit(dep, reads, writes)
        return dep

    def barrier(self):
        tgt = {}
        for e in self.ENG:
            if e != 'sp' and self.cnt[e] > 0:
                tgt[e] = self.cnt[e]
        for k, v in self.slot_cnt.items():
            tgt[k] = v
        for e in self.ENG:
            waits = []
            for k, v in tgt.items():
                if k == e:
                    continue
                if self.seen[e].get(k, 0) < v:
                    self.seen[e][k] = v
                    waits.append((k, v))
            if waits:
                self.q[e].append((waits, None, None))

    def emit(self):
        nc = self.nc
        with contextlib.ExitStack() as st:
            for e in self.ENG:
                if e != 'sp':
                    self.sems[e] = st.enter_context(nc.semaphore('s_' + e))
            for key in sorted(self.slot_cnt):
                self.sems[key] = st.enter_context(nc.semaphore('d_%d' % key[1]))
            fin = {}
            for k, v in self.slot_cnt.items():
                fin[k] = v
            for e in self.ENG:
                if e != 'sp' and self.cnt[e] > 0:
                    fin[e] = self.cnt[e]
            block = st.enter_context(nc.Block())
            sems = self.sems

            def replay(name, eng, final=False):
                for waits, fn, inc in self.q[name]:
                    for k, v in waits:
                        eng.wait_ge(sems[k], v)
                    if fn is not None:
                        fn(eng).then_inc(sems[inc[0]], inc[1])
                if final:
                    for k, v in fin.items():
                        eng.wait_ge(sems[k], v)

            @block.tensor
            def _(e):
                replay('pe', e)

            @block.scalar
            def _(e):
                replay('act', e)

            @block.vector
            def _(e):
                replay('dve', e)

            @block.gpsimd
            def _(e):
                replay('pool', e)

            @block.sync
            def _(e):
                replay('sp', e, final=True)


def build():
    nc = bass.Bass("TRN2", target_bir_lowering=False)
    dt_in = lambda n, s, d=F32: nc.dram_tensor(n, s, d, kind="ExternalInput").ap()
    dt_out = lambda n, s, d=F32: nc.dram_tensor(n, s, d, kind="ExternalOutput").ap()
    xin = dt_in("xin", [NT, D])
    cT_d = dt_in("cT", [128, 8, 5])
    ckv_d = dt_in("cache_kv", [NL, NPOOL * 128, 1024])
    clf_d = dt_in("cache_logf", [NL, NPOOL, 1024])
    pt_d = dt_in("pt", [1, NSEQ * NPG], I32)
    ptT_d = dt_in("ptT", [128, NSEQ], I32)
    spT_d = dt_in("spT", [NL, 128, 4, NSEQ, 15])
    sp_d = dt_in("sp_tm", [NL, NSEQ, 15, 512])
    scT_d = dt_in("scT", [NL, 128, 4, NSEQ, 2])
    w_in_d = dt_in("w_in", [NL, D, 8200])
    w_cond_d = dt_in("w_cond", [NL, D, 3072])
    w_bra_d = dt_in("w_br_a", [NL, 512, D])
    w_brb_d = dt_in("w_br_b", [NL, 512, D])
    w_brc_d = dt_in("w_br_c", [NL, 512, D])
    w_o_d = dt_in("w_o", [NL, D, D])
    pool_w_d = dt_in("pool_w", [NL, 4, 128, 128])
    ng_d = dt_in("norm_g_fm", [128, NL, 8])
    bc_d = dt_in("b_cond_fm", [128, NL, 24])
    bf_d = dt_in("b_f", [8, NL])
    psc_d = dt_in("pool_scale_fm", [128, NL, 4])
    cw_d = dt_in("conv_w_fm", [128, NL, 4, 3])
    fg_d = dt_in("final_g", [1, D])
    ident_d = dt_in("ident", [128, 128])
    maskc_d = dt_in("maskc", [128, 128])
    u64_d = dt_in("u64", [64, 64])
    i8_d = dt_in("i8", [8, 8])
    maskn_d = dt_in("maskn", [32, NSEQ, 8])
    rc_d = dt_in("rc", [128, 4, 16])
    piota_d = dt_in("piota", [128, 1])

    y_d = dt_out("y", [NT, D])
    nk_d = dt_out("nk", [NL, NT, 512])
    nv_d = dt_out("nv", [NL, NT, 512])
    nlf_d = dt_out("nlf", [NL, NT, 8])
    npp_d = dt_out("npool_p", [NL, 15, 512])
    ncp_d = dt_out("nconv_p", [NL, 2, 512])
    nps_d = dt_out("npool_s", [NL, NSEQ, 15, 512])
    ncs_d = dt_out("nconv_s", [NL, NSEQ, 2, 512])
    x1_d = nc.dram_tensor("x1s", [NT, D], F32, kind="Internal").ap()

    with contextlib.ExitStack() as st:
        def sb(name, shape, dt=F32):
            return st.enter_context(nc.sbuf_tensor(name, shape, dt))
        XT = [sb("xt%d" % i, [128, D]) for i in range(2)]
        XN = sb("xn", [128, D], BF16)
        HT = sb("ht", [128, 8, NT], BF16)
        R1 = sb("r1", [128, 16896], BF16)
        R2 = sb("r2", [128, 22528], BF16)
        BINB = sb("binb", [128, 4, NT], BF16)
        W = [sb("w%d" % i, [128, 8, 512], BF16) for i in range(3)]
        WBR = [sb("wbr%d" % i, [128, 4, 512], BF16) for i in range(3)]
        WPW = sb("wpw", [128, 4, 128], BF16)
        WFL = sb("wfl", [128, 8, 8], BF16)
        MOD = sb("mod", [128, NL, 24, 5])
        CT32 = sb("ct32", [128, 8, 5])
        CTB = sb("ctb", [128, 8, 5], BF16)
        NG = sb("ng", [128, NL, 8])
        BC = sb("bc", [128, NL, 24])
        BFN = sb("bfn", [8, NL])
        PSC = sb("psc", [128, NL, 4])
        CW = sb("cw", [128, NL, 4, 3])
        IDF = sb("idf", [128, 128])
        IDB = sb("idb", [128, 128], BF16)
        MASKC = sb("maskc_s", [128, 128])
        U64 = sb("u64_s", [64, 64])
        I8 = sb("i8_s", [8, 8])
        MASKN = sb("maskn_s", [32, NSEQ, 8])
        RC = sb("rc_s", [128, 4, 16])
        PIOTA = sb("piota_s", [128, 1])
        ONE1 = sb("one1", [128, 1])
        ONES8 = sb("ones8", [8, 128])
        ONEB = sb("oneb", [128, 1], BF16)
        SAGS = sb("sags", [128, 4, NSM], BF16)
        NCV = sb("ncv", [2, 5, 128])
        NPV = sb("npv", [16, 5, 128])
        SELH = sb("selh", [8, 2, 128])
        T16 = sb("t16", [128, 16])
        LC = sb("lc", [8, 2, NT])
        LFTM = sb("lftm", [128, 17, 8])
        NCTM = sb("nctm", [128, 17, 8])
        CTM = sb("ctm", [128, 8])
        SS = sb("ss", [128, 4])
        STGT = sb("stgt", [128, 2, 512])
        STG = [STGT[:, i, :] for i in range(2)]
        FG = STGT[:, :, :].rearrange("p a b -> p (a b)")
        PTB = sb("ptb", [128, NSEQ * NPG], I32)
        IDX = sb("idx", [128, NSEQ * NPG], I32)
        PTT = sb("ptt", [128, NSEQ], I32)
        PS = st.enter_context(nc.psum_tensor("ps", [128, 8, 512], F32))

        P = Prog(nc)
        cnt = {'bank': 0, 'sbank': 0, 'abank': 0, 'slot': 0, 'stg': 0, 'w': 0}

        def nbank():
            cnt['bank'] = (cnt['bank'] + 1) % 3
            return cnt['bank']

        def sbank():
            cnt['sbank'] = (cnt['sbank'] + 1) % 3
            return 3 + cnt['sbank']

        def abank():
            cnt['abank'] = (cnt['abank'] + 1) % 2
            return 6 + cnt['abank']

        def mm(out, lhsT, rhs, start, stop, reads, writes):
            P.op('pe', lambda e: e.matmul(out, lhsT, rhs, start=start, stop=stop), reads, writes)

        def tr(out, in_, ident, reads, writes):
            P.op('pe', lambda e: e.transpose(out, in_, ident), reads, writes)

        def act(out, in_, func, reads, writes, bias=0.0, scale=1.0, accum_out=None):
            if accum_out is None:
                P.op('act', lambda e: e.activation(out, in_, func, bias=bias, scale=scale), reads, writes)
            else:
                P.op('act', lambda e: e.activation(out, in_, func, bias=bias, scale=scale,
                                                   accum_out=accum_out), reads, writes)

        def tt(out, in0, in1, op, reads, writes, eng='dve'):
            P.op(eng, lambda e: e.tensor_tensor(out, in0, in1, op), reads, writes)

        def ts(out, in0, s1, s2, op0, op1, reads, writes, eng='dve'):
            if s2 is None:
                P.op(eng, lambda e: e.tensor_scalar(out, in0, s1, None, op0), reads, writes)
            else:
                P.op(eng, lambda e: e.tensor_scalar(out, in0, s1, s2, op0, op1), reads, writes)

        def stt(out, in0, scalar, in1, op0, op1, reads, writes, eng='dve'):
            P.op(eng, lambda e: e.scalar_tensor_tensor(out, in0, scalar, in1, op0, op1), reads, writes)

        def cp(out, in_, reads, writes, eng='dve'):
            if eng == 'act':
                P.op('act', lambda e: e.copy(out, in_), reads, writes)
            else:
                P.op(eng, lambda e: e.tensor_copy(out, in_), reads, writes)

        def dma(eng, out, in_, reads, writes, slot=None):
            if slot is None:
                cnt['slot'] = (cnt['slot'] + 1) % 6
                slot = cnt['slot']
            P.dma(eng, slot, lambda e: e.dma_start(out=out, in_=in_), reads, writes)

        def gather(out, in_, idx_ap, reads, writes, slot, eoff=0):
            P.dma('pool', slot, lambda e: e.indirect_dma_start(
                out=out, out_offset=None, in_=in_,
                in_offset=bass.IndirectOffsetOnAxis(ap=idx_ap, axis=0), element_offset=eoff), reads, writes)

        def load_w(src2d, ncols, nk=8, dst=None, key=None, slot=None, coff=0):
            if dst is None:
                cnt['w'] = (cnt['w'] + 1) % 3
                s = cnt['w']
                dst = W[s]
                key = ('w', s)
                slot = 8 + s
            if slot is None:
                slot = 11
            P.dma('pool', slot, lambda e: e.dma_start(
                out=dst[:, 0:nk, coff:coff + ncols], in_=src2d.rearrange("(k p) n -> p k n", p=128)), [], [key])
            return dst, key

        def blk_of_tile(i):
            return min(i // 4, 4) if i < 16 else 4

        for dst, src in ((CT32, cT_d), (NG, ng_d), (BC, bc_d), (BFN, bf_d), (PSC, psc_d), (CW, cw_d),
                         (IDF, ident_d), (MASKC, maskc_d), (U64, u64_d), (I8, i8_d),
                         (MASKN, maskn_d), (RC, rc_d), (PIOTA, piota_d), (PTT, ptT_d)):
            dma('sp', dst[:], src, [], [('c', dst.name)], slot=6)
        dma('sp', PTB[:], pt_d.partition_broadcast(128), [], [('c', 'ptb')], slot=6)
        cp(IDB[:], IDF[:], [('c', 'idf')], [('c', 'idb')])
        cp(CTB[:], CT32[:], [('c', 'ct32')], [('c', 'ctb')])
        P.op('dve', lambda e: e.memset(ONE1[:], 1.0), [], [('c', 'one1')])
        P.op('dve', lambda e: e.memset(ONES8[:], 1.0), [], [('c', 'ones8')])
        P.op('dve', lambda e: e.memset(ONEB[:], 1.0), [], [('c', 'oneb')])
        ts(IDX[:], PTB[:], 128.0, PIOTA[:, 0:1], ALU.mult, ALU.add, [('c', 'ptb'), ('c', 'piota_s')],
           [('c', 'idx')])
        ts(BFN[:], BFN[:], -1.0, None, ALU.mult, None, [('c', 'bfn')], [('c', 'bfn')])
        ts(BC[:, :, 8:16], BC[:, :, 8:16], 1.0, None, ALU.add, None, [('c', 'bc')], [('c', 'bc')])
        P.barrier()

        for l in range(NL):
            for cb in range(6):
                wt, wk = load_w(w_cond_d[l][:, cb * 512:(cb + 1) * 512], 512)
                for jj in range(4):
                    j = cb * 4 + jj
                    bk = nbank()
                    for kc in range(8):
                        mm(PS[:, bk, 0:5], wt[:, kc, jj * 128:(jj + 1) * 128], CTB[:, kc, :], kc == 0, kc == 7,
                           [wk, ('c', 'ctb')], [('ps', bk)])
                    if j < 8 or j >= 16:
                        ts(MOD[:, l, j, :], PS[:, bk, 0:5], BC[:, l, j:j + 1], None, ALU.add, None,
                           [('ps', bk), ('c', 'bc')], [('mod',)])
                    else:
                        ts(MOD[:, l, j, :], PS[:, bk, 0:5], BC[:, l, j:j + 1], NG[:, l, j - 8:j - 7], ALU.add,
                           ALU.mult, [('ps', bk), ('c', 'bc'), ('c', 'ng')], [('mod',)])
        P.barrier()
        if STOP <= 1:
            P.emit()
            return nc

        def seq_groups(c0, n):
            if c0 < NPR:
                return [(c0, n, 0)]
            return [(NPR + 8 * s, 8, 1 + s) for s in range(NSEQ)]

        def norm_tile(l, i, xt, xkey):
            t0, n = TIL[i]
            b = blk_of_tile(i)
            P.op('dve', lambda e: e.memset(SS[:, 0:1], 0.0), [], [('ss',)])
            act(XN[:n, :], xt[:n, :], AF.Square, [xkey, ('ss',)], [('xn',), ('ss',)], accum_out=SS[:n, 0:1])
            ts(SS[:n, 1:2], SS[:n, 0:1], 1.0 / D, 1e-6, ALU.mult, ALU.add, [('ss',)], [('ss',)])
            act(SS[:n, 3:4], SS[:n, 1:2], AF.Sqrt, [('ss',)], [('ss',)])
            P.op('dve', lambda e: e.reciprocal(SS[:n, 2:3], SS[:n, 3:4]), [('ss',)], [('ss',)])
            act(XN[:n, :], xt[:n, :], AF.Copy, [xkey, ('ss',)], [('xn',)], scale=SS[:n, 2:3])
            bk = nbank()
            pb = PS[:, bk, :].bitcast(BF16).rearrange("p (a b) -> p a b", a=8)
            for kc in range(8):
                tr(pb[:, kc, 0:n], XN[:n, kc * 128:(kc + 1) * 128], IDB[:n, :n], [('xn',), ('c', 'idb')],
                   [('ps', bk)])
            for kc in range(8):
                for (c0, nn, sq) in seq_groups(t0, n):
                    ts(HT[:, kc, c0:c0 + nn], pb[:, kc, c0 - t0:c0 - t0 + nn], MOD[:, l, 8 + kc, sq:sq + 1],
                       MOD[:, l, kc, sq:sq + 1], ALU.mult, ALU.add, [('ps', bk), ('mod',)], [('ht', b)])

        def proj_fm(wt, wk, chunks, evac, blocks=None, nk=8, rhs=None, rkey=None, m=128):
            for jj in chunks:
                for b, (c0, n) in enumerate(BLK):
                    if blocks is not None and b not in blocks:
                        continue
                    bk = nbank()
                    for kc in range(nk):
                        if rhs is None:
                            r_ap, rk = HT[:, kc, c0:c0 + n], ('ht', b)
                        else:
                            r_ap, rk = rhs[:, kc, c0:c0 + n], rkey
                        mm(PS[0:m, bk, 0:n], wt[:, kc, jj * 128:jj * 128 + m], r_ap, kc == 0, kc == nk - 1,
                           [wk, rk], [('ps', bk)])
                    evac(jj, b, c0, n, bk)

        def proj_tm(wt, wk, i, ncols=512):
            t0, n = TIL[i]
            bk = nbank()
            for kc in range(8):
                mm(PS[0:n, bk, 0:ncols], HT[:, kc, t0:t0 + n], wt[:, kc, 0:ncols], kc == 0, kc == 7,
                   [wk, ('ht', blk_of_tile(i))], [('ps', bk)])
            return bk

        def stage():
            cnt['stg'] = (cnt['stg'] + 1) % 2
            return cnt['stg']

        for l in range(NL):
            if l == 0:
                for i, (t0, n) in enumerate(TIL):
                    xt = XT[i % 2]
                    dma('sp', xt[:n, :], xin[t0:t0 + n, :], [], [('xt', i % 2)])
                    norm_tile(0, i, xt, ('xt', i % 2))
            P.barrier()
            if STOP <= 2:
                P.emit()
                return nc

            QT = R1[:, 0:4 * NT].rearrange("p (c t) -> p c t", c=4)
            KT = R1[:, 4 * NT:8 * NT].rearrange("p (c t) -> p c t", c=4)
            VA = R2[:, 0:16 * 768].rearrange("p (i c) -> p i c", i=16)
            o = 16 * 768
            LQ = [R2[:, o + 1024 * k:o + 1024 * (k + 1)].bitcast(F32) for k in range(2)]
            o += 2048
            TMP = [R2[:, o + 1024 * k:o + 1024 * (k + 1)].bitcast(F32) for k in range(3)]
            o += 3072
            PT = [R2[:, o + 512 * k:o + 512 * (k + 1)] for k in range(3)]
            o += 1536
            SAG = R2[:, o:o + NT]
            o += NT
            RD = TMP[1]
            OTM = TMP[0]
            VS = R2[:, o:o + 512]
            o += 512
            assert o <= 22528

            VA4 = VA.rearrange("p i (c k) -> p i c k", c=4)
            P.op('dve', lambda e: e.memset(VA4[:, :, :, 64:128], 1.0), [], [('va', i) for i in range(16)])

            wt, wk = load_w(w_in_d[l][:, COL['q']:COL['q'] + 512], 512)
            proj_fm(wt, wk, range(4), lambda jj, b, c0, n, bk: cp(
                QT[:, jj, c0:c0 + n], PS[:, bk, 0:n], [('ps', bk)], [('qt', b)], eng='act'))
            if STOP <= 2.1:
                P.emit()
                return nc
            wt, wk = load_w(w_in_d[l][:, COL['k']:COL['k'] + 512], 512)
            proj_fm(wt, wk, range(4), lambda jj, b, c0, n, bk: cp(
                KT[:, jj, c0:c0 + n], PS[:, bk, 0:n], [('ps', bk)], [('kt', b)], eng='act'))
            if STOP <= 2.15:
                P.emit()
                return nc
            for i, (t0, n) in enumerate(TIL):
                bk = proj_tm(wt, wk, i)
                s = stage()
                cp(STG[s][:n, :], PS[:n, bk, :], [('ps', bk)], [('stg', s)])
                dma('sp', nk_d[l, t0:t0 + n, :], STG[s][:n, :], [('stg', s)], [])
            if STOP <= 2.2:
                P.emit()
                return nc
            wt, wk = load_w(w_in_d[l][:, COL['v']:COL['v'] + 512], 512)
            for i, (t0, n) in enumerate(TIL):
                bk = proj_tm(wt, wk, i)
                s = stage()
                cp(STG[s][:n, :], PS[:n, bk, :], [('ps', bk)], [('stg', s)])
                dma('sp', nv_d[l, t0:t0 + n, :], STG[s][:n, :], [('stg', s)], [])
                if NOVA:
                    pass
                elif i < 16:
                    src = PS[:, bk, :].rearrange("p (c h k) -> p c h k", c=4, h=2)
                    dst = VA[:, i, :].rearrange("p (c k) -> p c k", c=4)
                    cp(dst[:, :, 0:64], src[:, :, 0, :], [('ps', bk)], [('va', i)])
                    cp(dst[:, :, 128:192], src[:, :, 1, :], [('ps', bk)], [('va', i)])
                else:
                    cp(VS[:n, :], PS[:n, bk, :], [('ps', bk)], [('vs',)])
            if STOP <= 2.3:
                P.emit()
                return nc
            load_w(w_in_d[l][:, COL['fl']:COL['fl'] + 8], 8, dst=WFL, key=('wfl',))

            def ev_fl(jj, b, c0, n, bk):
                act(LC[:, 1, c0:c0 + n], PS[0:8, bk, 0:n], AF.Exp, [('ps', bk), ('c', 'bfn')], [('lc',)],
                    bias=BFN[:, l:l + 1], scale=-1.0)
                act(LC[:, 1, c0:c0 + n], LC[:, 1, c0:c0 + n], AF.Ln, [('lc',)], [('lc',)], bias=1.0)
                ts(LC[:, 0, c0:c0 + n], LC[:, 1, c0:c0 + n], -1.0, None, ALU.mult, None, [('lc',)], [('lc',)])
            proj_fm(WFL, ('wfl',), [0], ev_fl, m=8)
            if STOP <= 2.5:
                P.emit()
                return nc
            P.op('dve', lambda e: e.tensor_tensor_scan(
                LC[:, 1, 0:NPR], ONE1[0:8, 0:1].to_broadcast([8, NPR]), LC[:, 0, 0:NPR], 0.0, ALU.mult,
                ALU.add), [('lc',), ('c', 'one1')], [('lc',)])
            for s in range(NSEQ):
                a0 = NPR + 8 * s
                P.op('dve', lambda e, a0=a0: e.tensor_tensor_scan(
                    LC[:, 1, a0:a0 + 8], ONE1[0:8, 0:1].to_broadcast([8, 8]), LC[:, 0, a0:a0 + 8], 0.0,
                    ALU.mult, ALU.add), [('lc',), ('c', 'one1')], [('lc',)])
            if STOP <= 2.6:
                P.emit()
                return nc
            for i, (t0, n) in enumerate(TIL):
                bk = nbank()
                tr(PS[0:n, bk, 0:8], LC[:, 0, t0:t0 + n], IDF[0:8, 0:8], [('lc',), ('c', 'idf')], [('ps', bk)])
                tr(PS[0:n, bk, 8:16], LC[:, 1, t0:t0 + n], IDF[0:8, 0:8], [('lc',), ('c', 'idf')], [('ps', bk)])
                cp(LFTM[:n, i, :], PS[0:n, bk, 0:8], [('ps', bk)], [('lftm',)])
                ts(NCTM[:n, i, :], PS[0:n, bk, 8:16], -1.0, None, ALU.mult, None, [('ps', bk)], [('nctm',)])
                if i == 16:
                    cp(CTM[:n, :], PS[0:n, bk, 8:16], [('ps', bk)], [('ctm',)])
            if STOP <= 2.8:
                P.emit()
                return nc
            dma('sp', nlf_d[l, 0:NPR, :].rearrange("(i p) h -> p i h", p=128), LFTM[:, 0:16, :], [('lftm',)], [])
            dma('sp', nlf_d[l, NPR:NT, :], LFTM[0:32, 16, :], [('lftm',)], [])

            if STOP <= 3:
                P.emit()
                return nc
            wag, wagk = load_w(w_in_d[l][:, COL['ag']:COL['ag'] + 512], 512)
            for c in range(4):
                def ev_ag(jj, b, c0, n, bk, c=c):
                    if b < 4:
                        act(SAG[:, c0:c0 + n], PS[:, bk, 0:n], AF.Silu, [('ps', bk)], [('sag',)])
                    else:
                        act(SAGS[:, c, :], PS[:, bk, 0:n], AF.Silu, [('ps', bk)], [('sags',)])
                proj_fm(wag, wagk, [c], ev_ag)
                its = [(half, b) for half in range(2) for b in range(4)]

                def emit_lq(j, c=c):
                    half, b = its[j]
                    h = 2 * c + half
                    q0 = 512 * b
                    bk = nbank()
                    ts(SELH[:, j % 2, :], ONES8[:, :], I8[:, h:h + 1], None, ALU.mult, None,
                       [('c', 'ones8'), ('c', 'i8_s')], [('selh', j % 2)])
                    mm(PS[:, bk, :], SELH[:, j % 2, :], LC[:, 1, q0:q0 + 512], True, True,
                       [('selh', j % 2), ('lc',)], [('ps', bk)])
                    cp(LQ[j % 2], PS[:, bk, :], [('ps', bk)], [('lq', j % 2)], eng='act')
                emit_lq(0)
                for j, (half, b) in enumerate(its):
                    h = 2 * c + half
                    p0 = 64 * half
                    q0 = 512 * b
                    lq = LQ[j % 2]
                    lqk = ('lq', j % 2)
                    ab = abank()
                    last = 4 * b + 3

                    def emit_s(i, c=c, p0=p0, q0=q0, b=b):
                        r = i - 4 * b
                        lo = 128 * r if r >= 0 else 0
                        n = 512 - lo
                        sbk = sbank()
                        mm(PS[:, sbk, 0:n], KT[p0:p0 + 64, c, 128 * i:128 * i + 128],
                           QT[p0:p0 + 64, c, q0 + lo:q0 + 512], True, True,
                           [('kt', i // 4), ('qt', b)], [('ps', sbk)])
                        return sbk, r, lo, n
                    pend = [emit_s(0)]
                    if last >= 1:
                        pend.append(emit_s(1))
                    if j + 1 < len(its):
                        emit_lq(j + 1)
                    for i in range(last + 1):
                        sbk, r, lo, n = pend.pop(0)
                        if i + 2 <= last:
                            pend.append(emit_s(i + 2))
                        k2 = (i % 3)
                        stt(TMP[k2][:, 0:n], PS[:, sbk, 0:n], 0.125, lq[:, lo:512], ALU.mult, ALU.add,
                            [('ps', sbk), lqk], [('tmp', k2)])
                        if r >= 0:
                            tt(TMP[k2][:, 0:128], TMP[k2][:, 0:128], MASKC[:], ALU.add,
                               [('tmp', k2), ('c', 'maskc_s')], [('tmp', k2)])
                        act(PT[k2][:, 0:n], TMP[k2][:, 0:n], AF.Exp, [('tmp', k2), ('nctm',)], [('pt', k2)],
                            bias=NCTM[:, i, h:h + 1])
                        mm(PS[:, ab, lo:512], VA[:, i, c * 192 + 64 * half:c * 192 + 64 * half + 128],
                           PT[k2][:, 0:n], i == 0, i == last, [('va', i), ('pt', k2)], [('ps', ab)])
                    d0 = 64 - p0
                    act(RD[d0:d0 + 64, :], PS[d0:d0 + 64, ab, :], AF.Ln, [('ps', ab)], [('tmp', 1)])
                    act(RD[d0:d0 + 64, :], RD[d0:d0 + 64, :], AF.Exp, [('tmp', 1)], [('tmp', 1)], scale=-1.0)
                    tt(OTM[p0:p0 + 64, :], PS[p0:p0 + 64, ab, :], RD[d0:d0 + 64, :], ALU.mult,
                       [('ps', ab), ('tmp', 1)], [('tmp', 0)])
                    tt(BINB[p0:p0 + 64, c, q0:q0 + 512], OTM[p0:p0 + 64, :], SAG[p0:p0 + 64, q0:q0 + 512],
                       ALU.mult, [('tmp', 0), ('sag',)], [('binb', b)])
            P.barrier()

            if STOP <= 4:
                P.emit()
                return nc
            o = 0
            NKV = 10
            KV = [R2[:, o + 1024 * k:o + 1024 * (k + 1)] for k in range(NKV)]
            o += 1024 * NKV
            KTP = [R2[:, o + 512 * k:o + 512 * (k + 1)].rearrange("p (c t) -> p c t", c=4) for k in range(2)]
            o += 1024
            QBD = R2[:, o:o + 64].rearrange("p (c k) -> p c k", c=4)
            o += 64
            LFPF = R2[:, o:o + 2048].bitcast(F32)
            LFP = LFPF[0:64, :]
            o += 2048
            PFX = R2[0:64, o:o + 2048].bitcast(F32)
            o += 2048
            EPG = R2[0:64, o:o + 16].bitcast(F32)
            o += 16
            SUF = R2[:, o:o + 1024].bitcast(F32).rearrange("p (h g) -> p h g", h=8)
            o += 1024
            BD = R2[0:8, o:o + 128].bitcast(F32).rearrange("p (h t) -> p h t", h=8)
            o += 128
            LN = R2[:, o:o + 128].bitcast(F32)
            o += 128
            BN = R2[0:32, o:o + 128].bitcast(F32)
            o += 128
            T1 = [R2[:, o + 1024 * k:o + 1024 * (k + 1)].bitcast(F32) for k in range(2)]
            o += 2048
            PD = [R2[:, o + 512 * k:o + 512 * (k + 1)] for k in range(2)]
            o += 1024
            OD = R2[0:64, o:o + 1024].bitcast(F32)
            o += 1024
            RDD = R2[0:64, o:o + 2].bitcast(F32)
            o += 2
            assert o <= 22528
            kv_l = ckv_d.rearrange("l r c -> (l r) c")
            clf_l = clf_d.rearrange("l r c -> (l r) c")
            kv_off = l * NPOOL * 128 * 1024
            lf_off = l * NPOOL * 1024
            for s in range(NSEQ):
                sc0 = NPR + 8 * s
                gather(LFPF[:, :], clf_l, PTT[:, s:s + 1], [('c', 'ptt')], [('lfp',)], slot=12, eoff=lf_off)
                for h in range(8):
                    P.op('dve', lambda e, h=h: e.tensor_tensor_scan(
                        PFX[:, h:1024:8], ONE1[0:64, 0:1].to_broadcast([64, 128]), LFP[:, h:1024:8], 0.0,
                        ALU.mult, ALU.add), [('lfp',), ('c', 'one1')], [('pfx',)])
                bk = nbank()
                mm(PS[0:64, bk, 0:8], U64[:, :], PFX[:, 1016:1024], True, True, [('c', 'u64_s'), ('pfx',)],
                   [('ps', bk)])
                tt(EPG[:, :], PS[0:64, bk, 0:8], PFX[:, 1016:1024], ALU.add, [('ps', bk), ('pfx',)], [('epg',)])
                pf3 = PFX.rearrange("p (t h) -> p t h", h=8)
                tt(pf3, EPG[:, :].unsqueeze(1).to_broadcast([64, 128, 8]), pf3, ALU.subtract,
                   [('epg',), ('pfx',)], [('pfx',)])
                bk = nbank()
                for h in range(8):
                    tr(PS[:, bk, 64 * h:64 * h + 64], PFX[:, h:1024:8], IDF[0:64, 0:64], [('pfx',), ('c', 'idf')],
                       [('ps', bk)])
                cp(SUF.rearrange("p h g -> p (h g)"), PS[:, bk, :], [('ps', bk)], [('suf',)])
                tt(BD, LC[:, 1, sc0:sc0 + 8].unsqueeze(1).to_broadcast([8, 8, 8]),
                   I8[:, :].unsqueeze(2).to_broadcast([8, 8, 8]), ALU.mult, [('lc',), ('c', 'i8_s')], [('bd',)])
                bk = nbank()
                mm(PS[:, bk, 0:64], ONES8[:, :], BD.rearrange("p h t -> p (h t)"),
                   True, True, [('bd',), ('c', 'ones8')], [('ps', bk)])
                cp(LN[:, :], PS[:, bk, 0:64], [('ps', bk)], [('ln',)])
                bn3 = BN.rearrange("p (h t) -> p h t", h=8)
                tt(bn3, LN[0:32, :].rearrange("p (h t) -> p h t", h=8),
                   CTM[0:32, :].unsqueeze(2).to_broadcast([32, 8, 8]), ALU.subtract, [('ln',), ('ctm',)], [('bn',)])
                tt(bn3, bn3, MASKN[:, s, :].unsqueeze(1).to_broadcast([32, 8, 8]), ALU.add,
                   [('bn',), ('c', 'maskn_s')], [('bn',)])
                P.op('dve', lambda e: e.memset(QBD, 0.0), [], [('qbd',)])
                for c in range(4):
                    cp(QBD[0:64, c, 0:8], QT[0:64, c, sc0:sc0 + 8], [('qt', 4)], [('qbd',)])
                    cp(QBD[64:128, c, 8:16], QT[64:128, c, sc0:sc0 + 8], [('qt', 4)], [('qbd',)])
                ob = abank()
                db = abank()
                npages = NPG + 1
                def emit_T(g, s=s):
                    kslot = g % NKV
                    col = s * NPG + g
                    gather(KV[kslot], kv_l, IDX[:, col:col + 1], [('c', 'idx')], [('kv', kslot)], slot=13 + g % 4, eoff=kv_off)
                    tb = nbank()
                    tpb = PS[:, tb, :].bitcast(BF16).rearrange("p (a b) -> p a b", a=8)
                    for c in range(4):
                        tr(tpb[:, c, :], KV[kslot][:, 128 * c:128 * c + 128], IDB[:, :], [('kv', kslot), ('c', 'idb')],
                           [('ps', tb)])
                    k2 = g % 2
                    cp(KTP[k2], tpb[:, 0:4, :], [('ps', tb)], [('ktp', k2)], eng='act' if g % 2 == 0 else 'dve')

                def emit_S(g, sbk):
                    gi = g % 8
                    k2 = g % 2
                    for c in range(4):
                        mm(PS[:, sbk, 64 * gi + 16 * c:64 * gi + 16 * c + 16], KTP[k2][:, c, :], QBD[:, c, :],
                           True, True, [('ktp', k2), ('qbd',)], [('ps', sbk)])

                def emit_be(bt, sbk):
                    g0 = 8 * bt
                    k3 = bt % 2
                    for gi in range(8):
                        s3 = PS[:, sbk, 64 * gi:64 * gi + 64].rearrange("p (h t) -> p h t", h=8)
                        t3 = T1[k3][:, 64 * gi:64 * gi + 64].rearrange("p (h t) -> p h t", h=8)
                        suf3 = SUF[:, :, g0 + gi:g0 + gi + 1].to_broadcast([128, 8, 8])
                        stt(t3, s3, 0.125, suf3, ALU.mult, ALU.add, [('ps', sbk), ('suf',)], [('t1', k3)])
                    t3 = T1[k3].rearrange("p (g k) -> p g k", g=8)
                    ln3 = LN[:, :].unsqueeze(1).to_broadcast([128, 8, 64])
                    tt(t3, t3, ln3, ALU.add, [('t1', k3), ('ln',)], [('t1', k3)])
                    act(PD[k3][:, :], T1[k3], AF.Exp, [('t1', k3)], [('pd', k3)])

                def emit_PV(bt):
                    k3 = bt % 2
                    for gi in range(8):
                        g = 8 * bt + gi
                        vslot = g % NKV
                        mm(PS[0:64, ob, :], PD[k3][:, 64 * gi:64 * gi + 64], KV[vslot][:, 512:1024], g == 0, False,
                           [('pd', k3), ('kv', vslot)], [('ps', ob)])
                        mm(PS[0:64, db, 0:1], PD[k3][:, 64 * gi:64 * gi + 64], ONEB[:, 0:1], g == 0, False,
                           [('pd', k3), ('c', 'oneb')], [('ps', db)])
                emit_T(0)
                sbk = None
                for g in range(NPG):
                    if g % 8 == 0:
                        sbk = sbank()
                    if g % 8 == 1 and g >= 8:
                        emit_PV(g // 8 - 1)
                    if g + 1 < NPG:
                        emit_T(g + 1)
                    emit_S(g, sbk)
                    if g % 8 == 7:
                        emit_be(g // 8, sbk)
                emit_PV(NPG // 8 - 1)
                sbk = sbank()
                for c in range(4):
                    mm(PS[0:32, sbk, 16 * c:16 * c + 16], KT[:, c, NPR:NT], QBD[:, c, :], True, True,
                       [('kt', 4), ('qbd',)], [('ps', sbk)])
                stt(T1[0][0:32, 0:64], PS[0:32, sbk, 0:64], 0.125, BN[:, :], ALU.mult, ALU.add,
                    [('ps', sbk), ('bn',)], [('t1', 0)])
                act(PD[0][0:32, 0:64], T1[0][0:32, 0:64], AF.Exp, [('t1', 0)], [('pd', 0)])
                mm(PS[0:64, ob, :], PD[0][0:32, 0:64], VS[0:32, :], False, True, [('pd', 0), ('vs',)], [('ps', ob)])
                mm(PS[0:64, db, 0:1], PD[0][0:32, 0:64], ONEB[0:32, 0:1], False, True, [('pd', 0), ('c', 'oneb')],
                   [('ps', db)])
                P.op('dve', lambda e, db=db: e.reciprocal(RDD[:, 0:1], PS[0:64, db, 0:1]), [('ps', db)], [('rdd',)])
                ts(OD[:, :], PS[0:64, ob, :], RDD[:, 0:1], None, ALU.mult, None, [('ps', ob), ('rdd',)], [('od',)])
                for c in range(4):
                    bk = nbank()
                    tr(PS[:, bk, 0:64], OD[:, 128 * c:128 * c + 128], IDF[0:64, 0:64], [('od',), ('c', 'idf')],
                       [('ps', bk)])
                    tt(BINB[0:64, c, sc0:sc0 + 8], PS[0:64, bk, 16 * c:16 * c + 8], SAGS[0:64, c, 8 * s:8 * s + 8],
                       ALU.mult, [('ps', bk), ('sags',)], [('binb', 4)])
                    tt(BINB[64:128, c, sc0:sc0 + 8], PS[64:128, bk, 16 * c + 8:16 * c + 16],
                       SAGS[64:128, c, 8 * s:8 * s + 8], ALU.mult, [('ps', bk), ('sags',)], [('binb', 4)])
            P.barrier()
            if STOP <= 5:
                P.emit()
                return nc
            BINA = R1[:, 0:4 * NT].rearrange("p (c t) -> p c t", c=4)
            BINC = R1[:, 4 * NT:8 * NT].rearrange("p (c t) -> p c t", c=4)
            LP = 15 + NPR
            o = 0
            PU = R2[:, o:o + 4128].bitcast(F32)
            o += 4128
            SA = R2[:, o:o + 4128].bitcast(F32)
            o += 4128
            SB_ = R2[:, o:o + 4128].bitcast(F32)
            o += 4128
            DD = R2[:, o:o + NT]
            o += NT
            SPG = R2[:, o:o + NT]
            o += NT
            PUS = R2[:, o:o + 184].bitcast(F32).rearrange("p (s t) -> p s t", s=4)
            o += 184
            SAS = R2[:, o:o + 184].bitcast(F32).rearrange("p (s t) -> p s t", s=4)
            o += 184
            SBS = R2[:, o:o + 184].bitcast(F32).rearrange("p (s t) -> p s t", s=4)
            o += 184
            o = 0
            UU = R2[:, o:o + 4104].bitcast(F32)
            o += 4104
            US = R2[:, o:o + 80].bitcast(F32).rearrange("p (s t) -> p s t", s=4)
            o += 80
            CHS = R2[:, o:o + 1024].bitcast(F32)
            o += 1024
            SG = R2[:, o:o + 1024].bitcast(F32)
            o += 1024
            TB = R2[:, o:o + 1024].bitcast(F32)
            o += 1024
            YC = R2[:, o:o + 1024].bitcast(F32)
            o += 1024
            assert o <= 22528
            P.op('dve', lambda e: e.memset(R2[:, 0:18000], 0.0), [], [('r2z',)])
            P.barrier()
            wpu, wpuk = load_w(w_in_d[l][:, COL['pu']:COL['pu'] + 512], 512)
            wpg, wpgk = load_w(w_in_d[l][:, COL['pg']:COL['pg'] + 512], 512)
            load_w(pool_w_d[l].rearrange("g c d -> (g c) d"), 128, nk=4, dst=WPW, key=('wpw',))
            dma('sp', nps_d[l, :, 0:7, :], sp_d[l, :, 8:15, :], [], [])
            for g in range(4):
                w = WIN[g]

                def ev_pu(jj, b, c0, n, bk):
                    if b < 4:
                        cp(PU[:, 15 + c0:15 + c0 + n], PS[:, bk, 0:n], [('ps', bk)], [('pu',)], eng='act')
                    else:
                        cp(PUS[:, :, 15:23], PS[:, bk, 0:32].rearrange("p (s t) -> p s t", s=4), [('ps', bk)],
                           [('pus',)], eng='act')
                dma('sp', PUS[:, :, 0:15], spT_d[l][:, g, :, :], [], [('pus',)])
                proj_fm(wpu, wpuk, [g], ev_pu)

                def ev_pg(jj, b, c0, n, bk):
                    act(SPG[:, c0:c0 + n], PS[:, bk, 0:n], AF.Silu, [('ps', bk)], [('spg',)])
                proj_fm(wpg, wpgk, [g], ev_pg)
                cur, curs = PU, PUS
                k = 1
                nxt = [(SA, SAS), (SB_, SBS)]
                ni = 0
                while k < w:
                    d, ds_ = nxt[ni]
                    ni ^= 1
                    tt(d[:, k:LP], cur[:, k:LP], cur[:, 0:LP - k], ALU.add, [('pu',), ('sw',)], [('sw',)])
                    tt(ds_[:, :, k:23], curs[:, :, k:23], curs[:, :, 0:23 - k], ALU.add, [('pus',), ('sws',)],
                       [('sws',)])
                    cur, curs = d, ds_
                    k *= 2
                for b, (c0, n) in enumerate(BLK):
                    if b < 4:
                        stt(DD[:, c0:c0 + n], cur[:, 15 + c0:15 + c0 + n], 1.0 / w, PU[:, 15 + c0:15 + c0 + n],
                            ALU.mult, ALU.subtract, [('sw',), ('pu',)], [('dd', b)])
                        if b == 0:
                            tt(T16[:, :], cur[:, 15:31], RC[:, g, :], ALU.mult, [('sw',), ('c', 'rc_s')], [('t16',)])
                            tt(DD[:, 0:16], T16[:, :], PU[:, 15:31], ALU.subtract, [('t16',), ('pu',)], [('dd', 0)])
                    else:
                        stt(DD[:, NPR:NT].rearrange("p (s t) -> p s t", s=4), curs[:, :, 15:23], 1.0 / w,
                            PUS[:, :, 15:23], ALU.mult, ALU.subtract, [('sws',), ('pus',)], [('dd', 4)])
                    bk = nbank()
                    mm(PS[:, bk, 0:n], WPW[:, g, :], DD[:, c0:c0 + n], True, True, [('wpw',), ('dd', b)],
                       [('ps', bk)])
                    stt(BINA[:, g, c0:c0 + n], PS[:, bk, 0:n], PSC[:, l, g:g + 1], SPG[:, c0:c0 + n], ALU.mult,
                        ALU.mult, [('ps', bk), ('c', 'psc'), ('spg',)], [('bina', b)])
                bk = nbank()
                tr(PS[0:15, bk, 0:128], PU[:, LP - 15:LP], IDF[:, :], [('pu',), ('c', 'idf')], [('ps', bk)])
                cp(NPV[0:15, 0, :], PS[0:15, bk, 0:128], [('ps', bk)], [('npv', 0)])
                dma('sp', npp_d[l, :, 128 * g:128 * g + 128], NPV[0:15, 0, :], [('npv', 0)], [])
                for s in range(NSEQ):
                    bk = nbank()
                    tr(PS[0:8, bk, 0:128], PUS[:, s, 15:23], IDF[:, :], [('pus',), ('c', 'idf')], [('ps', bk)])
                    cp(NPV[0:8, 1 + s, :], PS[0:8, bk, 0:128], [('ps', bk)], [('npv', 1 + s)])
                    dma('sp', nps_d[l, s, 7:15, 128 * g:128 * g + 128], NPV[0:8, 1 + s, :], [('npv', 1 + s)], [])

            P.barrier()
            if STOP <= 6:
                P.emit()
                return nc
            for c in range(4):
                wa, wak = W[0], ('w', 0)
                wb, wbk = W[1], ('w', 1)
                a0 = c * 128
                load_w(w_in_d[l][:, COL['ch'] + a0:COL['ch'] + a0 + 128], 128, dst=W[0], key=wak, slot=8, coff=0)
                load_w(w_in_d[l][:, COL['cc'] + a0:COL['cc'] + a0 + 128], 128, dst=W[0], key=wak, slot=8, coff=128)
                load_w(w_in_d[l][:, COL['cb'] + a0:COL['cb'] + a0 + 128], 128, dst=W[1], key=wbk, slot=9, coff=0)
                load_w(w_in_d[l][:, COL['cg'] + a0:COL['cg'] + a0 + 128], 128, dst=W[1], key=wbk, slot=9, coff=128)
                dma('sp', US[:, :, 0:2], scT_d[l][:, c, :, :], [], [('us',)])
                P.op('dve', lambda e: e.memset(UU[:, 0:2], 0.0), [], [('uu',)])
                for b, (c0, n) in enumerate(BLK):
                    def one(wt, wk, jj):
                        bk = nbank()
                        for kc in range(8):
                            mm(PS[:, bk, 0:n], wt[:, kc, jj * 128:jj * 128 + 128], HT[:, kc, c0:c0 + n], kc == 0,
                               kc == 7, [wk, ('ht', b)], [('ps', bk)])
                        return bk
                    bk = one(wa, wak, 0)
                    cp(CHS[:, 0:n], PS[:, bk, 0:n], [('ps', bk)], [('chs',)], eng='act')
                    bk = one(wa, wak, 1)
                    if b < 4:
                        tt(UU[:, 2 + c0:2 + c0 + n], PS[:, bk, 0:n], CHS[:, 0:n], ALU.mult, [('ps', bk), ('chs',)],
                           [('uu',)])
                    else:
                        tt(US[:, :, 2:10], PS[:, bk, 0:32].rearrange("p (s t) -> p s t", s=4),
                           CHS[:, 0:32].rearrange("p (s t) -> p s t", s=4), ALU.mult, [('ps', bk), ('chs',)],
                           [('us',)])
                    bk = one(wb, wbk, 1)
                    act(SG[:, 0:n], PS[:, bk, 0:n], AF.Silu, [('ps', bk)], [('sg',)])
                    bk = one(wb, wbk, 0)
                    tt(TB[:, 0:n], PS[:, bk, 0:n], SG[:, 0:n], ALU.mult, [('ps', bk), ('sg',)], [('tb',)])
                    if b < 4:
                        u0, u1, u2 = (UU[:, c0 + k:c0 + k + n] for k in range(3))
                        yc, tb, oc = YC[:, 0:n], TB[:, 0:n], BINC[:, c, c0:c0 + n]
                        uk = ('uu',)
                    else:
                        u0, u1, u2 = (US[:, :, k:k + 8] for k in range(3))
                        yc = YC[:, 0:32].rearrange("p (s t) -> p s t", s=4)
                        tb = TB[:, 0:32].rearrange("p (s t) -> p s t", s=4)
                        oc = BINC[:, c, NPR:NT].rearrange("p (s t) -> p s t", s=4)
                        uk = ('us',)
                    ts(yc, u0, CW[:, l, c, 0:1], None, ALU.mult, None, [uk, ('c', 'cw')], [('yc',)])
                    stt(yc, u1, CW[:, l, c, 1:2], yc, ALU.mult, ALU.add, [uk, ('c', 'cw'), ('yc',)], [('yc',)])
                    stt(yc, u2, CW[:, l, c, 2:3], yc, ALU.mult, ALU.add, [uk, ('c', 'cw'), ('yc',)], [('yc',)])
                    tt(oc, yc, tb, ALU.mult, [('yc',), ('tb',)], [('binc', b)])
                bk = nbank()
                tr(PS[0:2, bk, 0:128], UU[:, NPR:NPR + 2], IDF[:, :], [('uu',), ('c', 'idf')], [('ps', bk)])
                cp(NCV[0:2, 0, :], PS[0:2, bk, 0:128], [('ps', bk)], [('ncv', 0)])
                dma('sp', ncp_d[l, :, 128 * c:128 * c + 128], NCV[0:2, 0, :], [('ncv', 0)], [])
                for s in range(NSEQ):
                    bk = nbank()
                    tr(PS[0:2, bk, 0:128], US[:, s, 8:10], IDF[:, :], [('us',), ('c', 'idf')], [('ps', bk)])
                    cp(NCV[0:2, 1 + s, :], PS[0:2, bk, 0:128], [('ps', bk)], [('ncv', 1 + s)])
                    dma('sp', ncs_d[l, s, :, 128 * c:128 * c + 128], NCV[0:2, 1 + s, :], [('ncv', 1 + s)], [])
            P.barrier()

            if STOP <= 7:
                P.emit()
                return nc
            MRG = R2[:, 0:8 * NT].rearrange("p (c t) -> p c t", c=8)
            o = 8 * NT
            GS = [R2[:, o + 1024 * k:o + 1024 * (k + 1)].bitcast(F32) for k in range(3)]
            o += 3072
            TP = [R2[:, o + 1024 * k:o + 1024 * (k + 1)].bitcast(F32) for k in range(2)]
            o += 2048
            assert o <= 22528
            bins = (BINA, BINB, BINC)
            binkeys = ('bina', 'binb', 'binc')
            wbr_d = (w_bra_d, w_brb_d, w_brc_d)
            for J in range(2):
                gw = []
                for x in range(3):
                    gw.append(load_w(w_in_d[l][:, COL['mg'] + 1024 * x + 512 * J:COL['mg'] + 1024 * x + 512 * J + 512],
                                     512, dst=W[x], key=('w', x), slot=8 + x))
                    load_w(wbr_d[x][l][:, 512 * J:512 * J + 512], 512, nk=4, dst=WBR[x], key=('wbr', x), slot=11)
                for jj in range(4):
                    for b, (c0, n) in enumerate(BLK):
                        for x in range(3):
                            bk = nbank()
                            for kc in range(8):
                                mm(PS[:, bk, 0:n], W[x][:, kc, jj * 128:jj * 128 + 128], HT[:, kc, c0:c0 + n],
                                   kc == 0, kc == 7, [('w', x), ('ht', b)], [('ps', bk)])
                            act(GS[x][:, 0:n], PS[:, bk, 0:n], AF.Sigmoid, [('ps', bk)], [('gs', x)])
                            bk = nbank()
                            for kc in range(4):
                                mm(PS[:, bk, 0:n], WBR[x][:, kc, jj * 128:jj * 128 + 128], bins[x][:, kc, c0:c0 + n],
                                   kc == 0, kc == 3, [('wbr', x), (binkeys[x], b)], [('ps', bk)])
                            if x == 0:
                                tt(TP[0][:, 0:n], PS[:, bk, 0:n], GS[x][:, 0:n], ALU.mult, [('ps', bk), ('gs', x)],
                                   [('tp', 0)])
                            else:
                                tt(TP[1][:, 0:n], PS[:, bk, 0:n], GS[x][:, 0:n], ALU.mult, [('ps', bk), ('gs', x)],
                                   [('tp', 1)])
                                dst = TP[0][:, 0:n] if x == 1 else MRG[:, 4 * J + jj, c0:c0 + n]
                                dk = ('tp', 0) if x == 1 else ('mrg', b)
                                tt(dst, TP[0][:, 0:n], TP[1][:, 0:n], ALU.add, [('tp', 0), ('tp', 1)], [dk],
                                   eng='pool')
            P.barrier()

            if STOP <= 8:
                P.emit()
                return nc
            WO = R1[:, 0:8192].rearrange("p (k n) -> p k n", k=8)
            GO = R1[:, 8192:16384].bitcast(F32).rearrange("p (j t) -> p j t", j=8)
            load_w(w_o_d[l], 1024, dst=WO, key=('wo',), slot=8)
            if l == 1:
                dma('sp', FG, fg_d.partition_broadcast(128), [], [('c', 'fg')], slot=6)
            for b, (c0, n) in enumerate(BLK):
                for j in range(8):
                    bk = nbank()
                    for kc in range(8):
                        mm(PS[:, bk, 0:n], WO[:, kc, j * 128:j * 128 + 128], MRG[:, kc, c0:c0 + n], kc == 0, kc == 7,
                           [('wo',), ('mrg', b)], [('ps', bk)])
                    for (cc0, nn, sq) in seq_groups(c0, n):
                        ts(GO[:, j, cc0 - c0:cc0 - c0 + nn], PS[:, bk, cc0 - c0:cc0 - c0 + nn],
                           MOD[:, l, 16 + j, sq:sq + 1], None, ALU.mult, None, [('ps', bk), ('mod',)], [('go',)])
                tiles = [i for i in range(17) if blk_of_tile(i) == b]
                for i in tiles:
                    t0, tn = TIL[i]
                    xt = XT[i % 2]
                    xk = ('xt', i % 2)
                    src = xin if l == 0 else x1_d
                    dma('sp', xt[:tn, :], src[t0:t0 + tn, :], [('x1', i)] if l == 1 else [], [xk])
                    bA = nbank()
                    bB = nbank()
                    for j in range(8):
                        bk = bA if j < 4 else bB
                        tr(PS[0:tn, bk, (j % 4) * 128:(j % 4) * 128 + 128], GO[:, j, t0 - c0:t0 - c0 + tn], IDF[:, :],
                           [('go',), ('c', 'idf')], [('ps', bk)])
                    tt(xt[:tn, 0:512], PS[0:tn, bA, :], xt[:tn, 0:512], ALU.add, [('ps', bA), xk], [xk])
                    tt(xt[:tn, 512:1024], PS[0:tn, bB, :], xt[:tn, 512:1024], ALU.add, [('ps', bB), xk], [xk])
                    if l == 0:
                        dma('sp', x1_d[t0:t0 + tn, :], xt[:tn, :], [xk], [('x1', i)])
                        norm_tile(1, i, xt, xk)
                    else:
                        P.op('dve', lambda e: e.memset(SS[:, 0:1], 0.0), [], [('ss',)])
                        act(XN[:tn, :], xt[:tn, :], AF.Square, [xk, ('ss',)], [('xn',), ('ss',)],
                            accum_out=SS[:tn, 0:1])
                        ts(SS[:tn, 1:2], SS[:tn, 0:1], 1.0 / D, 1e-6, ALU.mult, ALU.add, [('ss',)], [('ss',)])
                        act(SS[:tn, 3:4], SS[:tn, 1:2], AF.Sqrt, [('ss',)], [('ss',)])
                        P.op('dve', lambda e, tn=tn: e.reciprocal(SS[:tn, 2:3], SS[:tn, 3:4]), [('ss',)], [('ss',)])
                        stt(xt[:tn, :], xt[:tn, :], SS[:tn, 2:3], FG[:tn, :], ALU.mult, ALU.mult,
                            [xk, ('ss',), ('c', 'fg')], [xk])
                        dma('sp', y_d[t0:t0 + tn, :], xt[:tn, :], [xk], [])
            P.barrier()
        P.emit()
    return nc


_CACHE = {}


def _consts():
    ident = np.eye(128, dtype=np.float32)
    p = np.arange(128)[:, None]
    q = np.arange(128)[None, :]
    maskc = np.where(p <= q, 0.0, NEG).astype(np.float32)
    sel = np.zeros((8, 8, 128), np.float32)
    for h in range(8):
        sel[h, h, :] = 1.0
    a = np.arange(64)
    u64 = (a[:, None] > a[None, :]).astype(np.float32)
    i8 = np.eye(8, dtype=np.float32)
    maskn = np.full((32, NSEQ, 8), NEG, np.float32)
    for tk in range(32):
        for s in range(NSEQ):
            for tq in range(8):
                if tk // 8 == s and tk % 8 <= tq:
                    maskn[tk, s, tq] = 0.0
    rc = np.zeros((128, 4, 16), np.float32)
    for g, w in enumerate(WIN):
        rc[:, g, :] = 1.0 / np.minimum(np.arange(16) + 1, w)
    piota = np.arange(128, dtype=np.float32)[:, None]
    return dict(ident=ident, maskc=maskc, u64=u64, i8=i8, maskn=maskn, rc=rc, piota=piota)


def _fm(v, nchunk):
    v = np.asarray(v, np.float32)
    lead = v.shape[:-1]
    v = v.reshape(lead + (nchunk, 128))
    return np.ascontiguousarray(np.moveaxis(v, -1, 0))


def kernel(x_prompt, x_sample, cache_k, cache_v, cache_logf, state_pool, state_conv, page_table,
           c_prompt, c_sample, norm_g, w_cond, b_cond, w_in, b_f, pool_w, pool_scale, conv_w,
           w_br_a, w_br_b, w_br_c, w_o, final_g):
    f32 = lambda a: np.ascontiguousarray(np.asarray(a, dtype=np.float32))
    if 'nc' not in _CACHE:
        _CACHE['nc'] = build()
    nc = _CACHE['nc']
    consts = _consts()
    ckv = np.empty((NL, NPOOL * 128, 2, 512), np.float32)
    ckv[:, :, 0, :] = np.asarray(cache_k, np.float32).reshape(NL, NPOOL * 128, 512)
    ckv[:, :, 1, :] = np.asarray(cache_v, np.float32).reshape(NL, NPOOL * 128, 512)
    ckv = ckv.reshape(NL, NPOOL * 128, 1024)
    clf = f32(cache_logf).reshape(NL, NPOOL, 1024)
    shared = dict(cache_kv=ckv, cache_logf=clf, w_in=f32(w_in), w_cond=f32(w_cond),
                  w_br_a=f32(w_br_a), w_br_b=f32(w_br_b), w_br_c=f32(w_br_c), w_o=f32(w_o), pool_w=f32(pool_w),
                  norm_g_fm=_fm(norm_g, 8), b_cond_fm=_fm(b_cond, 24),
                  b_f=np.ascontiguousarray(f32(b_f).T), pool_scale_fm=_fm(pool_scale, 4),
                  conv_w_fm=np.ascontiguousarray(_fm(conv_w, 4).transpose(0, 1, 3, 2)),
                  final_g=f32(final_g).reshape(1, D), **consts)
    pt = np.asarray(page_table, dtype=np.int32)
    in_maps = []
    for c in range(8):
        sl = slice(4 * c, 4 * c + 4)
        m = dict(shared)
        m['xin'] = np.concatenate([f32(x_prompt[c]), f32(x_sample[sl]).reshape(NSM, D)], axis=0)
        cc = np.concatenate([f32(c_prompt[c:c + 1]), f32(c_sample[sl])], axis=0)
        m['cT'] = np.ascontiguousarray(cc.reshape(5, 8, 128).transpose(2, 1, 0))
        m['pt'] = np.ascontiguousarray(pt[sl].reshape(1, NSEQ * NPG))
        m['ptT'] = np.ascontiguousarray(np.concatenate([pt[sl].T, pt[sl].T], axis=0))
        sp = f32(state_pool[:, sl])
        m['sp_tm'] = sp
        m['spT'] = np.ascontiguousarray(sp.reshape(NL, NSEQ, 15, 4, 128).transpose(0, 4, 3, 1, 2))
        sc = f32(state_conv[:, sl])
        m['scT'] = np.ascontiguousarray(sc.reshape(NL, NSEQ, 2, 4, 128).transpose(0, 4, 3, 1, 2))
        in_maps.append(m)
    ncores = _CACHE.get('ncores', 8)
    if _CACHE.get('trace'):
        res = run_bass_kernel_spmd(nc, in_maps[:ncores], core_ids=list(range(ncores)), trace=True)
        print('EXEC_TIME_NS', res.exec_time_ns, flush=True)
    else:
        res = run_bass_kernel_spmd(nc, in_maps[:ncores], core_ids=list(range(ncores)))
    R = list(res.results)
    while len(R) < 8:
        R.append({k: np.zeros_like(v) for k, v in R[0].items()})
    cat = lambda f: np.concatenate([f(r) for r in R], axis=0)
    y_prompt = np.stack([r['y'][:NPR] for r in R])
    y_sample = cat(lambda r: r['y'][NPR:].reshape(NSEQ, 8, D))
    nkp = np.stack([r['nk'][:, :NPR].reshape(NL, NPR, 8, 64) for r in R], axis=1)
    nvp = np.stack([r['nv'][:, :NPR].reshape(NL, NPR, 8, 64) for r in R], axis=1)
    nlp = np.stack([r['nlf'][:, :NPR] for r in R], axis=1)
    npp = np.stack([r['npool_p'] for r in R], axis=1)
    ncp = np.stack([r['nconv_p'] for r in R], axis=1)
    nks = np.concatenate([r['nk'][:, NPR:].reshape(NL, NSEQ, 8, 8, 64) for r in R], axis=1)
    nvs = np.concatenate([r['nv'][:, NPR:].reshape(NL, NSEQ, 8, 8, 64) for r in R], axis=1)
    nls = np.concatenate([r['nlf'][:, NPR:].reshape(NL, NSEQ, 8, 8) for r in R], axis=1)
    nps = np.concatenate([r['npool_s'] for r in R], axis=1)
    ncs = np.concatenate([r['nconv_s'] for r in R], axis=1)
    outs = (y_prompt, y_sample, nkp, nvp, nlp, npp, ncp, nks, nvs, nls, nps, ncs)
    return tuple(np.ascontiguousarray(o, dtype=np.float32) for o in outs)
```

```python
import contextlib
import numpy as np
import concourse.bass as bass
import concourse.mybir as mybir
from concourse.bass_utils import run_bass_kernel_spmd

F32 = mybir.dt.float32
BF16 = mybir.dt.bfloat16
I32 = mybir.dt.int32
AF = mybir.ActivationFunctionType
ALU = mybir.AluOpType

NL = 2
D = 1024
NPR = 2048
NSM = 32
NT = NPR + NSM
NSEQ = 4
BLK = [(0, 512), (512, 512), (1024, 512), (1536, 512), (2048, 32)]
TIL = [(i * 128, 128) for i in range(16)] + [(2048, 32)]
COL = dict(pu=0, pg=512, q=1024, k=1536, v=2048, fl=2560, ag=2568, ch=3080, cb=3592, cc=4104, cg=4616,
           mg=5128)
WIN = (2, 4, 8, 16)
NPOOL = 2560
NPG = 64
NEG = -30000.0
DEBUG = False
STOP = 99
NOVA = False


class Prog:
    ENG = ('pe', 'act', 'dve', 'pool', 'sp')

    def __init__(self, nc):
        self.nc = nc
        self.q = {e: [] for e in self.ENG}
        self.cnt = {e: 0 for e in self.ENG}
        self.last_w = {}
        self.readers = {}
        self.seen = {e: {} for e in self.ENG}
        self.slot_cnt = {}
        self.sems = {}

    def _deps(self, eng, reads, writes):
        deps = {}

        def add(d):
            if d is None:
                return
            k, v = d
            if k == eng and eng == 'pe':
                return
            if deps.get(k, 0) < v:
                deps[k] = v
        for r in reads:
            add(self.last_w.get(r))
        for w in writes:
            add(self.last_w.get(w))
            for d in self.readers.get(w, ()):
                add(d)
        out = []
        seen = self.seen[eng]
        for k, v in deps.items():
            if seen.get(k, 0) >= v:
                continue
            seen[k] = v
            out.append((k, v))
        return out

    def _commit(self, dep, reads, writes):
        for w in writes:
            self.last_w[w] = dep
            self.readers[w] = []
        for r in reads:
            self.readers.setdefault(r, []).append(dep)

    def op(self, eng, fn, reads=(), writes=()):
        reads = list(reads)
        writes = list(writes)
        waits = self._deps(eng, reads, writes)
        self.cnt[eng] += 1
        dep = (eng, self.cnt[eng])
        self.q[eng].append((waits, fn, (eng, 1)))
        self._commit(dep, reads, writes)
        return dep

    def dma(self, eng, slot, fn, reads=(), writes=()):
        reads = list(reads)
        writes = list(writes)
        key = ('dma', slot)
        waits = self._deps(eng, reads, writes)
        prev = self.slot_cnt.get(key, 0)
        if prev and self.seen[eng].get(key, 0) < prev:
            self.seen[eng][key] = prev
            waits.append((key, prev))
        self.slot_cnt[key] = prev + 16
        dep = (key, prev + 16)
        self.q[eng].append((waits, fn, (key, 16)))
        self._commit(dep, reads, writes)
        return dep

    def barrier(self):
        tgt = {}
        for e in self.ENG:
            if e != 'sp' and self.cnt[e] > 0:
                tgt[e] = self.cnt[e]
        for k, v in self.slot_cnt.items():
            tgt[k] = v
        for e in self.ENG:
            waits = []
            for k, v in tgt.items():
                if k == e:
                    continue
                if self.seen[e].get(k, 0) < v:
                    self.seen[e][k] = v
                    waits.append((k, v))
            if waits:
                self.q[e].append((waits, None, None))

    def emit(self):
        nc = self.nc
        with contextlib.ExitStack() as st:
            for e in self.ENG:
                if e != 'sp':
                    self.sems[e] = st.enter_context(nc.semaphore('s_' + e))
            for key in sorted(self.slot_cnt):
                self.sems[key] = st.enter_context(nc.semaphore('d_%d' % key[1]))
            fin = {}
            for k, v in self.slot_cnt.items():
                fin[k] = v
            for e in self.ENG:
                if e != 'sp' and self.cnt[e] > 0:
                    fin[e] = self.cnt[e]
            block = st.enter_context(nc.Block())
            sems = self.sems

            def replay(name, eng, final=False):
                for waits, fn, inc in self.q[name]:
                    for k, v in waits:
                        eng.wait_ge(sems[k], v)
                    if fn is not None:
                        fn(eng).then_inc(sems[inc[0]], inc[1])
                if final:
                    for k, v in fin.items():
                        eng.wait_ge(sems[k], v)

            @block.tensor
            def _(e):
                replay('pe', e)

            @block.scalar
            def _(e):
                replay('act', e)

            @block.vector
            def _(e):
                replay('dve', e)

            @block.gpsimd
            def _(e):
                replay('pool', e)

            @block.sync
            def _(e):
                replay('sp', e, final=True)


def build():
    nc = bass.Bass("TRN2", target_bir_lowering=False)
    dt_in = lambda n, s, d=F32: nc.dram_tensor(n, s, d, kind="ExternalInput").ap()
    dt_out = lambda n, s, d=F32: nc.dram_tensor(n, s, d, kind="ExternalOutput").ap()
    xin = dt_in("xin", [NT, D])
    cT_d = dt_in("cT", [128, 8, 5])
    ckv_d = dt_in("cache_kv", [NL, NPOOL * 128, 1024])
    clf_d = dt_in("cache_logf", [NL, NPOOL, 1024])
    pt_d = dt_in("pt", [1, NSEQ * NPG], I32)
    ptT_d = dt_in("ptT", [128, NSEQ], I32)
    spT_d = dt_in("spT", [NL, 128, 4, NSEQ, 15])
    sp_d = dt_in("sp_tm", [NL, NSEQ, 15, 512])
    scT_d = dt_in("scT", [NL, 128, 4, NSEQ, 2])
    w_in_d = dt_in("w_in", [NL, D, 8200])
    w_cond_d = dt_in("w_cond", [NL, D, 3072])
    w_bra_d = dt_in("w_br_a", [NL, 512, D])
    w_brb_d = dt_in("w_br_b", [NL, 512, D])
    w_brc_d = dt_in("w_br_c", [NL, 512, D])
    w_o_d = dt_in("w_o", [NL, D, D])
    pool_w_d = dt_in("pool_w", [NL, 4, 128, 128])
    ng_d = dt_in("norm_g_fm", [128, NL, 8])
    bc_d = dt_in("b_cond_fm", [128, NL, 24])
    bf_d = dt_in("b_f", [8, NL])
    psc_d = dt_in("pool_scale_fm", [128, NL, 4])
    cw_d = dt_in("conv_w_fm", [128, NL, 4, 3])
    fg_d = dt_in("final_g", [1, D])
    ident_d = dt_in("ident", [128, 128])
    maskc_d = dt_in("maskc", [128, 128])
    u64_d = dt_in("u64", [64, 64])
    i8_d = dt_in("i8", [8, 8])
    maskn_d = dt_in("maskn", [32, NSEQ, 8])
    rc_d = dt_in("rc", [128, 4, 16])
    piota_d = dt_in("piota", [128, 1])

    y_d = dt_out("y", [NT, D])
    nk_d = dt_out("nk", [NL, NT, 512])
    nv_d = dt_out("nv", [NL, NT, 512])
    nlf_d = dt_out("nlf", [NL, NT, 8])
    npp_d = dt_out("npool_p", [NL, 15, 512])
    ncp_d = dt_out("nconv_p", [NL, 2, 512])
    nps_d = dt_out("npool_s", [NL, NSEQ, 15, 512])
    ncs_d = dt_out("nconv_s", [NL, NSEQ, 2, 512])
    x1_d = nc.dram_tensor("x1s", [NT, D], F32, kind="Internal").ap()

    with contextlib.ExitStack() as st:
        def sb(name, shape, dt=F32):
            return st.enter_context(nc.sbuf_tensor(name, shape, dt))
        XT = [sb("xt%d" % i, [128, D]) for i in range(2)]
        XN = sb("xn", [128, D], BF16)
        HT = sb("ht", [128, 8, NT], BF16)
        R1 = sb("r1", [128, 16896], BF16)
        R2 = sb("r2", [128, 22528], BF16)
        BINB = sb("binb", [128, 4, NT], BF16)
        W = [sb("w%d" % i, [128, 8, 512], BF16) for i in range(3)]
        WBR = [sb("wbr%d" % i, [128, 4, 512], BF16) for i in range(3)]
        WPW = sb("wpw", [128, 4, 128], BF16)
        WFL = sb("wfl", [128, 8, 8], BF16)
        MOD = sb("mod", [128, NL, 24, 5])
        CT32 = sb("ct32", [128, 8, 5])
        CTB = sb("ctb", [128, 8, 5], BF16)
        NG = sb("ng", [128, NL, 8])
        BC = sb("bc", [128, NL, 24])
        BFN = sb("bfn", [8, NL])
        PSC = sb("psc", [128, NL, 4])
        CW = sb("cw", [128, NL, 4, 3])
        IDF = sb("idf", [128, 128])
        IDB = sb("idb", [128, 128], BF16)
        MASKC = sb("maskc_s", [128, 128])
        U64 = sb("u64_s", [64, 64])
        I8 = sb("i8_s", [8, 8])
        MASKN = sb("maskn_s", [32, NSEQ, 8])
        RC = sb("rc_s", [128, 4, 16])
        PIOTA = sb("piota_s", [128, 1])
        ONE1 = sb("one1", [128, 1])
        ONES8 = sb("ones8", [8, 128])
        ONEB = sb("oneb", [128, 1], BF16)
        SAGS = sb("sags", [128, 4, NSM], BF16)
        NCV = sb("ncv", [2, 5, 128])
        NPV = sb("npv", [16, 5, 128])
        SELH = sb("selh", [8, 2, 128])
        T16 = sb("t16", [128, 16])
        LC = sb("lc", [8, 2, NT])
        LFTM = sb("lftm", [128, 17, 8])
        NCTM = sb("nctm", [128, 17, 8])
        CTM = sb("ctm", [128, 8])
        SS = sb("ss", [128, 4])
        STGT = sb("stgt", [128, 2, 512])
        STG = [STGT[:, i, :] for i in range(2)]
        FG = STGT[:, :, :].rearrange("p a b -> p (a b)")
        PTB = sb("ptb", [128, NSEQ * NPG], I32)
        IDX = sb("idx", [128, NSEQ * NPG], I32)
        PTT = sb("ptt", [128, NSEQ], I32)
        PS = st.enter_context(nc.psum_tensor("ps", [128, 8, 512], F32))

        P = Prog(nc)
        cnt = {'bank': 0, 'sbank': 0, 'abank': 0, 'slot': 0, 'stg': 0, 'w': 0}

        def nbank():
            cnt['bank'] = (cnt['bank'] + 1) % 3
            return cnt['bank']

        def sbank():
            cnt['sbank'] = (cnt['sbank'] + 1) % 3
            return 3 + cnt['sbank']

        def abank():
            cnt['abank'] = (cnt['abank'] + 1) % 2
            return 6 + cnt['abank']

        def mm(out, lhsT, rhs, start, stop, reads, writes):
            P.op('pe', lambda e: e.matmul(out, lhsT, rhs, start=start, stop=stop), reads, writes)

        def tr(out, in_, ident, reads, writes):
            P.op('pe', lambda e: e.transpose(out, in_, ident), reads, writes)

        def act(out, in_, func, reads, writes, bias=0.0, scale=1.0, accum_out=None):
            if accum_out is None:
                P.op('act', lambda e: e.activation(out, in_, func, bias=bias, scale=scale), reads, writes)
            else:
                P.op('act', lambda e: e.activation(out, in_, func, bias=bias, scale=scale,
                                                   accum_out=accum_out), reads, writes)

        def tt(out, in0, in1, op, reads, writes, eng='dve'):
            P.op(eng, lambda e: e.tensor_tensor(out, in0, in1, op), reads, writes)

        def ts(out, in0, s1, s2, op0, op1, reads, writes, eng='dve'):
            if s2 is None:
                P.op(eng, lambda e: e.tensor_scalar(out, in0, s1, None, op0), reads, writes)
            else:
                P.op(eng, lambda e: e.tensor_scalar(out, in0, s1, s2, op0, op1), reads, writes)

        def stt(out, in0, scalar, in1, op0, op1, reads, writes, eng='dve'):
            P.op(eng, lambda e: e.scalar_tensor_tensor(out, in0, scalar, in1, op0, op1), reads, writes)

        def cp(out, in_, reads, writes, eng='dve'):
            if eng == 'act':
                P.op('act', lambda e: e.copy(out, in_), reads, writes)
            else:
                P.op(eng, lambda e: e.tensor_copy(out, in_), reads, writes)

        def dma(eng, out, in_, reads, writes, slot=None):
            if slot is None:
                cnt['slot'] = (cnt['slot'] + 1) % 6
                slot = cnt['slot']
            P.dma(eng, slot, lambda e: e.dma_start(out=out, in_=in_), reads, writes)

        def gather(out, in_, idx_ap, reads, writes, slot, eoff=0):
            P.dma('pool', slot, lambda e: e.indirect_dma_start(
                out=out, out_offset=None, in_=in_,
                in_offset=bass.IndirectOffsetOnAxis(ap=idx_ap, axis=0), element_offset=eoff), reads, writes)

        def load_w(src2d, ncols, nk=8, dst=None, key=None, slot=None, coff=0):
            if dst is None:
                cnt['w'] = (cnt['w'] + 1) % 3
                s = cnt['w']
                dst = W[s]
                key = ('w', s)
                slot = 8 + s
            if slot is None:
                slot = 11
            P.dma('pool', slot, lambda e: e.dma_start(
                out=dst[:, 0:nk, coff:coff + ncols], in_=src2d.rearrange("(k p) n -> p k n", p=128)), [], [key])
            return dst, key

        def blk_of_tile(i):
            return min(i // 4, 4) if i < 16 else 4

        for dst, src in ((CT32, cT_d), (NG, ng_d), (BC, bc_d), (BFN, bf_d), (PSC, psc_d), (CW, cw_d),
                         (IDF, ident_d), (MASKC, maskc_d), (U64, u64_d), (I8, i8_d),
                         (MASKN, maskn_d), (RC, rc_d), (PIOTA, piota_d), (PTT, ptT_d)):
            dma('sp', dst[:], src, [], [('c', dst.name)], slot=6)
        dma('sp', PTB[:], pt_d.partition_broadcast(128), [], [('c', 'ptb')], slot=6)
        cp(IDB[:], IDF[:], [('c', 'idf')], [('c', 'idb')])
        cp(CTB[:], CT32[:], [('c', 'ct32')], [('c', 'ctb')])
        P.op('dve', lambda e: e.memset(ONE1[:], 1.0), [], [('c', 'one1')])
        P.op('dve', lambda e: e.memset(ONES8[:], 1.0), [], [('c', 'ones8')])
        P.op('dve', lambda e: e.memset(ONEB[:], 1.0), [], [('c', 'oneb')])
        ts(IDX[:], PTB[:], 128.0, PIOTA[:, 0:1], ALU.mult, ALU.add, [('c', 'ptb'), ('c', 'piota_s')],
           [('c', 'idx')])
        ts(BFN[:], BFN[:], -1.0, None, ALU.mult, None, [('c', 'bfn')], [('c', 'bfn')])
        ts(BC[:, :, 8:16], BC[:, :, 8:16], 1.0, None, ALU.add, None, [('c', 'bc')], [('c', 'bc')])
        P.barrier()

        for l in range(NL):
            for cb in range(6):
                wt, wk = load_w(w_cond_d[l][:, cb * 512:(cb + 1) * 512], 512)
                for jj in range(4):
                    j = cb * 4 + jj
                    bk = nbank()
                    for kc in range(8):
                        mm(PS[:, bk, 0:5], wt[:, kc, jj * 128:(jj + 1) * 128], CTB[:, kc, :], kc == 0, kc == 7,
                           [wk, ('c', 'ctb')], [('ps', bk)])
                    if j < 8 or j >= 16:
                        ts(MOD[:, l, j, :], PS[:, bk, 0:5], BC[:, l, j:j + 1], None, ALU.add, None,
                           [('ps', bk), ('c', 'bc')], [('mod',)])
                    else:
                        ts(MOD[:, l, j, :], PS[:, bk, 0:5], BC[:, l, j:j + 1], NG[:, l, j - 8:j - 7], ALU.add,
                           ALU.mult, [('ps', bk), ('c', 'bc'), ('c', 'ng')], [('mod',)])
        P.barrier()
        if STOP <= 1:
            P.emit()
            return nc

        def seq_groups(c0, n):
            if c0 < NPR:
                return [(c0, n, 0)]
            return [(NPR + 8 * s, 8, 1 + s) for s in range(NSEQ)]

        def norm_tile(l, i, xt, xkey):
            t0, n = TIL[i]
            b = blk_of_tile(i)
            P.op('dve', lambda e: e.memset(SS[:, 0:1], 0.0), [], [('ss',)])
            act(XN[:n, :], xt[:n, :], AF.Square, [xkey, ('ss',)], [('xn',), ('ss',)], accum_out=SS[:n, 0:1])
            ts(SS[:n, 1:2], SS[:n, 0:1], 1.0 / D, 1e-6, ALU.mult, ALU.add, [('ss',)], [('ss',)])
            act(SS[:n, 3:4], SS[:n, 1:2], AF.Sqrt, [('ss',)], [('ss',)])
            P.op('dve', lambda e: e.reciprocal(SS[:n, 2:3], SS[:n, 3:4]), [('ss',)], [('ss',)])
            act(XN[:n, :], xt[:n, :], AF.Copy, [xkey, ('ss',)], [('xn',)], scale=SS[:n, 2:3])
            bk = nbank()
            pb = PS[:, bk, :].bitcast(BF16).rearrange("p (a b) -> p a b", a=8)
            for kc in range(8):
                tr(pb[:, kc, 0:n], XN[:n, kc * 128:(kc + 1) * 128], IDB[:n, :n], [('xn',), ('c', 'idb')],
                   [('ps', bk)])
            for kc in range(8):
                for (c0, nn, sq) in seq_groups(t0, n):
                    ts(HT[:, kc, c0:c0 + nn], pb[:, kc, c0 - t0:c0 - t0 + nn], MOD[:, l, 8 + kc, sq:sq + 1],
                       MOD[:, l, kc, sq:sq + 1], ALU.mult, ALU.add, [('ps', bk), ('mod',)], [('ht', b)])

        def proj_fm(wt, wk, chunks, evac, blocks=None, nk=8, rhs=None, rkey=None, m=128):
            for jj in chunks:
                for b, (c0, n) in enumerate(BLK):
                    if blocks is not None and b not in blocks:
                        continue
                    bk = nbank()
                    for kc in range(nk):
                        if rhs is None:
                            r_ap, rk = HT[:, kc, c0:c0 + n], ('ht', b)
                        else:
                            r_ap, rk = rhs[:, kc, c0:c0 + n], rkey
                        mm(PS[0:m, bk, 0:n], wt[:, kc, jj * 128:jj * 128 + m], r_ap, kc == 0, kc == nk - 1,
                           [wk, rk], [('ps', bk)])
                    evac(jj, b, c0, n, bk)

        def proj_tm(wt, wk, i, ncols=512):
            t0, n = TIL[i]
            bk = nbank()
            for kc in range(8):
                mm(PS[0:n, bk, 0:ncols], HT[:, kc, t0:t0 + n], wt[:, kc, 0:ncols], kc == 0, kc == 7,
                   [wk, ('ht', blk_of_tile(i))], [('ps', bk)])
            return bk

        def stage():
            cnt['stg'] = (cnt['stg'] + 1) % 2
            return cnt['stg']

        for l in range(NL):
            if l == 0:
                for i, (t0, n) in enumerate(TIL):
                    xt = XT[i % 2]
                    dma('sp', xt[:n, :], xin[t0:t0 + n, :], [], [('xt', i % 2)])
                    norm_tile(0, i, xt, ('xt', i % 2))
            P.barrier()
            if STOP <= 2:
                P.emit()
                return nc

            QT = R1[:, 0:4 * NT].rearrange("p (c t) -> p c t", c=4)
            KT = R1[:, 4 * NT:8 * NT].rearrange("p (c t) -> p c t", c=4)
            VA = R2[:, 0:16 * 768].rearrange("p (i c) -> p i c", i=16)
            o = 16 * 768
            LQ = [R2[:, o + 1024 * k:o + 1024 * (k + 1)].bitcast(F32) for k in range(2)]
            o += 2048
            TMP = [R2[:, o + 1024 * k:o + 1024 * (k + 1)].bitcast(F32) for k in range(3)]
            o += 3072
            PT = [R2[:, o + 512 * k:o + 512 * (k + 1)] for k in range(3)]
            o += 1536
            SAG = R2[:, o:o + NT]
            o += NT
            RD = TMP[1]
            OTM = TMP[0]
            VS = R2[:, o:o + 512]
            o += 512
            assert o <= 22528

            VA4 = VA.rearrange("p i (c k) -> p i c k", c=4)
            P.op('dve', lambda e: e.memset(VA4[:, :, :, 64:128], 1.0), [], [('va', i) for i in range(16)])

            wt, wk = load_w(w_in_d[l][:, COL['q']:COL['q'] + 512], 512)
            proj_fm(wt, wk, range(4), lambda jj, b, c0, n, bk: cp(
                QT[:, jj, c0:c0 + n], PS[:, bk, 0:n], [('ps', bk)], [('qt', b)], eng='act'))
            if STOP <= 2.1:
                P.emit()
                return nc
            wt, wk = load_w(w_in_d[l][:, COL['k']:COL['k'] + 512], 512)
            proj_fm(wt, wk, range(4), lambda jj, b, c0, n, bk: cp(
                KT[:, jj, c0:c0 + n], PS[:, bk, 0:n], [('ps', bk)], [('kt', b)], eng='act'))
            if STOP <= 2.15:
                P.emit()
                return nc
            for i, (t0, n) in enumerate(TIL):
                bk = proj_tm(wt, wk, i)
                s = stage()
                cp(STG[s][:n, :], PS[:n, bk, :], [('ps', bk)], [('stg', s)])
                dma('sp', nk_d[l, t0:t0 + n, :], STG[s][:n, :], [('stg', s)], [])
            if STOP <= 2.2:
                P.emit()
                return nc
            wt, wk = load_w(w_in_d[l][:, COL['v']:COL['v'] + 512], 512)
            for i, (t0, n) in enumerate(TIL):
                bk = proj_tm(wt, wk, i)
                s = stage()
                cp(STG[s][:n, :], PS[:n, bk, :], [('ps', bk)], [('stg', s)])
                dma('sp', nv_d[l, t0:t0 + n, :], STG[s][:n, :], [('stg', s)], [])
                if NOVA:
                    pass
                elif i < 16:
                    src = PS[:, bk, :].rearrange("p (c h k) -> p c h k", c=4, h=2)
                    dst = VA[:, i, :].rearrange("p (c k) -> p c k", c=4)
                    cp(dst[:, :, 0:64], src[:, :, 0, :], [('ps', bk)], [('va', i)])
                    cp(dst[:, :, 128:192], src[:, :, 1, :], [('ps', bk)], [('va', i)])
                else:
                    cp(VS[:n, :], PS[:n, bk, :], [('ps', bk)], [('vs',)])
            if STOP <= 2.3:
                P.emit()
                return nc
            load_w(w_in_d[l][:, COL['fl']:COL['fl'] + 8], 8, dst=WFL, key=('wfl',))

            def ev_fl(jj, b, c0, n, bk):
                act(LC[:, 1, c0:c0 + n], PS[0:8, bk, 0:n], AF.Exp, [('ps', bk), ('c', 'bfn')], [('lc',)],
                    bias=BFN[:, l:l + 1], scale=-1.0)
                act(LC[:, 1, c0:c0 + n], LC[:, 1, c0:c0 + n], AF.Ln, [('lc',)], [('lc',)], bias=1.0)
                ts(LC[:, 0, c0:c0 + n], LC[:, 1, c0:c0 + n], -1.0, None, ALU.mult, None, [('lc',)], [('lc',)])
            proj_fm(WFL, ('wfl',), [0], ev_fl, m=8)
            if STOP <= 2.5:
                P.emit()
                return nc
            P.op('dve', lambda e: e.tensor_tensor_scan(
                LC[:, 1, 0:NPR], ONE1[0:8, 0:1].to_broadcast([8, NPR]), LC[:, 0, 0:NPR], 0.0, ALU.mult,
                ALU.add), [('lc',), ('c', 'one1')], [('lc',)])
            for s in range(NSEQ):
                a0 = NPR + 8 * s
                P.op('dve', lambda e, a0=a0: e.tensor_tensor_scan(
                    LC[:, 1, a0:a0 + 8], ONE1[0:8, 0:1].to_broadcast([8, 8]), LC[:, 0, a0:a0 + 8], 0.0,
                    ALU.mult, ALU.add), [('lc',), ('c', 'one1')], [('lc',)])
            if STOP <= 2.6:
                P.emit()
                return nc
            for i, (t0, n) in enumerate(TIL):
                bk = nbank()
                tr(PS[0:n, bk, 0:8], LC[:, 0, t0:t0 + n], IDF[0:8, 0:8], [('lc',), ('c', 'idf')], [('ps', bk)])
                tr(PS[0:n, bk, 8:16], LC[:, 1, t0:t0 + n], IDF[0:8, 0:8], [('lc',), ('c', 'idf')], [('ps', bk)])
                cp(LFTM[:n, i, :], PS[0:n, bk, 0:8], [('ps', bk)], [('lftm',)])
                ts(NCTM[:n, i, :], PS[0:n, bk, 8:16], -1.0, None, ALU.mult, None, [('ps', bk)], [('nctm',)])
                if i == 16:
                    cp(CTM[:n, :], PS[0:n, bk, 8:16], [('ps', bk)], [('ctm',)])
            if STOP <= 2.8:
                P.emit()
                return nc
            dma('sp', nlf_d[l, 0:NPR, :].rearrange("(i p) h -> p i h", p=128), LFTM[:, 0:16, :], [('lftm',)], [])
            dma('sp', nlf_d[l, NPR:NT, :], LFTM[0:32, 16, :], [('lftm',)], [])

            if STOP <= 3:
                P.emit()
                return nc
            wag, wagk = load_w(w_in_d[l][:, COL['ag']:COL['ag'] + 512], 512)
            for c in range(4):
                def ev_ag(jj, b, c0, n, bk, c=c):
                    if b < 4:
                        act(SAG[:, c0:c0 + n], PS[:, bk, 0:n], AF.Silu, [('ps', bk)], [('sag',)])
                    else:
                        act(SAGS[:, c, :], PS[:, bk, 0:n], AF.Silu, [('ps', bk)], [('sags',)])
                proj_fm(wag, wagk, [c], ev_ag)
                its = [(half, b) for half in range(2) for b in range(4)]

                def emit_lq(j, c=c):
                    half, b = its[j]
                    h = 2 * c + half
                    q0 = 512 * b
                    bk = nbank()
                    ts(SELH[:, j % 2, :], ONES8[:, :], I8[:, h:h + 1], None, ALU.mult, None,
                       [('c', 'ones8'), ('c', 'i8_s')], [('selh', j % 2)])
                    mm(PS[:, bk, :], SELH[:, j % 2, :], LC[:, 1, q0:q0 + 512], True, True,
                       [('selh', j % 2), ('lc',)], [('ps', bk)])
                    cp(LQ[j % 2], PS[:, bk, :], [('ps', bk)], [('lq', j % 2)], eng='act')
                emit_lq(0)
                for j, (half, b) in enumerate(its):
                    h = 2 * c + half
                    p0 = 64 * half
                    q0 = 512 * b
                    lq = LQ[j % 2]
                    lqk = ('lq', j % 2)
                    ab = abank()
                    last = 4 * b + 3

                    def emit_s(i, c=c, p0=p0, q0=q0, b=b):
                        r = i - 4 * b
                        lo = 128 * r if r >= 0 else 0
                        n = 512 - lo
                        sbk = sbank()
                        mm(PS[:, sbk, 0:n], KT[p0:p0 + 64, c, 128 * i:128 * i + 128],
                           QT[p0:p0 + 64, c, q0 + lo:q0 + 512], True, True,
                           [('kt', i // 4), ('qt', b)], [('ps', sbk)])
                        return sbk, r, lo, n
                    pend = [emit_s(0)]
                    if last >= 1:
                        pend.append(emit_s(1))
                    if j + 1 < len(its):
                        emit_lq(j + 1)
                    for i in range(last + 1):
                        sbk, r, lo, n = pend.pop(0)
                        if i + 2 <= last:
                            pend.append(emit_s(i + 2))
                        k2 = (i % 3)
                        stt(TMP[k2][:, 0:n], PS[:, sbk, 0:n], 0.125, lq[:, lo:512], ALU.mult, ALU.add,
                            [('ps', sbk), lqk], [('tmp', k2)])
                        if r >= 0:
                            tt(TMP[k2][:, 0:128], TMP[k2][:, 0:128], MASKC[:], ALU.add,
                               [('tmp', k2), ('c', 'maskc_s')], [('tmp', k2)])
                        act(PT[k2][:, 0:n], TMP[k2][:, 0:n], AF.Exp, [('tmp', k2), ('nctm',)], [('pt', k2)],
                            bias=NCTM[:, i, h:h + 1])
                        mm(PS[:, ab, lo:512], VA[:, i, c * 192 + 64 * half:c * 192 + 64 * half + 128],
                           PT[k2][:, 0:n], i == 0, i == last, [('va', i), ('pt', k2)], [('ps', ab)])
                    d0 = 64 - p0
                    act(RD[d0:d0 + 64, :], PS[d0:d0 + 64, ab, :], AF.Ln, [('ps', ab)], [('tmp', 1)])
                    act(RD[d0:d0 + 64, :], RD[d0:d0 + 64, :], AF.Exp, [('tmp', 1)], [('tmp', 1)], scale=-1.0)
                    tt(OTM[p0:p0 + 64, :], PS[p0:p0 + 64, ab, :], RD[d0:d0 + 64, :], ALU.mult,
                       [('ps', ab), ('tmp', 1)], [('tmp', 0)])
                    tt(BINB[p0:p0 + 64, c, q0:q0 + 512], OTM[p0:p0 + 64, :], SAG[p0:p0 + 64, q0:q0 + 512],
                       ALU.mult, [('tmp', 0), ('sag',)], [('binb', b)])
            P.barrier()

            if STOP <= 4:
                P.emit()
                return nc
            o = 0
            NKV = 10
            KV = [R2[:, o + 1024 * k:o + 1024 * (k + 1)] for k in range(NKV)]
            o += 1024 * NKV
            KTP = [R2[:, o + 512 * k:o + 512 * (k + 1)].rearrange("p (c t) -> p c t", c=4) for k in range(2)]
            o += 1024
            QBD = R2[:, o:o + 64].rearrange("p (c k) -> p c k", c=4)
            o += 64
            LFPF = R2[:, o:o + 2048].bitcast(F32)
            LFP = LFPF[0:64, :]
            o += 2048
            PFX = R2[0:64, o:o + 2048].bitcast(F32)
            o += 2048
            EPG = R2[0:64, o:o + 16].bitcast(F32)
            o += 16
            SUFP = R2[:, o:o + 1024].bitcast(F32)
            o += 1024
            BD = R2[0:8, o:o + 128].bitcast(F32).rearrange("p (h t) -> p h t", h=8)
            o += 128
            LN = R2[:, o:o + 128].bitcast(F32)
            o += 128
            BN = R2[0:32, o:o + 128].bitcast(F32)
            o += 128
            T1 = [R2[:, o + 1024 * k:o + 1024 * (k + 1)].bitcast(F32) for k in range(2)]
            o += 2048
            PD = [R2[:, o + 512 * k:o + 512 * (k + 1)] for k in range(2)]
            o += 1024
            OD = R2[0:64, o:o + 1024].bitcast(F32)
            o += 1024
            RDD = R2[0:64, o:o + 2].bitcast(F32)
            o += 2
            assert o <= 22528
            kv_l = ckv_d.rearrange("l r c -> (l r) c")
            clf_l = clf_d.rearrange("l r c -> (l r) c")
            kv_off = l * NPOOL * 128 * 1024
            lf_off = l * NPOOL * 1024
            for s in range(NSEQ):
                sc0 = NPR + 8 * s
                gather(LFPF[:, :], clf_l, PTT[:, s:s + 1], [('c', 'ptt')], [('lfp',)], slot=12, eoff=lf_off)
                for h in range(8):
                    P.op('dve', lambda e, h=h: e.tensor_tensor_scan(
                        PFX[:, h:1024:8], ONE1[0:64, 0:1].to_broadcast([64, 128]), LFP[:, h:1024:8], 0.0,
                        ALU.mult, ALU.add), [('lfp',), ('c', 'one1')], [('pfx',)])
                bk = nbank()
                mm(PS[0:64, bk, 0:8], U64[:, :], PFX[:, 1016:1024], True, True, [('c', 'u64_s'), ('pfx',)],
                   [('ps', bk)])
                tt(EPG[:, :], PS[0:64, bk, 0:8], PFX[:, 1016:1024], ALU.add, [('ps', bk), ('pfx',)], [('epg',)])
                pf3 = PFX.rearrange("p (t h) -> p t h", h=8)
                tt(pf3, EPG[:, :].unsqueeze(1).to_broadcast([64, 128, 8]), pf3, ALU.subtract,
                   [('epg',), ('pfx',)], [('pfx',)])
                bk = nbank()
                for h in range(8):
                    tr(PS[:, bk, 64 * h:64 * h + 64], PFX[:, h:1024:8], IDF[0:64, 0:64], [('pfx',), ('c', 'idf')],
                       [('ps', bk)])
                cp(SUFP.rearrange("p (g h) -> p h g", h=8), PS[:, bk, :].rearrange("p (h g) -> p h g", h=8),
                   [('ps', bk)], [('suf',)])
                tt(BD, LC[:, 1, sc0:sc0 + 8].unsqueeze(1).to_broadcast([8, 8, 8]),
                   I8[:, :].unsqueeze(2).to_broadcast([8, 8, 8]), ALU.mult, [('lc',), ('c', 'i8_s')], [('bd',)])
                bk = nbank()
                mm(PS[:, bk, 0:64], ONES8[:, :], BD.rearrange("p h t -> p (h t)"),
                   True, True, [('bd',), ('c', 'ones8')], [('ps', bk)])
                cp(LN[:, :], PS[:, bk, 0:64], [('ps', bk)], [('ln',)])
                bn3 = BN.rearrange("p (h t) -> p h t", h=8)
                tt(bn3, LN[0:32, :].rearrange("p (h t) -> p h t", h=8),
                   CTM[0:32, :].unsqueeze(2).to_broadcast([32, 8, 8]), ALU.subtract, [('ln',), ('ctm',)], [('bn',)])
                tt(bn3, bn3, MASKN[:, s, :].unsqueeze(1).to_broadcast([32, 8, 8]), ALU.add,
                   [('bn',), ('c', 'maskn_s')], [('bn',)])
                P.op('dve', lambda e: e.memset(QBD, 0.0), [], [('qbd',)])
                for c in range(4):
                    cp(QBD[0:64, c, 0:8], QT[0:64, c, sc0:sc0 + 8], [('qt', 4)], [('qbd',)])
                    cp(QBD[64:128, c, 8:16], QT[64:128, c, sc0:sc0 + 8], [('qt', 4)], [('qbd',)])
                ob = abank()
                db = abank()
                npages = NPG + 1
                def emit_T(g, s=s):
                    kslot = g % NKV
                    col = s * NPG + g
                    gather(KV[kslot], kv_l, IDX[:, col:col + 1], [('c', 'idx')], [('kv', kslot)], slot=13 + g % 4, eoff=kv_off)
                    tb = nbank()
                    tpb = PS[:, tb, :].bitcast(BF16).rearrange("p (a b) -> p a b", a=8)
                    for c in range(4):
                        tr(tpb[:, c, :], KV[kslot][:, 128 * c:128 * c + 128], IDB[:, :], [('kv', kslot), ('c', 'idb')],
                           [('ps', tb)])
                    k2 = g % 2
                    cp(KTP[k2], tpb[:, 0:4, :], [('ps', tb)], [('ktp', k2)], eng='act' if g % 2 == 0 else 'dve')

                def emit_S(g, sbk):
                    gi = g % 8
                    k2 = g % 2
                    for c in range(4):
                        mm(PS[:, sbk, 64 * gi + 16 * c:64 * gi + 16 * c + 16], KTP[k2][:, c, :], QBD[:, c, :],
                           True, True, [('ktp', k2), ('qbd',)], [('ps', sbk)])

                def emit_be(bt, sbk):
                    g0 = 8 * bt
                    k3 = bt % 2
                    s3 = PS[:, sbk, :].rearrange("p (k t) -> p k t", t=8)
                    t3 = T1[k3].rearrange("p (k t) -> p k t", t=8)
                    suf3 = SUFP[:, 8 * g0:8 * g0 + 64].unsqueeze(2).to_broadcast([128, 64, 8])
                    stt(t3, s3, 0.125, suf3, ALU.mult, ALU.add, [('ps', sbk), ('suf',)], [('t1', k3)])
                    t3 = T1[k3].rearrange("p (g k) -> p g k", g=8)
                    ln3 = LN[:, :].unsqueeze(1).to_broadcast([128, 8, 64])
                    tt(t3, t3, ln3, ALU.add, [('t1', k3), ('ln',)], [('t1', k3)])
                    act(PD[k3][:, :], T1[k3], AF.Exp, [('t1', k3)], [('pd', k3)])

                def emit_PV(bt):
                    k3 = bt % 2
                    for gi in range(8):
                        g = 8 * bt + gi
                        vslot = g % NKV
                        mm(PS[0:64, ob, :], PD[k3][:, 64 * gi:64 * gi + 64], KV[vslot][:, 512:1024], g == 0, False,
                           [('pd', k3), ('kv', vslot)], [('ps', ob)])
                        mm(PS[0:64, db, 0:1], PD[k3][:, 64 * gi:64 * gi + 64], ONEB[:, 0:1], g == 0, False,
                           [('pd', k3), ('c', 'oneb')], [('ps', db)])
                emit_T(0)
                sbk = None
                for g in range(NPG):
                    if g % 8 == 0:
                        sbk = sbank()
                    if g % 8 == 1 and g >= 8:
                        emit_PV(g // 8 - 1)
                    if g + 1 < NPG:
                        emit_T(g + 1)
                    emit_S(g, sbk)
                    if g % 8 == 7:
                        emit_be(g // 8, sbk)
                emit_PV(NPG // 8 - 1)
                sbk = sbank()
                for c in range(4):
                    mm(PS[0:32, sbk, 16 * c:16 * c + 16], KT[:, c, NPR:NT], QBD[:, c, :], True, True,
                       [('kt', 4), ('qbd',)], [('ps', sbk)])
                stt(T1[0][0:32, 0:64], PS[0:32, sbk, 0:64], 0.125, BN[:, :], ALU.mult, ALU.add,
                    [('ps', sbk), ('bn',)], [('t1', 0)])
                act(PD[0][0:32, 0:64], T1[0][0:32, 0:64], AF.Exp, [('t1', 0)], [('pd', 0)])
                mm(PS[0:64, ob, :], PD[0][0:32, 0:64], VS[0:32, :], False, True, [('pd', 0), ('vs',)], [('ps', ob)])
                mm(PS[0:64, db, 0:1], PD[0][0:32, 0:64], ONEB[0:32, 0:1], False, True, [('pd', 0), ('c', 'oneb')],
                   [('ps', db)])
                P.op('dve', lambda e, db=db: e.reciprocal(RDD[:, 0:1], PS[0:64, db, 0:1]), [('ps', db)], [('rdd',)])
                ts(OD[:, :], PS[0:64, ob, :], RDD[:, 0:1], None, ALU.mult, None, [('ps', ob), ('rdd',)], [('od',)])
                for c in range(4):
                    bk = nbank()
                    tr(PS[:, bk, 0:64], OD[:, 128 * c:128 * c + 128], IDF[0:64, 0:64], [('od',), ('c', 'idf')],
                       [('ps', bk)])
                    tt(BINB[0:64, c, sc0:sc0 + 8], PS[0:64, bk, 16 * c:16 * c + 8], SAGS[0:64, c, 8 * s:8 * s + 8],
                       ALU.mult, [('ps', bk), ('sags',)], [('binb', 4)])
                    tt(BINB[64:128, c, sc0:sc0 + 8], PS[64:128, bk, 16 * c + 8:16 * c + 16],
                       SAGS[64:128, c, 8 * s:8 * s + 8], ALU.mult, [('ps', bk), ('sags',)], [('binb', 4)])
            P.barrier()
            if STOP <= 5:
                P.emit()
                return nc
            BINA = R1[:, 0:4 * NT].rearrange("p (c t) -> p c t", c=4)
            BINC = R1[:, 4 * NT:8 * NT].rearrange("p (c t) -> p c t", c=4)
            LP = 15 + NPR
            o = 0
            PU = R2[:, o:o + 4128].bitcast(F32)
            o += 4128
            SA = R2[:, o:o + 4128].bitcast(F32)
            o += 4128
            SB_ = R2[:, o:o + 4128].bitcast(F32)
            o += 4128
            DD = R2[:, o:o + NT]
            o += NT
            SPG = R2[:, o:o + NT]
            o += NT
            PUS = R2[:, o:o + 184].bitcast(F32).rearrange("p (s t) -> p s t", s=4)
            o += 184
            SAS = R2[:, o:o + 184].bitcast(F32).rearrange("p (s t) -> p s t", s=4)
            o += 184
            SBS = R2[:, o:o + 184].bitcast(F32).rearrange("p (s t) -> p s t", s=4)
            o += 184
            o = 0
            UU = R2[:, o:o + 4104].bitcast(F32)
            o += 4104
            US = R2[:, o:o + 80].bitcast(F32).rearrange("p (s t) -> p s t", s=4)
            o += 80
            CHS = R2[:, o:o + 1024].bitcast(F32)
            o += 1024
            SG = R2[:, o:o + 1024].bitcast(F32)
            o += 1024
            TB = R2[:, o:o + 1024].bitcast(F32)
            o += 1024
            YC = R2[:, o:o + 1024].bitcast(F32)
            o += 1024
            assert o <= 22528
            P.op('dve', lambda e: e.memset(R2[:, 0:18000], 0.0), [], [('r2z',)])
            P.barrier()
            wpu, wpuk = load_w(w_in_d[l][:, COL['pu']:COL['pu'] + 512], 512)
            wpg, wpgk = load_w(w_in_d[l][:, COL['pg']:COL['pg'] + 512], 512)
            load_w(pool_w_d[l].rearrange("g c d -> (g c) d"), 128, nk=4, dst=WPW, key=('wpw',))
            dma('sp', nps_d[l, :, 0:7, :], sp_d[l, :, 8:15, :], [], [])
            for g in range(4):
                w = WIN[g]

                def ev_pu(jj, b, c0, n, bk):
                    if b < 4:
                        cp(PU[:, 15 + c0:15 + c0 + n], PS[:, bk, 0:n], [('ps', bk)], [('pu',)], eng='act')
                    else:
                        cp(PUS[:, :, 15:23], PS[:, bk, 0:32].rearrange("p (s t) -> p s t", s=4), [('ps', bk)],
                           [('pus',)], eng='act')
                dma('sp', PUS[:, :, 0:15], spT_d[l][:, g, :, :], [], [('pus',)])
                proj_fm(wpu, wpuk, [g], ev_pu)

                def ev_pg(jj, b, c0, n, bk):
                    act(SPG[:, c0:c0 + n], PS[:, bk, 0:n], AF.Silu, [('ps', bk)], [('spg',)])
                proj_fm(wpg, wpgk, [g], ev_pg)
                cur, curs = PU, PUS
                k = 1
                nxt = [(SA, SAS), (SB_, SBS)]
                ni = 0
                while k < w:
                    d, ds_ = nxt[ni]
                    ni ^= 1
                    tt(d[:, k:LP], cur[:, k:LP], cur[:, 0:LP - k], ALU.add, [('pu',), ('sw',)], [('sw',)])
                    tt(ds_[:, :, k:23], curs[:, :, k:23], curs[:, :, 0:23 - k], ALU.add, [('pus',), ('sws',)],
                       [('sws',)])
                    cur, curs = d, ds_
                    k *= 2
                for b, (c0, n) in enumerate(BLK):
                    if b < 4:
                        stt(DD[:, c0:c0 + n], cur[:, 15 + c0:15 + c0 + n], 1.0 / w, PU[:, 15 + c0:15 + c0 + n],
                            ALU.mult, ALU.subtract, [('sw',), ('pu',)], [('dd', b)])
                        if b == 0:
                            tt(T16[:, :], cur[:, 15:31], RC[:, g, :], ALU.mult, [('sw',), ('c', 'rc_s')], [('t16',)])
                            tt(DD[:, 0:16], T16[:, :], PU[:, 15:31], ALU.subtract, [('t16',), ('pu',)], [('dd', 0)])
                    else:
                        stt(DD[:, NPR:NT].rearrange("p (s t) -> p s t", s=4), curs[:, :, 15:23], 1.0 / w,
                            PUS[:, :, 15:23], ALU.mult, ALU.subtract, [('sws',), ('pus',)], [('dd', 4)])
                    bk = nbank()
                    mm(PS[:, bk, 0:n], WPW[:, g, :], DD[:, c0:c0 + n], True, True, [('wpw',), ('dd', b)],
                       [('ps', bk)])
                    stt(BINA[:, g, c0:c0 + n], PS[:, bk, 0:n], PSC[:, l, g:g + 1], SPG[:, c0:c0 + n], ALU.mult,
                        ALU.mult, [('ps', bk), ('c', 'psc'), ('spg',)], [('bina', b)])
                bk = nbank()
                tr(PS[0:15, bk, 0:128], PU[:, LP - 15:LP], IDF[:, :], [('pu',), ('c', 'idf')], [('ps', bk)])
                cp(NPV[0:15, 0, :], PS[0:15, bk, 0:128], [('ps', bk)], [('npv', 0)])
                dma('sp', npp_d[l, :, 128 * g:128 * g + 128], NPV[0:15, 0, :], [('npv', 0)], [])
                for s in range(NSEQ):
                    bk = nbank()
                    tr(PS[0:8, bk, 0:128], PUS[:, s, 15:23], IDF[:, :], [('pus',), ('c', 'idf')], [('ps', bk)])
                    cp(NPV[0:8, 1 + s, :], PS[0:8, bk, 0:128], [('ps', bk)], [('npv', 1 + s)])
                    dma('sp', nps_d[l, s, 7:15, 128 * g:128 * g + 128], NPV[0:8, 1 + s, :], [('npv', 1 + s)], [])

            P.barrier()
            if STOP <= 6:
                P.emit()
                return nc
            for c in range(4):
                wa, wak = W[0], ('w', 0)
                wb, wbk = W[1], ('w', 1)
                a0 = c * 128
                load_w(w_in_d[l][:, COL['ch'] + a0:COL['ch'] + a0 + 128], 128, dst=W[0], key=wak, slot=8, coff=0)
                load_w(w_in_d[l][:, COL['cc'] + a0:COL['cc'] + a0 + 128], 128, dst=W[0], key=wak, slot=8, coff=128)
                load_w(w_in_d[l][:, COL['cb'] + a0:COL['cb'] + a0 + 128], 128, dst=W[1], key=wbk, slot=9, coff=0)
                load_w(w_in_d[l][:, COL['cg'] + a0:COL['cg'] + a0 + 128], 128, dst=W[1], key=wbk, slot=9, coff=128)
                dma('sp', US[:, :, 0:2], scT_d[l][:, c, :, :], [], [('us',)])
                P.op('dve', lambda e: e.memset(UU[:, 0:2], 0.0), [], [('uu',)])
                for b, (c0, n) in enumerate(BLK):
                    def one(wt, wk, jj):
                        bk = nbank()
                        for kc in range(8):
                            mm(PS[:, bk, 0:n], wt[:, kc, jj * 128:jj * 128 + 128], HT[:, kc, c0:c0 + n], kc == 0,
                               kc == 7, [wk, ('ht', b)], [('ps', bk)])
                        return bk
                    bk = one(wa, wak, 0)
                    cp(CHS[:, 0:n], PS[:, bk, 0:n], [('ps', bk)], [('chs',)], eng='act')
                    bk = one(wa, wak, 1)
                    if b < 4:
                        tt(UU[:, 2 + c0:2 + c0 + n], PS[:, bk, 0:n], CHS[:, 0:n], ALU.mult, [('ps', bk), ('chs',)],
                           [('uu',)])
                    else:
                        tt(US[:, :, 2:10], PS[:, bk, 0:32].rearrange("p (s t) -> p s t", s=4),
                           CHS[:, 0:32].rearrange("p (s t) -> p s t", s=4), ALU.mult, [('ps', bk), ('chs',)],
                           [('us',)])
                    bk = one(wb, wbk, 1)
                    act(SG[:, 0:n], PS[:, bk, 0:n], AF.Silu, [('ps', bk)], [('sg',)])
                    bk = one(wb, wbk, 0)
                    tt(TB[:, 0:n], PS[:, bk, 0:n], SG[:, 0:n], ALU.mult, [('ps', bk), ('sg',)], [('tb',)])
                    if b < 4:
                        u0, u1, u2 = (UU[:, c0 + k:c0 + k + n] for k in range(3))
                        yc, tb, oc = YC[:, 0:n], TB[:, 0:n], BINC[:, c, c0:c0 + n]
                        uk = ('uu',)
                    else:
                        u0, u1, u2 = (US[:, :, k:k + 8] for k in range(3))
                        yc = YC[:, 0:32].rearrange("p (s t) -> p s t", s=4)
                        tb = TB[:, 0:32].rearrange("p (s t) -> p s t", s=4)
                        oc = BINC[:, c, NPR:NT].rearrange("p (s t) -> p s t", s=4)
                        uk = ('us',)
                    ts(yc, u0, CW[:, l, c, 0:1], None, ALU.mult, None, [uk, ('c', 'cw')], [('yc',)])
                    stt(yc, u1, CW[:, l, c, 1:2], yc, ALU.mult, ALU.add, [uk, ('c', 'cw'), ('yc',)], [('yc',)])
                    stt(yc, u2, CW[:, l, c, 2:3], yc, ALU.mult, ALU.add, [uk, ('c', 'cw'), ('yc',)], [('yc',)])
                    tt(oc, yc, tb, ALU.mult, [('yc',), ('tb',)], [('binc', b)])
                bk = nbank()
                tr(PS[0:2, bk, 0:128], UU[:, NPR:NPR + 2], IDF[:, :], [('uu',), ('c', 'idf')], [('ps', bk)])
                cp(NCV[0:2, 0, :], PS[0:2, bk, 0:128], [('ps', bk)], [('ncv', 0)])
                dma('sp', ncp_d[l, :, 128 * c:128 * c + 128], NCV[0:2, 0, :], [('ncv', 0)], [])
                for s in range(NSEQ):
                    bk = nbank()
                    tr(PS[0:2, bk, 0:128], US[:, s, 8:10], IDF[:, :], [('us',), ('c', 'idf')], [('ps', bk)])
                    cp(NCV[0:2, 1 + s, :], PS[0:2, bk, 0:128], [('ps', bk)], [('ncv', 1 + s)])
                    dma('sp', ncs_d[l, s, :, 128 * c:128 * c + 128], NCV[0:2, 1 + s, :], [('ncv', 1 + s)], [])
            P.barrier()

            if STOP <= 7:
                P.emit()
                return nc
            MRG = R2[:, 0:8 * NT].rearrange("p (c t) -> p c t", c=8)
            o = 8 * NT
            GS = [R2[:, o + 1024 * k:o + 1024 * (k + 1)].bitcast(F32) for k in range(3)]
            o += 3072
            TP = [R2[:, o + 1024 * k:o + 1024 * (k + 1)].bitcast(F32) for k in range(2)]
            o += 2048
            assert o <= 22528
            bins = (BINA, BINB, BINC)
            binkeys = ('bina', 'binb', 'binc')
            wbr_d = (w_bra_d, w_brb_d, w_brc_d)
            for J in range(2):
                gw = []
                for x in range(3):
                    gw.append(load_w(w_in_d[l][:, COL['mg'] + 1024 * x + 512 * J:COL['mg'] + 1024 * x + 512 * J + 512],
                                     512, dst=W[x], key=('w', x), slot=8 + x))
                    load_w(wbr_d[x][l][:, 512 * J:512 * J + 512], 512, nk=4, dst=WBR[x], key=('wbr', x), slot=11)
                for jj in range(4):
                    for b, (c0, n) in enumerate(BLK):
                        for x in range(3):
                            bk = nbank()
                            for kc in range(8):
                                mm(PS[:, bk, 0:n], W[x][:, kc, jj * 128:jj * 128 + 128], HT[:, kc, c0:c0 + n],
                                   kc == 0, kc == 7, [('w', x), ('ht', b)], [('ps', bk)])
                            act(GS[x][:, 0:n], PS[:, bk, 0:n], AF.Sigmoid, [('ps', bk)], [('gs', x)])
                            bk = nbank()
                            for kc in range(4):
                                mm(PS[:, bk, 0:n], WBR[x][:, kc, jj * 128:jj * 128 + 128], bins[x][:, kc, c0:c0 + n],
                                   kc == 0, kc == 3, [('wbr', x), (binkeys[x], b)], [('ps', bk)])
                            if x == 0:
                                tt(TP[0][:, 0:n], PS[:, bk, 0:n], GS[x][:, 0:n], ALU.mult, [('ps', bk), ('gs', x)],
                                   [('tp', 0)])
                            else:
                                tt(TP[1][:, 0:n], PS[:, bk, 0:n], GS[x][:, 0:n], ALU.mult, [('ps', bk), ('gs', x)],
                                   [('tp', 1)])
                                dst = TP[0][:, 0:n] if x == 1 else MRG[:, 4 * J + jj, c0:c0 + n]
                                dk = ('tp', 0) if x == 1 else ('mrg', b)
                                tt(dst, TP[0][:, 0:n], TP[1][:, 0:n], ALU.add, [('tp', 0), ('tp', 1)], [dk],
                                   eng='pool')
            P.barrier()

            if STOP <= 8:
                P.emit()
                return nc
            WO = R1[:, 0:8192].rearrange("p (k n) -> p k n", k=8)
            GO = R1[:, 8192:16384].bitcast(F32).rearrange("p (j t) -> p j t", j=8)
            load_w(w_o_d[l], 1024, dst=WO, key=('wo',), slot=8)
            if l == 1:
                dma('sp', FG, fg_d.partition_broadcast(128), [], [('c', 'fg')], slot=6)
            for b, (c0, n) in enumerate(BLK):
                for j in range(8):
                    bk = nbank()
                    for kc in range(8):
                        mm(PS[:, bk, 0:n], WO[:, kc, j * 128:j * 128 + 128], MRG[:, kc, c0:c0 + n], kc == 0, kc == 7,
                           [('wo',), ('mrg', b)], [('ps', bk)])
                    for (cc0, nn, sq) in seq_groups(c0, n):
                        ts(GO[:, j, cc0 - c0:cc0 - c0 + nn], PS[:, bk, cc0 - c0:cc0 - c0 + nn],
                           MOD[:, l, 16 + j, sq:sq + 1], None, ALU.mult, None, [('ps', bk), ('mod',)], [('go',)])
                tiles = [i for i in range(17) if blk_of_tile(i) == b]
                for i in tiles:
                    t0, tn = TIL[i]
                    xt = XT[i % 2]
                    xk = ('xt', i % 2)
                    src = xin if l == 0 else x1_d
                    dma('sp', xt[:tn, :], src[t0:t0 + tn, :], [('x1', i)] if l == 1 else [], [xk])
                    bA = nbank()
                    bB = nbank()
                    for j in range(8):
                        bk = bA if j < 4 else bB
                        tr(PS[0:tn, bk, (j % 4) * 128:(j % 4) * 128 + 128], GO[:, j, t0 - c0:t0 - c0 + tn], IDF[:, :],
                           [('go',), ('c', 'idf')], [('ps', bk)])
                    tt(xt[:tn, 0:512], PS[0:tn, bA, :], xt[:tn, 0:512], ALU.add, [('ps', bA), xk], [xk])
                    tt(xt[:tn, 512:1024], PS[0:tn, bB, :], xt[:tn, 512:1024], ALU.add, [('ps', bB), xk], [xk])
                    if l == 0:
                        dma('sp', x1_d[t0:t0 + tn, :], xt[:tn, :], [xk], [('x1', i)])
                        norm_tile(1, i, xt, xk)
                    else:
                        P.op('dve', lambda e: e.memset(SS[:, 0:1], 0.0), [], [('ss',)])
                        act(XN[:tn, :], xt[:tn, :], AF.Square, [xk, ('ss',)], [('xn',), ('ss',)],
                            accum_out=SS[:tn, 0:1])
                        ts(SS[:tn, 1:2], SS[:tn, 0:1], 1.0 / D, 1e-6, ALU.mult, ALU.add, [('ss',)], [('ss',)])
                        act(SS[:tn, 3:4], SS[:tn, 1:2], AF.Sqrt, [('ss',)], [('ss',)])
                        P.op('dve', lambda e, tn=tn: e.reciprocal(SS[:tn, 2:3], SS[:tn, 3:4]), [('ss',)], [('ss',)])
                        stt(xt[:tn, :], xt[:tn, :], SS[:tn, 2:3], FG[:tn, :], ALU.mult, ALU.mult,
                            [xk, ('ss',), ('c', 'fg')], [xk])
                        dma('sp', y_d[t0:t0 + tn, :], xt[:tn, :], [xk], [])
            P.barrier()
        P.emit()
    return nc


_CACHE = {}


def _consts():
    ident = np.eye(128, dtype=np.float32)
    p = np.arange(128)[:, None]
    q = np.arange(128)[None, :]
    maskc = np.where(p <= q, 0.0, NEG).astype(np.float32)
    sel = np.zeros((8, 8, 128), np.float32)
    for h in range(8):
        sel[h, h, :] = 1.0
    a = np.arange(64)
    u64 = (a[:, None] > a[None, :]).astype(np.float32)
    i8 = np.eye(8, dtype=np.float32)
    maskn = np.full((32, NSEQ, 8), NEG, np.float32)
    for tk in range(32):
        for s in range(NSEQ):
            for tq in range(8):
                if tk // 8 == s and tk % 8 <= tq:
                    maskn[tk, s, tq] = 0.0
    rc = np.zeros((128, 4, 16), np.float32)
    for g, w in enumerate(WIN):
        rc[:, g, :] = 1.0 / np.minimum(np.arange(16) + 1, w)
    piota = np.arange(128, dtype=np.float32)[:, None]
    return dict(ident=ident, maskc=maskc, u64=u64, i8=i8, maskn=maskn, rc=rc, piota=piota)


def _fm(v, nchunk):
    v = np.asarray(v, np.float32)
    lead = v.shape[:-1]
    v = v.reshape(lead + (nchunk, 128))
    return np.ascontiguousarray(np.moveaxis(v, -1, 0))


def kernel(x_prompt, x_sample, cache_k, cache_v, cache_logf, state_pool, state_conv, page_table,
           c_prompt, c_sample, norm_g, w_cond, b_cond, w_in, b_f, pool_w, pool_scale, conv_w,
           w_br_a, w_br_b, w_br_c, w_o, final_g):
    f32 = lambda a: np.ascontiguousarray(np.asarray(a, dtype=np.float32))
    if 'nc' not in _CACHE:
        _CACHE['nc'] = build()
    nc = _CACHE['nc']
    consts = _consts()
    ckv = np.empty((NL, NPOOL * 128, 2, 512), np.float32)
    ckv[:, :, 0, :] = np.asarray(cache_k, np.float32).reshape(NL, NPOOL * 128, 512)
    ckv[:, :, 1, :] = np.asarray(cache_v, np.float32).reshape(NL, NPOOL * 128, 512)
    ckv = ckv.reshape(NL, NPOOL * 128, 1024)
    clf = f32(cache_logf).reshape(NL, NPOOL, 1024)
    shared = dict(cache_kv=ckv, cache_logf=clf, w_in=f32(w_in), w_cond=f32(w_cond),
                  w_br_a=f32(w_br_a), w_br_b=f32(w_br_b), w_br_c=f32(w_br_c), w_o=f32(w_o), pool_w=f32(pool_w),
                  norm_g_fm=_fm(norm_g, 8), b_cond_fm=_fm(b_cond, 24),
                  b_f=np.ascontiguousarray(f32(b_f).T), pool_scale_fm=_fm(pool_scale, 4),
                  conv_w_fm=np.ascontiguousarray(_fm(conv_w, 4).transpose(0, 1, 3, 2)),
                  final_g=f32(final_g).reshape(1, D), **consts)
    pt = np.asarray(page_table, dtype=np.int32)
    in_maps = []
    for c in range(8):
        sl = slice(4 * c, 4 * c + 4)
        m = dict(shared)
        m['xin'] = np.concatenate([f32(x_prompt[c]), f32(x_sample[sl]).reshape(NSM, D)], axis=0)
        cc = np.concatenate([f32(c_prompt[c:c + 1]), f32(c_sample[sl])], axis=0)
        m['cT'] = np.ascontiguousarray(cc.reshape(5, 8, 128).transpose(2, 1, 0))
        m['pt'] = np.ascontiguousarray(pt[sl].reshape(1, NSEQ * NPG))
        m['ptT'] = np.ascontiguousarray(np.concatenate([pt[sl].T, pt[sl].T], axis=0))
        sp = f32(state_pool[:, sl])
        m['sp_tm'] = sp
        m['spT'] = np.ascontiguousarray(sp.reshape(NL, NSEQ, 15, 4, 128).transpose(0, 4, 3, 1, 2))
        sc = f32(state_conv[:, sl])
        m['scT'] = np.ascontiguousarray(sc.reshape(NL, NSEQ, 2, 4, 128).transpose(0, 4, 3, 1, 2))
        in_maps.append(m)
    ncores = _CACHE.get('ncores', 8)
    if _CACHE.get('trace'):
        res = run_bass_kernel_spmd(nc, in_maps[:ncores], core_ids=list(range(ncores)), trace=True)
        print('EXEC_TIME_NS', res.exec_time_ns, flush=True)
    else:
        res = run_bass_kernel_spmd(nc, in_maps[:ncores], core_ids=list(range(ncores)))
    R = list(res.results)
    while len(R) < 8:
        R.append({k: np.zeros_like(v) for k, v in R[0].items()})
    cat = lambda f: np.concatenate([f(r) for r in R], axis=0)
    y_prompt = np.stack([r['y'][:NPR] for r in R])
    y_sample = cat(lambda r: r['y'][NPR:].reshape(NSEQ, 8, D))
    nkp = np.stack([r['nk'][:, :NPR].reshape(NL, NPR, 8, 64) for r in R], axis=1)
    nvp = np.stack([r['nv'][:, :NPR].reshape(NL, NPR, 8, 64) for r in R], axis=1)
    nlp = np.stack([r['nlf'][:, :NPR] for r in R], axis=1)
    npp = np.stack([r['npool_p'] for r in R], axis=1)
    ncp = np.stack([r['nconv_p'] for r in R], axis=1)
    nks = np.concatenate([r['nk'][:, NPR:].reshape(NL, NSEQ, 8, 8, 64) for r in R], axis=1)
    nvs = np.concatenate([r['nv'][:, NPR:].reshape(NL, NSEQ, 8, 8, 64) for r in R], axis=1)
    nls = np.concatenate([r['nlf'][:, NPR:].reshape(NL, NSEQ, 8, 8) for r in R], axis=1)
    nps = np.concatenate([r['npool_s'] for r in R], axis=1)
    ncs = np.concatenate([r['nconv_s'] for r in R], axis=1)
    outs = (y_prompt, y_sample, nkp, nvp, nlp, npp, ncp, nks, nvs, nls, nps, ncs)
    return tuple(np.ascontiguousarray(o, dtype=np.float32) for o in outs)
```
